# Optimizing a Trainium2 kernel written in Bass

```python
import math
import jax, jax.numpy as jnp
from jax import lax
import numpy as np

D_MODEL = 4096
BATCH = 4
SEQ = 4096
DEPTH = 1

SSM_WIDTH = D_MODEL // 2
SSM_GROUP = 16
SSM_GROUPS = SSM_WIDTH // SSM_GROUP
SSM_STATE = 64
DT_MIN = 1e-3
DT_MAX = 1e-1
N_HEADS = 16
HEAD_DIM = 128
N_KV = 4
HPG = N_HEADS // N_KV
ATT_WIDTH = N_HEADS * HEAD_DIM
KV_WIDTH = N_KV * HEAD_DIM
L_CMP = 32
STRIDE_CMP = 16
L_SEL = 64
N_SEL = 16
WINDOW = 512
Q_BLOCK = 128
SEL_Q_BLOCK = 64
D_FF = ((8 * D_MODEL // 3 + 255) // 256) * 256
RMS_EPS = 1e-6
NEG_INF = -1e30
FORCE_SCORE = 1e9
IN_SPLITS = (SSM_WIDTH, ATT_WIDTH, KV_WIDTH, KV_WIDTH, KV_WIDTH, KV_WIDTH, KV_WIDTH, KV_WIDTH, 3 * N_HEADS, D_MODEL, D_MODEL)
IN_WIDTH = SSM_WIDTH + ATT_WIDTH + 6 * KV_WIDTH + 3 * N_HEADS + 2 * D_MODEL

kernel_name = "hybrid_s5_nsa_gated_block"


def rms_norm(x, gain):
    xf = x.astype(jnp.float32)
    y = xf * lax.rsqrt(jnp.mean(xf * xf, axis=-1, keepdims=True) + RMS_EPS)
    return (y * gain.astype(jnp.float32)).astype(x.dtype)


def _ssm_combine(e1, e2):
    a1r, a1i, b1r, b1i = e1
    a2r, a2i, b2r, b2i = e2
    return (a1r * a2r - a1i * a2i,
            a1r * a2i + a1i * a2r,
            a2r * b1r - a2i * b1i + b2r,
            a2r * b1i + a2i * b1r + b2i)


def s5_mixer(u, a_re, a_im, log_dt, b_re, b_im, c_re, c_im, d_skip, w_glu, b_glu):
    f32 = jnp.float32
    bsz, seq, _ = u.shape
    uf = u.astype(f32).reshape(bsz, seq, SSM_GROUPS, SSM_GROUP)
    ar, ai = a_re.astype(f32), a_im.astype(f32)
    dt = jnp.exp(log_dt.astype(f32))[:, None]
    decay = jnp.exp(dt * ar)
    abar_r, abar_i = decay * jnp.cos(dt * ai), decay * jnp.sin(dt * ai)
    den = ar * ar + ai * ai
    zr = ((abar_r - 1.0) * ar + abar_i * ai) / den
    zi = (abar_i * ar - (abar_r - 1.0) * ai) / den
    br, bi = b_re.astype(f32), b_im.astype(f32)
    bbar_r = zr[..., None] * br - zi[..., None] * bi
    bbar_i = zr[..., None] * bi + zi[..., None] * br
    bu_r = jnp.einsum('gpc,bsgc->bsgp', bbar_r, uf)
    bu_i = jnp.einsum('gpc,bsgc->bsgp', bbar_i, uf)
    a_shape = (1, seq, SSM_GROUPS, SSM_STATE)
    elems = (jnp.broadcast_to(abar_r, a_shape), jnp.broadcast_to(abar_i, a_shape), bu_r, bu_i)
    _, _, h_r, h_i = lax.associative_scan(_ssm_combine, elems, axis=1)
    y = (jnp.einsum('gcp,bsgp->bsgc', c_re.astype(f32), h_r)
         - jnp.einsum('gcp,bsgp->bsgc', c_im.astype(f32), h_i)
         + d_skip.astype(f32).reshape(SSM_GROUPS, SSM_GROUP) * uf)
    y = jax.nn.gelu(y.reshape(bsz, seq, SSM_WIDTH))
    y = y * jax.nn.sigmoid(y @ w_glu.astype(f32) + b_glu.astype(f32))
    return y.astype(u.dtype)


def _compress(k, pe, w1, w2):
    bsz, seq = k.shape[:2]
    n_cmp = (seq - L_CMP) // STRIDE_CMP + 1
    idx = np.arange(n_cmp)[:, None] * STRIDE_CMP + np.arange(L_CMP)[None, :]
    blk = k[:, idx] + pe[:, None, :]
    blk = jnp.moveaxis(blk, 3, 2).reshape(bsz, n_cmp, N_KV, L_CMP * HEAD_DIM)
    return jax.nn.gelu(blk @ w1) @ w2


def nsa_mixer(q, k_c, v_c, k_s, v_s, k_w, v_w, gate_logits, pe_k, w1_k, w2_k, pe_v, w1_v, w2_v):
    f32 = jnp.float32
    bsz, seq = q.shape[:2]
    q = q.reshape(bsz, seq, N_KV, HPG, HEAD_DIM) * (HEAD_DIM ** -0.5)
    k_c, v_c, k_s, v_s, k_w, v_w = [a.reshape(bsz, seq, N_KV, HEAD_DIM) for a in (k_c, v_c, k_s, v_s, k_w, v_w)]
    t = jnp.arange(seq)

    n_cmp = (seq - L_CMP) // STRIDE_CMP + 1
    kc = _compress(k_c, pe_k, w1_k, w2_k)
    vc = _compress(v_c, pe_v, w1_v, w2_v)
    blk_end = jnp.arange(n_cmp) * STRIDE_CMP + (L_CMP - 1)
    cmp_ok = blk_end[None, :] <= t[:, None]
    s = jnp.einsum('bsghd,bngd->bghsn', q, kc).astype(f32)
    p_cmp = jax.nn.softmax(jnp.where(cmp_ok, s, NEG_INF), axis=-1) * cmp_ok
    o_cmp = jnp.einsum('bghsn,bngd->bsghd', p_cmp.astype(q.dtype), vc)

    n_blk = seq // L_SEL
    n_top = min(N_SEL, n_blk)
    ci = np.arange(n_cmp)[:, None]
    sj = np.arange(n_blk)[None, :]
    overlap = ((ci * STRIDE_CMP < (sj + 1) * L_SEL) & (ci * STRIDE_CMP + L_CMP > sj * L_SEL)).astype(np.float32)
    imp = jnp.einsum('bghsn,nj->bgsj', p_cmp, jnp.asarray(overlap))
    blk = jnp.arange(n_blk)[None, :]
    cur = (t // L_SEL)[:, None]
    allowed = blk * L_SEL <= t[:, None]
    forced = (blk == 0) | (blk == cur) | (blk == cur - 1)
    score = jnp.where(forced, FORCE_SCORE, jnp.where(allowed, imp, NEG_INF))
    _, sel_idx = lax.top_k(score, n_top)

    k_blk = k_s.reshape(bsz, n_blk, L_SEL, N_KV, HEAD_DIM).transpose(0, 3, 1, 2, 4)
    v_blk = v_s.reshape(bsz, n_blk, L_SEL, N_KV, HEAD_DIM).transpose(0, 3, 1, 2, 4)
    n_qc = seq // SEL_Q_BLOCK
    q_chunks = jnp.moveaxis(q.reshape(bsz, n_qc, SEL_Q_BLOCK, N_KV, HPG, HEAD_DIM), 1, 0)
    idx_chunks = jnp.moveaxis(sel_idx.reshape(bsz, N_KV, n_qc, SEL_Q_BLOCK, n_top), 2, 0)
    t_chunks = t.reshape(n_qc, SEL_Q_BLOCK)
    b_ix = jnp.arange(bsz)[:, None, None, None]
    g_ix = jnp.arange(N_KV)[None, :, None, None]
    offs = jnp.arange(L_SEL)

    def sel_block(args):
        qc, ic, tc = args
        kg = k_blk[b_ix, g_ix, ic]
        vg = v_blk[b_ix, g_ix, ic]
        pos = ic[..., None] * L_SEL + offs
        ok = (pos <= tc[None, None, :, None, None])[:, :, None]
        sc = jnp.einsum('bqghd,bgqkld->bghqkl', qc, kg).astype(f32)
        p = jax.nn.softmax(jnp.where(ok, sc, NEG_INF), axis=(-2, -1))
        return jnp.einsum('bghqkl,bgqkld->bqghd', p.astype(qc.dtype), vg)

    o_sel = lax.map(sel_block, (q_chunks, idx_chunks, t_chunks))
    o_sel = jnp.moveaxis(o_sel, 0, 1).reshape(bsz, seq, N_KV, HPG, HEAD_DIM)

    nb = seq // Q_BLOCK
    nw = WINDOW // Q_BLOCK

    def band(a):
        ab = a.reshape(bsz, nb, Q_BLOCK, N_KV, HEAD_DIM)
        ap = jnp.pad(ab, ((0, 0), (nw, 0), (0, 0), (0, 0), (0, 0)))
        return jnp.concatenate([ap[:, j:j + nb] for j in range(nw + 1)], axis=2)

    kwb, vwb = band(k_w), band(v_w)
    qpos = t.reshape(nb, Q_BLOCK)
    kpos = (jnp.arange(nb)[:, None] - nw) * Q_BLOCK + jnp.arange((nw + 1) * Q_BLOCK)[None, :]
    diff = qpos[:, :, None] - kpos[:, None, :]
    win_ok = (diff >= 0) & (diff < WINDOW) & (kpos[:, None, :] >= 0)
    qb = q.reshape(bsz, nb, Q_BLOCK, N_KV, HPG, HEAD_DIM)
    sw = jnp.einsum('bnqghd,bnkgd->bghnqk', qb, kwb).astype(f32)
    pw = jax.nn.softmax(jnp.where(win_ok, sw, NEG_INF), axis=-1)
    o_win = jnp.einsum('bghnqk,bnkgd->bnqghd', pw.astype(q.dtype), vwb).reshape(bsz, seq, N_KV, HPG, HEAD_DIM)

    g = jax.nn.sigmoid(gate_logits.astype(f32)).reshape(bsz, seq, 3, N_KV, HPG, 1).astype(q.dtype)
    o = g[:, :, 0] * o_cmp + g[:, :, 1] * o_sel + g[:, :, 2] * o_win
    return o.reshape(bsz, seq, ATT_WIDTH)


def setup_inputs(seed: int = 0) -> dict:
    key = jax.random.key(seed)
    ks = jax.random.split(key, 32)
    f32 = jnp.float32
    L = DEPTH

    def nrm(k, shape, scale):
        return jax.random.normal(k, shape, f32) * scale

    def gain(k, width):
        return 1.0 + 0.05 * jax.random.normal(k, (L, width), f32)

    n_idx = jnp.arange(SSM_STATE, dtype=f32)
    return {
        "x": nrm(ks[0], (BATCH, SEQ, D_MODEL), 1.0),
        "norm_mix_pre": gain(ks[1], D_MODEL),
        "w_in": nrm(ks[2], (L, D_MODEL, IN_WIDTH), D_MODEL ** -0.5),
        "ssm_a_re": -0.5 * jnp.exp(0.05 * jax.random.normal(ks[3], (L, SSM_GROUPS, SSM_STATE), f32)),
        "ssm_a_im": math.pi * n_idx + 0.05 * jax.random.normal(ks[4], (L, SSM_GROUPS, SSM_STATE), f32),
        "ssm_log_dt": jax.random.uniform(ks[5], (L, SSM_GROUPS), f32, math.log(DT_MIN), math.log(DT_MAX)),
        "ssm_b_re": nrm(ks[6], (L, SSM_GROUPS, SSM_STATE, SSM_GROUP), (2 * SSM_GROUP) ** -0.5),
        "ssm_b_im": nrm(ks[7], (L, SSM_GROUPS, SSM_STATE, SSM_GROUP), (2 * SSM_GROUP) ** -0.5),
        "ssm_c_re": nrm(ks[8], (L, SSM_GROUPS, SSM_GROUP, SSM_STATE), (2 * SSM_STATE) ** -0.5),
        "ssm_c_im": nrm(ks[9], (L, SSM_GROUPS, SSM_GROUP, SSM_STATE), (2 * SSM_STATE) ** -0.5),
        "ssm_d": nrm(ks[10], (L, SSM_WIDTH), 1.0),
        "ssm_w_glu": nrm(ks[11], (L, SSM_WIDTH, SSM_WIDTH), SSM_WIDTH ** -0.5),
        "ssm_b_glu": nrm(ks[12], (L, SSM_WIDTH), 0.01),
        "cmp_pe_k": nrm(ks[13], (L, L_CMP, HEAD_DIM), 0.1),
        "cmp_w1_k": nrm(ks[14], (L, L_CMP * HEAD_DIM, HEAD_DIM), (L_CMP * HEAD_DIM) ** -0.5),
        "cmp_w2_k": nrm(ks[15], (L, HEAD_DIM, HEAD_DIM), HEAD_DIM ** -0.5),
        "cmp_pe_v": nrm(ks[16], (L, L_CMP, HEAD_DIM), 0.1),
        "cmp_w1_v": nrm(ks[17], (L, L_CMP * HEAD_DIM, HEAD_DIM), (L_CMP * HEAD_DIM) ** -0.5),
        "cmp_w2_v": nrm(ks[18], (L, HEAD_DIM, HEAD_DIM), HEAD_DIM ** -0.5),
        "w_proj_a": nrm(ks[19], (L, SSM_WIDTH, D_MODEL), SSM_WIDTH ** -0.5),
        "w_proj_b": nrm(ks[20], (L, ATT_WIDTH, D_MODEL), ATT_WIDTH ** -0.5),
        "w_out": nrm(ks[21], (L, D_MODEL, D_MODEL), D_MODEL ** -0.5),
        "norm_mix_post": gain(ks[22], D_MODEL),
        "norm_ffn_pre": gain(ks[23], D_MODEL),
        "w_ffn_gate": nrm(ks[24], (L, D_MODEL, D_FF), D_MODEL ** -0.5),
        "w_ffn_up": nrm(ks[25], (L, D_MODEL, D_FF), D_MODEL ** -0.5),
        "w_ffn_down": nrm(ks[26], (L, D_FF, D_MODEL), D_FF ** -0.5),
        "norm_ffn_post": gain(ks[27], D_MODEL),
    }


def reference(x, norm_mix_pre, w_in, ssm_a_re, ssm_a_im, ssm_log_dt, ssm_b_re, ssm_b_im, ssm_c_re, ssm_c_im,
              ssm_d, ssm_w_glu, ssm_b_glu, cmp_pe_k, cmp_w1_k, cmp_w2_k, cmp_pe_v, cmp_w1_v, cmp_w2_v,
              w_proj_a, w_proj_b, w_out, norm_mix_post, norm_ffn_pre, w_ffn_gate, w_ffn_up, w_ffn_down,
              norm_ffn_post):
    split_at = [int(v) for v in np.cumsum(IN_SPLITS)[:-1]]
    h = x
    for l in range(DEPTH):
        hn = rms_norm(h, norm_mix_pre[l])
        proj = hn @ w_in[l]
        u, q, kc, vc, ks_, vs_, kw, vw, g_nsa, g_a, g_b = jnp.split(proj, split_at, axis=-1)
        y_a = s5_mixer(u, ssm_a_re[l], ssm_a_im[l], ssm_log_dt[l], ssm_b_re[l], ssm_b_im[l],
                       ssm_c_re[l], ssm_c_im[l], ssm_d[l], ssm_w_glu[l], ssm_b_glu[l])
        y_b = nsa_mixer(q, kc, vc, ks_, vs_, kw, vw, g_nsa, cmp_pe_k[l], cmp_w1_k[l], cmp_w2_k[l],
                        cmp_pe_v[l], cmp_w1_v[l], cmp_w2_v[l])
        merged = jax.nn.sigmoid(g_a) * (y_a @ w_proj_a[l]) + jax.nn.sigmoid(g_b) * (y_b @ w_proj_b[l])
        h = h + rms_norm(merged @ w_out[l], norm_mix_post[l])
        hn = rms_norm(h, norm_ffn_pre[l])
        f = (jax.nn.silu(hn @ w_ffn_gate[l]) * (hn @ w_ffn_up[l])) @ w_ffn_down[l]
        h = h + rms_norm(f, norm_ffn_post[l])
    return h
```

```python
import math
import types
from contextlib import ExitStack

import numpy as np
import concourse.bass as bass
import concourse.mybir as mybir
from concourse.bass_utils import run_bass_kernel_spmd

F32 = mybir.dt.float32
BF16 = mybir.dt.bfloat16
AF = mybir.ActivationFunctionType
ALU = mybir.AluOpType
NEG = -30000.0
EPS = 1e-6


class Cfg:
    def __init__(self, B=4, S=4096, D=4096, SW=2048, NH=16, NKV=4, DFF=11008):
        self.B, self.S, self.D, self.SW, self.NH, self.NKV, self.DFF = B, S, D, SW, NH, NKV, DFF
        self.SH = S // 2
        self.KD = D // 128
        self.NG = SW // 16
        self.KS = SW // 128
        self.HPG = NH // NKV
        self.AW = NH * 128
        self.KA = self.AW // 128
        self.KVW = NKV * 128
        self.KF = DFF // 128
        self.NB = S // 64
        self.NC = (S - 32) // 16 + 1
        self.NCT = (self.NC + 127) // 128
        self.NKT = S // 128
        self.NQT = self.SH // 128
        self.NQC = self.SH // 512
        self.INW = SW + self.AW + 6 * self.KVW + 3 * NH + 2 * D
        self.NK = int(math.log2(S))
        assert (1 << self.NK) == S


def _freeze(fn):
    if fn.__closure__ is None:
        return fn
    cells = []
    for cl in fn.__closure__:
        try:
            cells.append(types.CellType(cl.cell_contents))
        except ValueError:
            cells.append(cl)
    return types.FunctionType(fn.__code__, fn.__globals__, fn.__name__, fn.__defaults__, tuple(cells))


class Tok:
    __slots__ = ("sem", "val")

    def __init__(self, sem, val):
        self.sem, self.val = sem, val


class Buf:
    def __init__(self, name):
        self.name = name
        self.w = None
        self.r = {}
        self.dsem = None
        self.dcnt = 0


class Sch:
    ENG = ("pe", "act", "dve", "pool", "sp")

    def __init__(self, nc, es):
        self.nc, self.es = nc, es
        self.q = {e: [] for e in self.ENG}
        self.sem = {e: es.enter_context(nc.semaphore("c_" + e)) for e in ("pe", "act", "dve", "pool")}
        self.cnt = {e: 0 for e in self.ENG}
        self.seen = {e: {} for e in self.ENG}
        self.nsem = 4
        self.dma_toks = []

    def _wait(self, e, tok):
        if tok is None:
            return
        if e == "pe" and tok.sem is self.sem["pe"]:
            return
        k = id(tok.sem)
        if self.seen[e].get(k, 0) >= tok.val:
            return
        self.seen[e][k] = tok.val
        self.q[e].append(("w", tok.sem, tok.val))

    def _deps(self, e, reads, writes):
        for b in reads:
            self._wait(e, b.w)
        for b in writes:
            self._wait(e, b.w)
            for t in list(b.r.values()):
                self._wait(e, t)

    def _mark(self, tok, reads, writes):
        for b in reads:
            b.r[id(tok.sem)] = tok
        for b in writes:
            b.w = tok
            b.r = {}

    def op(self, e, fn, reads=(), writes=()):
        self._deps(e, reads, writes)
        self.cnt[e] += 1
        tok = Tok(self.sem[e], self.cnt[e])
        self.q[e].append(("o", _freeze(fn)))
        self._mark(tok, reads, writes)
        return tok

    def group(self, fns, reads=(), writes=()):
        self._deps("pe", reads, writes)
        for f in fns[:-1]:
            self.q["pe"].append(("n", _freeze(f)))
        self.cnt["pe"] += 1
        tok = Tok(self.sem["pe"], self.cnt["pe"])
        self.q["pe"].append(("o", _freeze(fns[-1])))
        self._mark(tok, reads, writes)
        return tok

    def dma(self, e, out, in_, reads, writes, slot, **kw):
        self._deps(e, reads, writes)
        if slot.dsem is None:
            slot.dsem = self.es.enter_context(self.nc.semaphore("d_" + slot.name))
            self.nsem += 1
        slot.dcnt += 16
        tok = Tok(slot.dsem, slot.dcnt)
        self.q[e].append(("d", out, in_, slot.dsem, kw))
        self._mark(tok, reads, writes)
        self.dma_toks.append(tok)
        return tok

    def barrier(self):
        last = {}
        for t in self.dma_toks:
            last[id(t.sem)] = t
        toks = list(last.values()) + [Tok(self.sem[x], self.cnt[x]) for x in ("pe", "act", "dve", "pool") if self.cnt[x]]
        for e in self.ENG:
            for t in toks:
                if e in self.sem and t.sem is self.sem[e]:
                    if e != "pe":
                        self._wait(e, t)
                    continue
                self._wait(e, t)
        self.dma_toks = list(last.values())

    def emit(self):
        nc = self.nc
        last = {}
        for t in self.dma_toks:
            last[id(t.sem)] = t
        for t in last.values():
            self._wait("sp", t)

        def replay(eng, items, mysem):
            for it in items:
                if it[0] == "w":
                    eng.wait_ge(it[1], it[2])
                elif it[0] == "o":
                    it[1](eng).then_inc(mysem, 1)
                elif it[0] == "n":
                    it[1](eng)
                else:
                    eng.dma_start(out=it[1], in_=it[2], **it[4]).then_inc(it[3], 16)

        with nc.Block() as block:
            @block.tensor
            def _(t):
                replay(t, self.q["pe"], self.sem["pe"])

            @block.scalar
            def _(a):
                replay(a, self.q["act"], self.sem["act"])

            @block.vector
            def _(v):
                replay(v, self.q["dve"], self.sem["dve"])

            @block.gpsimd
            def _(g):
                replay(g, self.q["pool"], self.sem["pool"])

            @block.sync
            def _(s):
                replay(s, self.q["sp"], None)


class Rot:
    def __init__(self, items):
        self.items = items
        self.i = 0

    def next(self):
        it = self.items[self.i % len(self.items)]
        self.i += 1
        return it


class Arena:
    def __init__(self, t, nelem):
        self.t, self.n, self.off = t, nelem, 0

    def alloc(self, shape, dt):
        p = shape[0]
        n = int(np.prod(shape[1:]))
        ne = n * (2 if dt == F32 else 1)
        self.off = (self.off + 1) // 2 * 2
        assert self.off + ne <= self.n, ("SBUF arena overflow", self.off, ne, self.n)
        ap = self.t[0:p, self.off:self.off + ne]
        self.off += ne
        if dt == F32:
            ap = ap.bitcast(F32)
        if len(shape) == 3:
            ap = ap.rearrange("p (a b) -> p a b", b=shape[2])
        elif len(shape) == 4:
            ap = ap.rearrange("p (a b c) -> p a b c", b=shape[2], c=shape[3])
        return ap

def build(cfg, debug_outs=(), phases="ABNCD"):
    c = cfg
    nc = bass.Bass("TRN2", target_bir_lowering=False)
    es = ExitStack()
    S = Sch(nc, es)
    D, SH, KD, SW, NG, KS, AW, KA, KVW, KF, NB, NC_, NCT, NKT, NQT, NQC, NH, NKV, HPG = (
        c.D, c.SH, c.KD, c.SW, c.NG, c.KS, c.AW, c.KA, c.KVW, c.KF, c.NB, c.NC, c.NCT, c.NKT, c.NQT,
        c.NQC, c.NH, c.NKV, c.HPG)
    SEQ = c.S
    TT = 512
    NTO = SH // TT

    def din(name, shape, dt=F32):
        return nc.dram_tensor(name, list(shape), dt, kind="ExternalInput").ap()

    def dscr(name, shape, dt):
        kind = "ExternalOutput" if name in debug_outs else "Internal"
        return nc.dram_tensor(name, list(shape), dt, kind=kind).ap()

    x_own = din("x_own", [SH, D])
    x_ctx = din("x_ctx", [SH, D])
    w_in = din("w_in", [D, c.INW])
    w_glu = din("w_glu", [SW, SW])
    w_pa = din("w_pa", [SW, D])
    w_pb = din("w_pb", [AW, D])
    w_out = din("w_out", [D, D])
    w_fg = din("w_fg", [D, c.DFF])
    w_fu = din("w_fu", [D, c.DFF])
    w_fd = din("w_fd", [c.DFF, D])
    i_g1T = din("g1T", [128, KD])
    i_g3T = din("g3T", [128, KD])
    i_g2rep = din("g2rep", [128, D])
    i_g4rep = din("g4rep", [128, D])
    i_are_pg = din("are_pg", [128, NG])
    i_aim_pg = din("aim_pg", [128, NG])
    i_ldt_pg = din("ldt_pg", [128, NG])
    i_are_gp = din("are_gp", [NG, 64])
    i_aim_gp = din("aim_gp", [NG, 64])
    i_ldt_gp = din("ldt_gp", [NG, 64])
    i_bre = din("bre_cg", [16, NG * 64])
    i_bim = din("bim_cg", [16, NG * 64])
    i_cc1 = din("cc1", [128, NG * 16])
    i_cc2 = din("cc2", [128, NG * 16])
    i_dsk = din("dskip", [16, NG])
    i_bglu = din("bgluT", [128, KS])
    i_flag = din("flag", [128, 1])
    i_w1k = din("w1k", [32 * 128, 128])
    i_w1v = din("w1v", [32 * 128, 128])
    i_w2k = din("w2k", [128, 128])
    i_w2v = din("w2v", [128, 128])
    i_pek = din("pekT", [128, 32])
    i_pev = din("pevT", [128, 32])
    i_cmpb = din("cmpbias", [NCT * 128, SH])
    i_ovl = din("ovl", [NCT * 128, NB])
    i_selA = din("selA", [SH, NB])
    i_selB = din("selB", [SH, NB])
    i_selM = din("selM", [SH, NB])
    i_exp = din("expand", [NB, NKT * 128])
    i_caus = din("caus", [4 * 128, 512])
    i_wlo = din("wlo", [128, 128])
    i_whi = din("whi", [128, 128])
    i_wctx = din("wctx", [128, 128])
    i_ident = din("ident", [128, 128])
    y_out = nc.dram_tensor("y", [SH, D], F32, kind="ExternalOutput").ap()

    s_uT = dscr("s_uT", [SW, SEQ], BF16)
    s_kT = dscr("s_kT", [3, KVW, SEQ], BF16)
    s_vcT = dscr("s_vcT", [KVW, SEQ], BF16)
    s_vt = dscr("s_vt", [2, SEQ, KVW], BF16)
    s_qT = dscr("s_qT", [AW, SH], BF16)
    s_gn = dscr("s_gn", [SH, 3 * NH], F32)
    s_gaT = dscr("s_gaT", [D, SH], BF16)
    s_gbT = dscr("s_gbT", [D, SH], BF16)
    s_bbar = dscr("s_bbar", [NG, SEQ // TT, 16, 256], BF16)
    s_zs = dscr("s_zs", [SEQ // TT, 2, NG * 64], F32)
    s_tab = dscr("s_tab", [128, NG, 64 + 2 * (TT // 32)], F32)
    s_yaT = dscr("s_yaT", [SW, SH], F32)
    s_ybT = dscr("s_ybT", [AW, SH], BF16)
    s_otok = dscr("s_otok", [SH, D], F32)
    s_h1 = dscr("s_h1", [SH, D], F32)
    s_ftok = dscr("s_ftok", [SH, D], F32)
    db = {n: Buf(n) for n in ("uT", "kT", "vcT", "vt", "qT", "gn", "gaT", "gbT", "bbar", "zs", "yaT", "ybT",
                              "otok", "h1", "ftok", "tab")}

    ARENA_N = 104000
    arena_t = es.enter_context(nc.sbuf_tensor("arena", [128, ARENA_N], BF16))
    AR = Arena(arena_t, ARENA_N)

    def sb(shape, dt):
        return AR.alloc(list(shape), dt)

    bufn = [0]

    def nb(prefix="b"):
        bufn[0] += 1
        return Buf("%s%d" % (prefix, bufn[0]))

    banks = []
    for i in range(8):
        t = es.enter_context(nc.psum_tensor("bank%d" % i, [128, 512], F32))
        banks.append((t, Buf("bank%d" % i)))

    ident = sb([128, 128], BF16)
    b_ident = nb("ident")
    S.dma("pool", ident, i_ident, [], [b_ident], b_ident)
    stg_bf = Rot([(sb([128, 512], BF16), nb("stgb")) for i in range(3)])
    stg_f = Rot([(sb([128, 512], F32), nb("stgf")) for i in range(2)])
    wst = {"rot": None, "sz": 0}

    def make_wslots(n, size):
        wst["rot"] = Rot([(sb([128, size], BF16), nb("wslot")) for i in range(n)])
        wst["sz"] = size
    stat = sb([128, 8], F32)
    b_stat = nb("stat")
    persist_mark = AR.off
    evac_i = [0]

    def evac_eng():
        evac_i[0] += 1
        return "act" if evac_i[0] % 2 else "dve"

    def evac(eng, out, in_, reads, writes, func=None, scale=1.0):
        if func is not None or eng == "act":
            f = func if func is not None else AF.Copy
            return S.op("act", lambda a: a.activation(out=out, in_=in_, func=f, scale=float(scale)), reads, writes)
        if scale != 1.0:
            return S.op("dve", lambda v: v.tensor_scalar(out=out, in0=in_, scalar1=float(scale), scalar2=None,
                                                         op0=ALU.mult), reads, writes)
        return S.op("dve", lambda v: v.tensor_copy(out=out, in_=in_), reads, writes)

    def wview(w_ap):
        return w_ap.rearrange("(kc p) n -> p kc n", p=128)

    wcache = {}

    def load_w(w3, kc0, nk, c0, cw, ck=None, ntiles=0):
        if ck is None and wst.get("ck"):
            ck, ntiles = wst["ck"]
        wt, wb = wst["rot"].next()
        sz = wst["sz"]
        assert nk * cw <= sz, (nk, cw, sz)
        flat = wt[:, 0:nk * cw]
        dst = flat.rearrange("p (k n) -> p k n", n=cw)
        if ck is None:
            S.dma("pool", dst, w3[:, kc0:kc0 + nk, c0:c0 + cw], [], [wb], wb)
            return dst, wb
        if ck not in wcache:
            wcache[ck] = dict(ap=nc.dram_tensor("wc_" + ck, [ntiles, 128, sz], BF16, kind="Internal").ap(),
                              buf=Buf("wc_" + ck), idx={})
        ent = wcache[ck]
        key = (kc0, nk, c0, cw)
        if key not in ent["idx"]:
            i = len(ent["idx"])
            assert i < ntiles, (ck, i, ntiles)
            ent["idx"][key] = i
            S.dma("pool", dst, w3[:, kc0:kc0 + nk, c0:c0 + cw], [], [wb], wb)
            flush_spill()
            wst["pend"] = (ent["ap"][i, :, 0:nk * cw], flat, wb, ent["buf"])
        else:
            i = ent["idx"][key]
            S.dma("pool", flat, ent["ap"][i, :, 0:nk * cw], [ent["buf"]], [wb], wb)
            flush_spill()
        return dst, wb

    def flush_spill():
        p = wst.get("pend")
        if p is not None:
            wst["pend"] = None
            S.dma("pool", p[0], p[1], [p[2]], [p[3]], p[2])

    _orig_barrier = S.barrier

    def _barrier():
        flush_spill()
        _orig_barrier()

    S.barrier = _barrier

    mm_rot = Rot(banks[0:4])

    def projF(w3, c0, width, nk, act_tile, b_act, dst_fn, dst_buf, func=None, scale=1.0, post=None, ck=None, nt=0):
        for cb in range(0, width, 512):
            cw = min(512, width - cb)
            wt, wb = load_w(w3, 0, nk, c0 + cb, cw, ck, nt)
            for m0 in range(0, cw, 128):
                mw = min(128, cw - m0)
                bk, bkb = mm_rot.next()
                fns = [lambda t, k=k, m0=m0, mw=mw, bk=bk, wt=wt: t.matmul(
                    bk[0:mw, 0:TT], wt[:, k, m0:m0 + mw], act_tile[:, k, :], start=(k == 0), stop=(k == nk - 1))
                    for k in range(nk)]
                S.group(fns, [wb, b_act], [bkb])
                if post is not None:
                    post(cb + m0, mw, bk, bkb)
                    continue
                st, stb = stg_bf.next()
                evac(evac_eng(), st[0:mw, 0:TT], bk[0:mw, 0:TT], [bkb], [stb], func=func, scale=scale)
                S.dma("sp", dst_fn(cb + m0, mw), st[0:mw, 0:TT], [stb], [dst_buf], stb)

    def projT(w3, c0, width, nk, act_tile, b_act, dst_fn, dst_buf, func=None, f32out=False, kchunk=None, ck=None, nt=0):
        kch = kchunk or nk
        for cb in range(0, width, 512):
            cw = min(512, width - cb)
            bks = [mm_rot.next() for _ in range(4)]
            for k0 in range(0, nk, kch):
                kn = min(kch, nk - k0)
                wt, wb = load_w(w3, k0, kn, c0 + cb, cw, ck, nt)
                for sub in range(4):
                    bk, bkb = bks[sub]
                    fns = [lambda t, k=k, k0=k0, sub=sub, bk=bk, wt=wt, cw=cw: t.matmul(
                        bk[:, 0:cw], act_tile[:, k0 + k, sub * 128:(sub + 1) * 128], wt[:, k, 0:cw],
                        start=(k0 + k == 0), stop=(k0 + k == nk - 1)) for k in range(kn)]
                    S.group(fns, [wb, b_act], [bkb])
            for sub in range(4):
                bk, bkb = bks[sub]
                st, stb = (stg_f if f32out else stg_bf).next()
                evac(evac_eng(), st[:, 0:cw], bk[:, 0:cw], [bkb], [stb], func=func)
                S.dma("sp", dst_fn(sub, cb, cw), st[:, 0:cw], [stb], [dst_buf], stb)

    def alloc_norm():
        d = {}
        d["xrot"] = Rot([(sb([128, D], F32), nb("xin")) for i in range(2)])
        d["xn"] = sb([128, 4, D], BF16)
        d["b_xn"] = [nb("xn") for i in range(4)]
        d["hnT"] = sb([128, KD, TT], BF16)
        d["b_hnT"] = nb("hnT")
        d["gT"] = sb([128, KD], F32)
        d["b_g"] = nb("gT")
        return d

    def rstd_of(nd, src_ap, bsrc, junk, b_junk, col=0):
        S.op("act", lambda a: a.activation(out=junk, in_=src_ap, func=AF.Square,
                                           accum_out=stat[:, col:col + 1]), [bsrc], [b_junk, b_stat])
        S.op("dve", lambda v: v.tensor_scalar(out=stat[:, col + 1:col + 2], in0=stat[:, col:col + 1],
                                              scalar1=1.0 / D, scalar2=EPS, op0=ALU.mult, op1=ALU.add),
             [b_stat], [b_stat])
        S.op("act", lambda a: a.activation(out=stat[:, col + 1:col + 2], in_=stat[:, col + 1:col + 2], func=AF.Sqrt),
             [b_stat], [b_stat])
        S.op("dve", lambda v: v.reciprocal(out=stat[:, col + 2:col + 3], in_=stat[:, col + 1:col + 2]),
             [b_stat], [b_stat])
        return stat[:, col + 2:col + 3]

    def transposes_to_hnT(nd):
        xn, hnT, gT = nd["xn"], nd["hnT"], nd["gT"]
        for kc in range(KD):
            bk, bkb = banks[6 + kc % 2]
            pst = bk[:].bitcast(BF16)
            for sub in range(4):
                S.group([lambda t, sub=sub, kc=kc, pst=pst: t.transpose(
                    out=pst[:, sub * 128:(sub + 1) * 128], in_=xn[:, sub, kc * 128:(kc + 1) * 128], identity=ident)],
                    nd["b_xn"] + [b_ident], [bkb])
            S.op("dve", lambda v, kc=kc, pst=pst: v.tensor_scalar(
                out=hnT[:, kc, :], in0=pst[:, 0:TT], scalar1=gT[:, kc:kc + 1], scalar2=None, op0=ALU.mult),
                 [bkb, nd["b_g"]], [nd["b_hnT"]])

    def phaseA():
        make_wslots(3, 16384 if KD * 512 <= 16384 else KD * 512)
        wst["ck"] = None
        nd = alloc_norm()
        S.dma("sp", nd["gT"], i_g1T, [], [nd["b_g"]], nd["b_g"])
        hnT, b_hnT = nd["hnT"], nd["b_hnT"]
        w_in3 = wview(w_in)
        HS = 128 ** -0.5
        o = 0
        segs = {}
        for nm, wd in (("u", SW), ("q", AW), ("kc", KVW), ("vc", KVW), ("ks", KVW), ("vs", KVW), ("kw", KVW),
                       ("vw", KVW), ("gn", 3 * NH), ("ga", D), ("gb", D)):
            segs[nm] = (o, wd)
            o += wd

        def tile(xsrc, ti, own):
            tok0 = ti * TT
            apos = (SH if own else 0) + tok0
            for sub in range(4):
                xt, xb = nd["xrot"].next()
                S.dma("sp", xt, xsrc[tok0 + sub * 128:tok0 + (sub + 1) * 128, :], [], [xb], xb)
                rs = rstd_of(nd, xt, xb, nd["xn"][:, sub, :], nd["b_xn"][sub])
                S.op("dve", lambda v, xt=xt, sub=sub, rs=rs: v.tensor_scalar(
                    out=nd["xn"][:, sub, :], in0=xt, scalar1=rs, scalar2=None, op0=ALU.mult),
                     [xb, b_stat], [nd["b_xn"][sub]])
            transposes_to_hnT(nd)
            names = ["u", "kc", "vc", "ks", "vs", "kw", "vw"] + (["q", "gn", "ga", "gb"] if own else [])
            for nm in names:
                c0, wd = segs[nm]
                if nm == "u":
                    projF(w_in3, c0, wd, KD, hnT, b_hnT, lambda r, n: s_uT[r:r + n, apos:apos + TT], db["uT"])
                elif nm in ("kc", "ks", "kw"):
                    ki = ("kc", "ks", "kw").index(nm)
                    projF(w_in3, c0, wd, KD, hnT, b_hnT, lambda r, n, ki=ki: s_kT[ki, r:r + n, apos:apos + TT],
                          db["kT"])
                elif nm == "vc":
                    projF(w_in3, c0, wd, KD, hnT, b_hnT, lambda r, n: s_vcT[r:r + n, apos:apos + TT], db["vcT"])
                elif nm in ("vs", "vw"):
                    vi = ("vs", "vw").index(nm)
                    projT(w_in3, c0, wd, KD, hnT, b_hnT,
                          lambda sub, cb, cw, vi=vi: s_vt[vi, apos + sub * 128:apos + (sub + 1) * 128, cb:cb + cw],
                          db["vt"])
                elif nm == "q":
                    projF(w_in3, c0, wd, KD, hnT, b_hnT, lambda r, n: s_qT[r:r + n, tok0:tok0 + TT], db["qT"],
                          scale=HS)
                elif nm == "gn":
                    projT(w_in3, c0, wd, KD, hnT, b_hnT,
                          lambda sub, cb, cw: s_gn[tok0 + sub * 128:tok0 + (sub + 1) * 128, cb:cb + cw], db["gn"],
                          func=AF.Sigmoid, f32out=True)
                elif nm == "ga":
                    projF(w_in3, c0, wd, KD, hnT, b_hnT, lambda r, n: s_gaT[r:r + n, tok0:tok0 + TT], db["gaT"],
                          func=AF.Sigmoid)
                elif nm == "gb":
                    projF(w_in3, c0, wd, KD, hnT, b_hnT, lambda r, n: s_gbT[r:r + n, tok0:tok0 + TT], db["gbT"],
                          func=AF.Sigmoid)

        for ti in range(NTO):
            tile(x_ctx, ti, False)
        for ti in range(NTO):
            tile(x_own, ti, True)

    if "A" in phases:
        phaseA()
    S.barrier()
    AR.off = persist_mark


    def phaseB():
        NK = c.NK

        def tt(eng, out, a, b, op, reads, writes):
            return S.op(eng, lambda v: v.tensor_tensor(out=out, in0=a, in1=b, op=op), reads, writes)

        def ts(eng, out, a, s1, s2, op0, op1, reads, writes):
            if op1 is None:
                return S.op(eng, lambda v: v.tensor_scalar(out=out, in0=a, scalar1=s1, scalar2=None, op0=op0),
                            reads, writes)
            return S.op(eng, lambda v: v.tensor_scalar(out=out, in0=a, scalar1=s1, scalar2=s2, op0=op0, op1=op1),
                        reads, writes)

        def consts(P, Fd, i_ar, i_ai, i_ld, npow):
            bb = nb("s5c")
            B = [bb]
            T = lambda: sb([P, Fd], F32)
            ar, ai, ld = T(), T(), T()
            S.dma("sp", ar, i_ar, [], B, bb)
            S.dma("sp", ai, i_ai, [], B, bb)
            S.dma("sp", ld, i_ld, [], B, bb)
            dt_, lam, th, dec, x2, ps_, pc_, s_, c_, t1, t2 = (T() for _ in range(11))
            S.op("act", lambda a: a.activation(out=dt_, in_=ld, func=AF.Exp), B, B)
            tt("dve", lam, dt_, ar, ALU.mult, B, B)
            tt("dve", th, dt_, ai, ALU.mult, B, B)
            S.op("act", lambda a: a.activation(out=dec, in_=lam, func=AF.Exp), B, B)
            ts("dve", th, th, 1.0 / 64, None, ALU.mult, None, B, B)
            tt("dve", x2, th, th, ALU.mult, B, B)

            def horner(out, coeffs):
                ts("dve", out, x2, coeffs[0], coeffs[1], ALU.mult, ALU.add, B, B)
                for cf in coeffs[2:]:
                    tt("dve", out, out, x2, ALU.mult, B, B)
                    ts("dve", out, out, cf, None, ALU.add, None, B, B)

            horner(ps_, [1.0 / 362880, -1.0 / 5040, 1.0 / 120, -1.0 / 6, 1.0])
            tt("dve", s_, ps_, th, ALU.mult, B, B)
            horner(c_, [1.0 / 40320, -1.0 / 720, 1.0 / 24, -0.5, 1.0])

            def dbl(co, so, ci, si):
                tt("dve", t1, si, si, ALU.mult, B, B)
                tt("dve", t2, ci, ci, ALU.mult, B, B)
                S.op("dve", lambda v: v.scalar_tensor_tensor(out=so, in0=si, scalar=2.0, in1=ci, op0=ALU.mult,
                                                             op1=ALU.mult), B, B)
                tt("dve", co, t2, t1, ALU.subtract, B, B)

            def renorm(cc, ss):
                tt("dve", t1, ss, ss, ALU.mult, B, B)
                tt("dve", t2, cc, cc, ALU.mult, B, B)
                tt("dve", t1, t1, t2, ALU.add, B, B)
                ts("dve", t1, t1, -0.5, 1.5, ALU.mult, ALU.add, B, B)
                tt("dve", cc, cc, t1, ALU.mult, B, B)
                tt("dve", ss, ss, t1, ALU.mult, B, B)

            c2, s2 = T(), T()
            cur = (c_, s_)
            oth = (c2, s2)
            for i in range(6):
                dbl(oth[0], oth[1], cur[0], cur[1])
                cur, oth = oth, cur
            renorm(cur[0], cur[1])
            wr, wi = [cur[0]], [cur[1]]
            for k in range(1, npow):
                a, b = T(), T()
                dbl(a, b, wr[-1], wi[-1])
                renorm(a, b)
                wr.append(a)
                wi.append(b)
            abr, abi, den, m, zr, zi = (T() for _ in range(6))
            tt("dve", abr, dec, wr[0], ALU.mult, B, B)
            tt("dve", abi, dec, wi[0], ALU.mult, B, B)
            tt("dve", t1, ar, ar, ALU.mult, B, B)
            tt("dve", t2, ai, ai, ALU.mult, B, B)
            tt("dve", den, t1, t2, ALU.add, B, B)
            S.op("dve", lambda v: v.reciprocal(out=den, in_=den), B, B)
            ts("dve", m, abr, -1.0, None, ALU.add, None, B, B)
            tt("dve", t1, m, ar, ALU.mult, B, B)
            tt("dve", t2, abi, ai, ALU.mult, B, B)
            tt("dve", t1, t1, t2, ALU.add, B, B)
            tt("dve", zr, t1, den, ALU.mult, B, B)
            tt("dve", t1, abi, ar, ALU.mult, B, B)
            tt("dve", t2, m, ai, ALU.mult, B, B)
            tt("dve", t1, t1, t2, ALU.subtract, B, B)
            tt("dve", zi, t1, den, ALU.mult, B, B)
            return dict(dec=dec, wr=wr, wi=wi, zr=zr, zi=zi, buf=bb)

        NT = SEQ // TT
        LT = int(math.log2(TT))
        mark0 = AR.off
        cg = consts(NG, 64, i_are_gp, i_aim_gp, i_ldt_gp, LT + 1)
        BG = [cg["buf"]]
        zr, zi, cT, sT = cg["zr"], cg["zi"], cg["wr"][LT], cg["wi"][LT]
        z2r, z2i, zt = sb([NG, 64], F32), sb([NG, 64], F32), sb([NG, 64], F32)
        cur, oth = (zr, zi), (z2r, z2i)
        for j in range(NT):
            S.dma("sp", s_zs[j, 0].rearrange("(g p) -> g p", p=64), cur[0], BG, [db["zs"]], db["zs"])
            S.dma("sp", s_zs[j, 1].rearrange("(g p) -> g p", p=64), cur[1], BG, [db["zs"]], db["zs"])
            if j == NT - 1:
                break
            tt("dve", zt, cur[1], sT, ALU.mult, BG, BG)
            tt("dve", oth[0], cur[0], cT, ALU.mult, BG, BG)
            tt("dve", oth[0], oth[0], zt, ALU.add, BG, BG)
            tt("dve", zt, cur[0], sT, ALU.mult, BG, BG)
            tt("dve", oth[1], cur[1], cT, ALU.mult, BG, BG)
            tt("dve", oth[1], oth[1], zt, ALU.subtract, BG, BG)
            cur, oth = oth, cur
        S.barrier()
        AR.off = mark0
        cp = consts(128, NG, i_are_pg, i_aim_pg, i_ldt_pg, LT + 1)
        b_cp = cp["buf"]
        BP = [b_cp]
        cT, sT = cp["wr"][LT], cp["wi"][LT]
        cosP = sb([128, NT, NG], F32)
        sinP = sb([128, NT, NG], F32)
        ztp = sb([128, NG], F32)
        S.op("dve", lambda v: v.memset(cosP[:, 0, :], 1.0), [], BP)
        S.op("dve", lambda v: v.memset(sinP[:, 0, :], 0.0), [], BP)
        for j in range(1, NT):
            tt("dve", ztp, sinP[:, j - 1, :], sT, ALU.mult, BP, BP)
            tt("dve", cosP[:, j, :], cosP[:, j - 1, :], cT, ALU.mult, BP, BP)
            tt("dve", cosP[:, j, :], cosP[:, j, :], ztp, ALU.subtract, BP, BP)
            tt("dve", ztp, cosP[:, j - 1, :], sT, ALU.mult, BP, BP)
            tt("dve", sinP[:, j, :], sinP[:, j - 1, :], cT, ALU.mult, BP, BP)
            tt("dve", sinP[:, j, :], sinP[:, j, :], ztp, ALU.add, BP, BP)
        NA_ = TT // 32
        mt = AR.off
        for (nm, nlen, k0) in (("B", 32, 0), ("A", NA_, 5)):
            Tc = sb([128, NG, nlen], F32)
            Ts = sb([128, NG, nlen], F32)
            q1 = sb([128, NG, nlen // 2], F32)
            q2 = sb([128, NG, nlen // 2], F32)
            b_T = nb("T2")
            BT = [b_T]
            S.op("dve", lambda v: v.memset(Tc[:, :, 0:1], 1.0), [], BT)
            S.op("dve", lambda v: v.memset(Ts[:, :, 0:1], 0.0), [], BT)
            for k in range(int(math.log2(nlen))):
                n_ = 1 << k
                wrb = cp["wr"][k0 + k].unsqueeze(2).to_broadcast([128, NG, n_])
                wib = cp["wi"][k0 + k].unsqueeze(2).to_broadcast([128, NG, n_])
                tt("dve", q1[:, :, 0:n_], Ts[:, :, 0:n_], wib, ALU.mult, BT + BP, BT)
                tt("dve", q2[:, :, 0:n_], Tc[:, :, 0:n_], wrb, ALU.mult, BT + BP, BT)
                tt("dve", Tc[:, :, n_:2 * n_], q2[:, :, 0:n_], q1[:, :, 0:n_], ALU.subtract, BT, BT)
                tt("dve", q1[:, :, 0:n_], Tc[:, :, 0:n_], wib, ALU.mult, BT + BP, BT)
                tt("dve", q2[:, :, 0:n_], Ts[:, :, 0:n_], wrb, ALU.mult, BT + BP, BT)
                tt("dve", Ts[:, :, n_:2 * n_], q1[:, :, 0:n_], q2[:, :, 0:n_], ALU.add, BT, BT)
            o_ = 0 if nm == "B" else 64
            S.dma("sp", s_tab[:, :, o_:o_ + nlen], Tc, BT, [db["tab"]], b_T)
            S.dma("sp", s_tab[:, :, o_ + nlen:o_ + 2 * nlen], Ts, BT, [db["tab"]], b_T)
        S.barrier()
        AR.off = mt
        W1 = sb([128, NT // 2, NG * 16], BF16)
        W2 = sb([128, NT // 2, NG * 16], BF16)
        b_W = nb("W12")
        sgn = sb([128, 1], F32)
        S.op("dve", lambda v: v.memset(sgn[0:64, :], 1.0), [], [b_W])
        S.op("dve", lambda v: v.memset(sgn[64:128, :], -1.0), [], [b_W])
        dsk = sb([16, NG], F32)
        flag = sb([128, 1], F32)
        diagd = sb([16, NG, 16], BF16)
        identf = sb([16, 16], F32)
        ident128f = sb([128, 128], F32)
        b_misc = nb("misc")
        S.dma("sp", dsk, i_dsk, [], [b_misc], b_misc)
        S.dma("sp", flag, i_flag, [], [b_misc], b_misc)
        S.op("dve", lambda v: v.tensor_copy(out=identf, in_=ident[0:16, 0:16]), [b_ident], [b_misc])
        S.op("dve", lambda v: v.tensor_copy(out=ident128f, in_=ident), [b_ident], [b_misc])
        S.op("dve", lambda v: v.tensor_tensor(out=diagd, in0=identf.unsqueeze(1).to_broadcast([16, NG, 16]),
                                              in1=dsk.unsqueeze(2).to_broadcast([16, NG, 16]), op=ALU.mult),
             [b_misc], [b_misc])
        mark1 = AR.off
        cc1 = sb([128, NG * 16], F32)
        cc2 = sb([128, NG * 16], F32)
        wa = sb([128, NG * 16], F32)
        wb_ = sb([128, NG * 16], F32)
        b_cc = nb("cc")
        BCC = [b_cc]
        S.dma("sp", cc1, i_cc1, [], BCC, b_cc)
        S.dma("sp", cc2, i_cc2, [], BCC, b_cc)
        v3w = lambda a: a.rearrange("p (g c) -> p g c", c=16)
        for j in range(NT // 2, NT):
            cb = cosP[:, j, :].unsqueeze(2).to_broadcast([128, NG, 16])
            sb_ = sinP[:, j, :].unsqueeze(2).to_broadcast([128, NG, 16])
            tt("dve", v3w(wa), v3w(cc1), cb, ALU.mult, BCC + BP, BCC)
            tt("dve", v3w(wb_), v3w(cc2), sb_, ALU.mult, BCC + BP, BCC)
            S.op("dve", lambda v, j=j: v.scalar_tensor_tensor(out=W1[:, j - NT // 2, :], in0=wa, scalar=sgn[:, 0:1], in1=wb_,
                                                              op0=ALU.mult, op1=ALU.subtract), BCC + [b_W], [b_W])
            tt("dve", v3w(wa), v3w(cc2), cb, ALU.mult, BCC + BP, BCC)
            tt("dve", v3w(wb_), v3w(cc1), sb_, ALU.mult, BCC + BP, BCC)
            S.op("dve", lambda v: v.scalar_tensor_tensor(out=wa, in0=wb_, scalar=sgn[:, 0:1], in1=wa,
                                                         op0=ALU.mult, op1=ALU.add), BCC + [b_W], BCC)
            S.op("dve", lambda v, j=j: v.tensor_scalar(out=W2[:, j - NT // 2, :], in0=wa, scalar1=-1.0, scalar2=None,
                                                       op0=ALU.mult), BCC, [b_W])
        S.barrier()
        AR.off = mark1
        GC = min(16, NG)
        NGC = NG // GC
        PB = NGC * 16
        n = GC * 64
        zrb, zib, bre, bim, t1, t2 = (sb([PB, n], F32) for _ in range(6))
        obs = [(sb([PB, GC, 256], BF16), nb("ob")) for _ in range(2)]
        bz, bbi = nb("bz"), nb("bbi")
        v3 = lambda a: a.rearrange("c (g p) -> c g p", p=64)
        for gc in range(NGC):
            ps_ = slice(gc * 16, (gc + 1) * 16)
            S.dma("sp", bre[ps_, :], i_bre[:, gc * n:(gc + 1) * n], [], [bbi], bbi)
            S.dma("sp", bim[ps_, :], i_bim[:, gc * n:(gc + 1) * n], [], [bbi], bbi)
        for j in range(NT):
            for gc in range(NGC):
                ps_ = slice(gc * 16, (gc + 1) * 16)
                S.dma("sp", zrb[ps_, :], s_zs[j, 0, gc * n:(gc + 1) * n].partition_broadcast(16), [db["zs"]], [bz], bz)
                S.dma("sp", zib[ps_, :], s_zs[j, 1, gc * n:(gc + 1) * n].partition_broadcast(16), [db["zs"]], [bz], bz)
            ob, b_ob = obs[j % 2]
            B = [bz]
            tt("dve", t1, zrb, bre, ALU.mult, B + [bbi], B)
            tt("dve", t2, zib, bim, ALU.mult, B + [bbi], B)
            tt("dve", t1, t1, t2, ALU.subtract, B, B)
            tt("dve", t2, zrb, bim, ALU.mult, B + [bbi], B)
            tt("dve", zrb, zib, bre, ALU.mult, B + [bbi], B)
            tt("dve", t2, t2, zrb, ALU.add, B, B)
            S.op("dve", lambda v: v.tensor_copy(out=ob[:, :, 0:64], in_=v3(t1)), B, [b_ob])
            S.op("dve", lambda v: v.tensor_copy(out=ob[:, :, 64:128], in_=v3(t2)), B, [b_ob])
            S.op("dve", lambda v: v.tensor_copy(out=ob[:, :, 128:192], in_=v3(t2)), B, [b_ob])
            S.op("dve", lambda v: v.tensor_scalar(out=ob[:, :, 192:256], in0=v3(t1), scalar1=-1.0, scalar2=None,
                                                  op0=ALU.mult), B, [b_ob])
            for gc in range(NGC):
                ps_ = slice(gc * 16, (gc + 1) * 16)
                S.dma("sp", s_bbar[gc * GC:(gc + 1) * GC, j].rearrange("g c f -> c g f"), ob[ps_], [b_ob],
                      [db["bbar"]], b_ob)
        S.barrier()
        AR.off = mark1

        NTAB = NUG = (4 if NT >= 6 else 5)
        tabs = [(sb([128, TT], F32), sb([128, TT], F32), nb("tab")) for _ in range(NTAB)]
        ugs = [(sb([16, SEQ], BF16), sb([16, NT, 256], BF16), nb("ug")) for _ in range(NUG)]
        tmpA = (sb([128, TT], F32), nb("tmpA"))
        tmpB = (sb([128, TT], F32), nb("tmpB"))
        tls = [(sb([128, 64 + 2 * NA_], F32), nb("tl")) for _ in range(NTAB)]
        t1s = [(sb([128, TT], F32), nb("t1s")) for _ in range(3)]
        t2s = [(sb([128, TT], F32), nb("t2s")) for _ in range(2)]
        sbBs = [(sb([128, TT], F32), nb("sbB")) for _ in range(2)]
        gts = [(sb([128, TT], F32), nb("gt")) for _ in range(3)]
        X1s = [(sb([128, TT], BF16), nb("X1")) for _ in range(2)]
        X2s = [(sb([128, TT], BF16), nb("X2")) for _ in range(2)]
        ysts = [(sb([16, TT], F32), nb("yst")) for _ in range(3)]
        init = sb([128, 1], F32)
        b_init = nb("init")
        NTOT = NG * NT

        def load_group(g):
            ug, bbt, b_ug = ugs[g % NUG]
            S.dma("sp", ug, s_uT[g * 16:(g + 1) * 16, :], [db["uT"]], [b_ug], b_ug)
            S.dma("sp", bbt, s_bbar[g].rearrange("j c f -> c j f"), [db["bbar"]], [b_ug], b_ug)

        def tab_level(g, lvl):
            Ct, St, b_tab = tabs[g % NTAB]
            tl, b_tl = tls[g % NTAB]
            ta, b_ta = tmpA
            tb, b_tb = tmpB
            shp = [128, NA_, 32]
            Bc = tl[:, 0:32].unsqueeze(1).to_broadcast(shp)
            Bs = tl[:, 32:64].unsqueeze(1).to_broadcast(shp)
            Ac = tl[:, 64:64 + NA_].unsqueeze(2).to_broadcast(shp)
            As = tl[:, 64 + NA_:64 + 2 * NA_].unsqueeze(2).to_broadcast(shp)
            v3t = lambda a: a.rearrange("p (a b) -> p a b", b=32)
            def outer(dst, acol0, b0, bdst):
                d3 = v3t(dst)
                for a_ in range(NA_):
                    S.op("act", lambda a, a_=a_: a.activation(out=d3[:, a_, :], in_=tl[:, b0:b0 + 32], func=AF.Copy,
                                                              scale=tl[:, acol0 + a_:acol0 + a_ + 1]),
                         [b_tl], [bdst])

            if lvl == 0:
                S.dma("sp", tl, s_tab[:, g, :], [db["tab"]], [b_tl], b_tl)
                outer(ta, 64, 0, b_ta)
            elif lvl == 1:
                outer(tb, 64 + NA_, 32, b_tb)
            elif lvl == 2:
                tt("dve", Ct, ta, tb, ALU.subtract, [b_ta, b_tb], [b_tab])
            elif lvl == 3:
                outer(ta, 64 + NA_, 0, b_ta)
            elif lvl == 4:
                outer(tb, 64, 32, b_tb)
            elif lvl == 5:
                tt("dve", St, ta, tb, ALU.add, [b_ta, b_tb], [b_tab])

        NLV = 6
        for g in range(min(2, NG)):
            load_group(g)
            for lvl in range(NLV):
                tab_level(g, lvl)
        lv_per_it = (NLV + NT - 1) // NT
        prevgt = {}
        for it in range(NTOT + 8):
            gi, ji = divmod(it, NT)
            if it < NTOT:
                if ji == 0 and gi + 2 < NG:
                    load_group(gi + 2)
                if gi + 2 < NG:
                    for lvl in range(ji * lv_per_it, min(NLV, (ji + 1) * lv_per_it)):
                        tab_level(gi + 2, lvl)
            i = it
            if 0 <= i < NTOT:
                g, j = divmod(i, NT)
                ug, bbt, b_ug = ugs[g % NUG]
                js = slice(j * TT, (j + 1) * TT)
                bA, bAb = banks[(i % 2) * 2]
                bB, bBb = banks[(i % 2) * 2 + 1]
                S.group([lambda t: t.matmul(bA[:, :], bbt[:, j, 0:128], ug[:, js], start=True, stop=True)], [b_ug], [bAb])
                S.group([lambda t: t.matmul(bB[:, :], bbt[:, j, 128:256], ug[:, js], start=True, stop=True)], [b_ug], [bBb])
            i = it - 1
            if 0 <= i < NTOT:
                g, j = divmod(i, NT)
                Ct, St, b_tab = tabs[g % NTAB]
                bA, bAb = banks[(i % 2) * 2]
                bB, bBb = banks[(i % 2) * 2 + 1]
                t1, b_t1 = t1s[i % 3]
                sbB, b_sbB = sbBs[i % 2]
                tt("dve", t1, bA[:, :], Ct, ALU.mult, [bAb, b_tab], [b_t1])
                S.op("act", lambda a: a.activation(out=sbB, in_=bB[:, :], func=AF.Copy), [bBb], [b_sbB])
            i = it - 2
            if 0 <= i < NTOT:
                g, j = divmod(i, NT)
                Ct, St, b_tab = tabs[g % NTAB]
                sbB, b_sbB = sbBs[i % 2]
                t2, b_t2 = t2s[i % 2]
                tt("pool", t2, sbB, St, ALU.mult, [b_sbB, b_tab], [b_t2])
            i = it - 3
            if 0 <= i < NTOT:
                t1, b_t1 = t1s[i % 3]
                t2, b_t2 = t2s[i % 2]
                bP, bPb = banks[6 + i % 2]
                S.group([lambda t: t.matmul(bP[:, :], ident128f, t1, start=True, stop=False),
                         lambda t: t.matmul(bP[:, :], ident128f, t2, start=False, stop=True)],
                        [b_t1, b_t2, b_misc], [bPb])
            i = it - 4
            if 0 <= i < NTOT:
                g, j = divmod(i, NT)
                bP, bPb = banks[6 + i % 2]
                gt, b_gt = gts[i % 3]
                rbc = cp["dec"][:, g:g + 1].to_broadcast([128, TT])
                if j == 0:
                    ini, rd = 0.0, []
                elif j == NT // 2:
                    pgt = prevgt[i - 1]
                    S.op("dve", lambda v: v.tensor_tensor(out=init, in0=pgt[0][:, TT - 1:TT], in1=flag, op=ALU.mult),
                         [pgt[1], b_misc], [b_init])
                    ini, rd = init, [b_init]
                else:
                    pgt = prevgt[i - 1]
                    ini, rd = pgt[0][:, TT - 1:TT], [pgt[1]]
                S.op("dve", lambda v: v.tensor_tensor_scan(out=gt, data0=rbc, data1=bP[:, :], initial=ini, op0=ALU.mult,
                                                           op1=ALU.add), [bPb, b_cp] + rd, [b_gt])
                prevgt[i] = (gt, b_gt)
                prevgt.pop(i - 2, None)
            i = it - 5
            if 0 <= i < NTOT and (i % NT) >= NT // 2:
                g, j = divmod(i, NT)
                Ct, St, b_tab = tabs[g % NTAB]
                gt, b_gt = gts[i % 3]
                X1, b_X1 = X1s[i % 2]
                X2, b_X2 = X2s[i % 2]
                tt("pool", X1, gt, Ct, ALU.mult, [b_gt, b_tab], [b_X1])
                tt("pool", X2, gt, St, ALU.mult, [b_gt, b_tab], [b_X2])
            i = it - 6
            if 0 <= i < NTOT and (i % NT) >= NT // 2:
                g, j = divmod(i, NT)
                ug, bbt, b_ug = ugs[g % NUG]
                js = slice(j * TT, (j + 1) * TT)
                X1, b_X1 = X1s[i % 2]
                X2, b_X2 = X2s[i % 2]
                bY, bYb = banks[4 + i % 2]
                gsl = slice(g * 16, (g + 1) * 16)
                S.group([lambda t: t.matmul(bY[0:16, :], W1[:, j - NT // 2, gsl], X1, start=True, stop=False),
                         lambda t: t.matmul(bY[0:16, :], W2[:, j - NT // 2, gsl], X2, start=False, stop=False),
                         lambda t: t.matmul(bY[0:16, :], diagd[:, g, :], ug[:, js], start=False, stop=True)],
                        [b_W, b_X1, b_X2, b_ug, b_misc], [bYb])
            i = it - 7
            if 0 <= i < NTOT and (i % NT) >= NT // 2:
                g, j = divmod(i, NT)
                bY, bYb = banks[4 + i % 2]
                yst, b_yst = ysts[i % 3]
                S.op("act", lambda a: a.activation(out=yst, in_=bY[0:16, :], func=AF.Copy), [bYb], [b_yst])
                o0 = (j - NT // 2) * TT
                S.dma("sp", s_yaT[g * 16:(g + 1) * 16, o0:o0 + TT], yst, [b_yst], [db["yaT"]], b_yst)

    if "B" in phases:
        phaseB()
    S.barrier()
    AR.off = persist_mark
    def phaseN():
        NQ4 = NQC
        b_k = nb("ncon")
        BK = [b_k]

        def cload(shape, src, dt=BF16, q="pool", **kw):
            t = sb(shape, dt)
            S.dma(q, t, src, [], BK, b_k, **kw)
            return t

        w1k = cload([128, 32, 128], i_w1k.rearrange("(l d) h -> d l h", d=128))
        w1v = cload([128, 32, 128], i_w1v.rearrange("(l d) h -> d l h", d=128))
        w2k = cload([128, 128], i_w2k)
        w2v = cload([128, 128], i_w2v)
        pek = cload([128, 32], i_pek)
        pev = cload([128, 32], i_pev)
        cmpb = cload([128, NCT, SH], i_cmpb.rearrange("(a p) t -> p a t", p=128))
        expd = cload([NB, NKT * 128], i_exp, max_dma_last_dim=4096).rearrange("j (k m) -> j k m", m=128)
        caus = cload([128, 4, 512], i_caus.rearrange("(v p) t -> p v t", p=128))
        wlo = cload([128, 128], i_wlo)
        whi = cload([128, 128], i_whi)
        wctx = cload([128, 128], i_wctx)
        selA = cload([128, NQT, NB], i_selA.rearrange("(q p) j -> p q j", p=128), F32, "sp")
        selB = cload([128, NQT, NB], i_selB.rearrange("(q p) j -> p q j", p=128), F32, "sp")
        selM = cload([128, NQT, NB], i_selM.rearrange("(q p) j -> p q j", p=128), F32, "sp")
        gnt = sb([128, NQT, 3 * NH], F32)
        S.dma("sp", gnt, s_gn.rearrange("(q p) j -> p q j", p=128), [db["gn"]], BK, b_k)
        NW = 128 + 1 + NB
        pebias = sb([128, 2], F32)
        for i, (w1, pe) in enumerate(((w1k, pek), (w1v, pev))):
            bk, bkb = banks[4 + i]
            S.group([lambda t, l=l, w1=w1, pe=pe, bk=bk: t.matmul(bk[:, 0:1], w1[:, l, :], pe[:, l:l + 1],
                                                                 start=(l == 0), stop=(l == 31)) for l in range(32)],
                    BK, [bkb])
            S.op("dve", lambda v, i=i, bk=bk: v.tensor_copy(out=pebias[:, i:i + 1], in_=bk[:, 0:1]), [bkb], BK)
        mark = AR.off
        for g in range(NKV):
            AR.off = mark
            b_kv = nb("kv")
            BKV = [b_kv]
            kcT = sb([128, SEQ], BF16)
            vcT = sb([128, SEQ], BF16)
            ksT = sb([128, SEQ], BF16)
            kwT = sb([128, SEQ], BF16)
            vs = sb([128, NKT, 132], BF16)
            vw = sb([128, NKT, 132], BF16)
            qT = sb([128, HPG, SH], BF16)
            gs = slice(g * 128, (g + 1) * 128)
            S.dma("sp", kcT, s_kT[0, gs, :], [db["kT"]], BKV, b_kv)
            S.dma("sp", ksT, s_kT[1, gs, :], [db["kT"]], BKV, b_kv)
            S.dma("sp", kwT, s_kT[2, gs, :], [db["kT"]], BKV, b_kv)
            S.dma("sp", vcT, s_vcT[gs, :], [db["vcT"]], BKV, b_kv)
            S.dma("sp", vs[:, :, 0:128], s_vt[0, :, gs].rearrange("(k p) d -> p k d", p=128), [db["vt"]], BKV, b_kv)
            S.dma("sp", vw[:, :, 0:128], s_vt[1, :, gs].rearrange("(k p) d -> p k d", p=128), [db["vt"]], BKV, b_kv)
            S.dma("sp", qT, s_qT[g * HPG * 128:(g + 1) * HPG * 128, :].rearrange("(h d) t -> d h t", d=128),
                  [db["qT"]], BKV, b_kv)
            S.op("pool", lambda v, vs=vs: v.memset(vs[:, :, 128:129], 1.0), [], BKV)
            S.op("pool", lambda v, vw=vw: v.memset(vw[:, :, 128:129], 1.0), [], BKV)
            hT = sb([128, 2, NC_], BF16)
            kcmpT = sb([128, NC_], BF16)
            vext = sb([128, NCT, NW], BF16)
            b_cmp = nb("cmp")
            BC = [b_cmp]
            S.op("pool", lambda v, vext=vext: v.memset(vext[:, :, 128:129], 1.0), [], BC)
            S.dma("pool", vext[:, :, 129:NW], i_ovl.rearrange("(a p) j -> p a j", p=128), [], BC, b_cmp)
            gtmp = sb([128, 3, NC_], F32)
            for i, (srcT, w1) in enumerate(((kcT, w1k), (vcT, w1v))):
                bk, bkb = banks[i]
                S.group([lambda t, l=l, w1=w1, srcT=srcT, bk=bk: t.matmul(
                    bk[:, 0:NC_], w1[:, l, :], srcT[:, l:l + 16 * (NC_ - 1) + 1:16], start=(l == 0), stop=(l == 31))
                    for l in range(32)], BK + BKV, [bkb])
                xx, x3, sg = gtmp[:, 0, :], gtmp[:, 1, :], gtmp[:, 2, :]
                S.op("dve", lambda v, xx=xx, bk=bk, i=i: v.tensor_scalar(
                    out=xx, in0=bk[:, 0:NC_], scalar1=pebias[:, i:i + 1], scalar2=None, op0=ALU.add), [bkb] + BK, BC)
                S.op("dve", lambda v, xx=xx, x3=x3: v.tensor_tensor(out=x3, in0=xx, in1=xx, op=ALU.mult), BC, BC)
                S.op("dve", lambda v, x3=x3: v.tensor_scalar(out=x3, in0=x3, scalar1=0.044715, scalar2=1.0,
                                                              op0=ALU.mult, op1=ALU.add), BC, BC)
                S.op("dve", lambda v, xx=xx, x3=x3: v.tensor_tensor(out=x3, in0=x3, in1=xx, op=ALU.mult), BC, BC)
                S.op("act", lambda a, x3=x3, sg=sg: a.activation(out=sg, in_=x3, func=AF.Sigmoid, scale=1.5957691216),
                     BC, BC)
                S.op("dve", lambda v, xx=xx, sg=sg, i=i: v.tensor_tensor(out=hT[:, i, :], in0=xx, in1=sg, op=ALU.mult),
                     BC, BC)
            bk, bkb = banks[2]
            S.group([lambda t, bk=bk: t.matmul(bk[:, 0:NC_], w2k, hT[:, 0, :], start=True, stop=True)], BK + BC,
                    [bkb])
            S.op("act", lambda a, bk=bk: a.activation(out=kcmpT, in_=bk[:, 0:NC_], func=AF.Copy), [bkb], BC)
            for a_ in range(NCT):
                na = min(128, NC_ - a_ * 128)
                bk, bkb = banks[3]
                S.group([lambda t, bk=bk, a_=a_, na=na: t.matmul(bk[0:na, 0:128], hT[:, 1, a_ * 128:a_ * 128 + na],
                                                                 w2v, start=True, stop=True)], BK + BC, [bkb])
                S.op("dve", lambda v, bk=bk, a_=a_, na=na: v.tensor_copy(out=vext[0:na, a_, 0:128],
                                                                         in_=bk[0:na, 0:128]), [bkb], BC)
            yb = sb([128, NQT, HPG * 128], F32)
            b_yb = [nb("yb") for _ in range(NQT)]
            imp = sb([128, NQT, NB], F32)
            b_imp = [nb("imp") for _ in range(NQT)]
            sc8 = sb([128, 16], F32)
            b_sc = nb("sc8")
            erot = Rot([(sb([128, 512], BF16), nb("e")) for _ in range(4)])

            def gate_scale(qt, h, br, den_ap, den_reads):
                col = br * NH + g * HPG + h
                S.op("dve", lambda v: v.tensor_scalar(out=sc8[:, 1:2], in0=den_ap, scalar1=1e-30, scalar2=None,
                                                      op0=ALU.max), den_reads, [b_sc])
                S.op("dve", lambda v: v.reciprocal(out=sc8[:, 2:3], in_=sc8[:, 1:2]), [b_sc], [b_sc])
                S.op("dve", lambda v: v.tensor_tensor(out=sc8[:, 0:1], in0=sc8[:, 2:3], in1=gnt[:, qt, col:col + 1],
                                                      op=ALU.mult), [b_sc] + BK, [b_sc])

            def accum_out(qt, h, num_ap, num_reads, first):
                dst = yb[:, qt, h * 128:(h + 1) * 128]
                if first:
                    S.op("dve", lambda v: v.tensor_scalar(out=dst, in0=num_ap, scalar1=sc8[:, 0:1], scalar2=None,
                                                          op0=ALU.mult), num_reads + [b_sc], [b_yb[qt]])
                else:
                    S.op("dve", lambda v: v.scalar_tensor_tensor(out=dst, in0=num_ap, scalar=sc8[:, 0:1], in1=dst,
                                                                 op0=ALU.mult, op1=ALU.add),
                         num_reads + [b_sc], [b_yb[qt]])

            selbT = sb([NB, SH], BF16)
            b_selT = nb("selT")
            scr = sb([128, 3, NB], F32)
            m8 = sb([128, 16], F32)
            selb = sb([128, NB], BF16)
            b_s3 = nb("s3")
            B3 = [b_s3]

            def n3(qt):
                sc_, wk_, se_ = scr[:, 0, :], scr[:, 1, :], scr[:, 2, :]
                S.op("dve", lambda v: v.tensor_tensor(out=sc_, in0=imp[:, qt, :], in1=selA[:, qt, :], op=ALU.mult),
                     [b_imp[qt]] + BK, B3)
                S.op("dve", lambda v: v.tensor_tensor(out=sc_, in0=sc_, in1=selB[:, qt, :], op=ALU.add), BK, B3)
                S.op("dve", lambda v: v.max(out=m8[:, 0:8], in_=sc_), B3, B3)
                S.op("dve", lambda v: v.match_replace(out=wk_, in_to_replace=m8[:, 0:8], in_values=sc_,
                                                      imm_value=-3.0e38), B3, B3)
                S.op("dve", lambda v: v.max(out=m8[:, 8:16], in_=wk_), B3, B3)
                S.op("dve", lambda v: v.tensor_scalar(out=se_, in0=sc_, scalar1=m8[:, 15:16], scalar2=None,
                                                      op0=ALU.is_ge), B3, B3)
                S.op("dve", lambda v: v.tensor_tensor(out=se_, in0=se_, in1=selM[:, qt, :], op=ALU.mult), BK, B3)
                S.op("dve", lambda v: v.tensor_scalar(out=selb, in0=se_, scalar1=-1.0, scalar2=-NEG, op0=ALU.add,
                                                      op1=ALU.mult), B3, B3)
                bk, bkb = banks[6 + qt % 2]
                pst = bk[:].bitcast(BF16)
                S.group([lambda t: t.transpose(out=pst[0:NB, 0:128], in_=selb, identity=ident)], B3 + [b_ident], [bkb])
                S.op("act", lambda a: a.activation(out=selbT[:, qt * 128:(qt + 1) * 128], in_=pst[0:NB, 0:128],
                                                   func=AF.Copy), [bkb], [b_selT])

            def n2_scores(h, qc):
                qs = slice(qc * 512, (qc + 1) * 512)
                es_ = []
                for a_ in range(NCT):
                    na = min(128, NC_ - a_ * 128)
                    bk, bkb = banks[a_ % 2]
                    S.group([lambda t: t.matmul(bk[0:na, :], kcmpT[:, a_ * 128:a_ * 128 + na], qT[:, h, qs],
                                                start=True, stop=False),
                             lambda t: t.matmul(bk[0:na, :], ident[0:na, 0:na], cmpb[0:na, a_, qs], start=False,
                                                stop=True)], BK + BKV + BC + [b_ident], [bkb])
                    e, b_e = erot.next()
                    S.op("act", lambda a: a.activation(out=e[0:na, :], in_=bk[0:na, :], func=AF.Exp), [bkb], [b_e])
                    es_.append((e, b_e, na))
                return es_

            def n2_pv(h, qc, es_):
                for q4 in range(4):
                    qt = qc * 4 + q4
                    bk, bkb = banks[2 + q4 % 2]
                    S.group([lambda t, a_=a_, e=es_[a_][0], na=es_[a_][2]: t.matmul(
                        bk[:, 0:NW], e[0:na, q4 * 128:(q4 + 1) * 128], vext[0:na, a_, :], start=(a_ == 0),
                        stop=(a_ == NCT - 1)) for a_ in range(NCT)], [x[1] for x in es_] + BC, [bkb])
                    gate_scale(qt, h, 0, bk[:, 128:129], [bkb])
                    accum_out(qt, h, bk[:, 0:128], [bkb], True)
                    if h == 0:
                        S.op("dve", lambda v: v.tensor_scalar(out=imp[:, qt, :], in0=bk[:, 129:NW],
                                                              scalar1=sc8[:, 2:3], scalar2=None, op0=ALU.mult),
                             [bkb, b_sc], [b_imp[qt]])
                    else:
                        S.op("dve", lambda v: v.scalar_tensor_tensor(out=imp[:, qt, :], in0=bk[:, 129:NW],
                                                                     scalar=sc8[:, 2:3], in1=imp[:, qt, :],
                                                                     op0=ALU.mult, op1=ALU.add),
                             [bkb, b_sc], [b_imp[qt]])

            pend = None
            for qc in range(NQ4):
                for h in range(HPG):
                    es_ = n2_scores(h, qc)
                    if pend is not None:
                        n2_pv(*pend)
                        if pend[0] == HPG - 1:
                            for q4 in range(4):
                                n3(pend[1] * 4 + q4)
                    pend = (h, qc, es_)
            n2_pv(*pend)
            for q4 in range(4):
                n3(pend[1] * 4 + q4)
            def n4_pv(h, qc, ki, nk_, kt, e, b_e, obk):
                for q4 in range(4):
                    ob, obb = obk[q4]
                    S.group([lambda t: t.matmul(ob[:, 0:129], e[:, q4 * 128:(q4 + 1) * 128], vs[:, kt, 0:129],
                                                start=(ki == 0), stop=(ki == nk_ - 1))], [b_e] + BKV, [obb])
                if ki == nk_ - 1:
                    for q4 in range(4):
                        qt = qc * 4 + q4
                        ob, obb = obk[q4]
                        gate_scale(qt, h, 1, ob[:, 128:129], [obb])
                        accum_out(qt, h, ob[:, 0:128], [obb], False)

            pend = None
            stepn = 0
            for h in range(HPG):
                for qc in range(NQ4):
                    qs = slice(qc * 512, (qc + 1) * 512)
                    kts = list(range(NQT)) + [NQT + i for i in range(4 * qc + 4)]
                    obk = [banks[4 + q4] for q4 in range(4)]
                    for ki, kt in enumerate(kts):
                        bk, bkb = banks[stepn % 2]
                        stepn += 1
                        fns = [lambda t: t.matmul(bk[:, :], ksT[:, kt * 128:(kt + 1) * 128], qT[:, h, qs], start=True,
                                                  stop=False)]
                        diag = kt - NQT - 4 * qc
                        last_is_sel = not (0 <= diag < 4)
                        fns.append(lambda t: t.matmul(bk[:, :], expd[:, kt, :], selbT[:, qs], start=False,
                                                      stop=last_is_sel))
                        if not last_is_sel:
                            fns.append(lambda t: t.matmul(bk[:, :], ident, caus[:, diag, :], start=False, stop=True))
                        S.group(fns, BK + BKV + [b_selT, b_ident], [bkb])
                        e, b_e = erot.next()
                        S.op("act", lambda a: a.activation(out=e, in_=bk[:, :], func=AF.Exp), [bkb], [b_e])
                        if pend is not None:
                            n4_pv(*pend)
                        pend = (h, qc, ki, len(kts), kt, e, b_e, obk)
            n4_pv(*pend)
            def n5_pv(h, qt, wi_, kt, e, b_e, ob, obb):
                S.group([lambda t: t.matmul(ob[:, 0:129], e[:, 0:128], vw[:, kt, 0:129], start=(wi_ == 0),
                                            stop=(wi_ == 4))], [b_e] + BKV, [obb])
                if wi_ == 4:
                    gate_scale(qt, h, 2, ob[:, 128:129], [obb])
                    accum_out(qt, h, ob[:, 0:128], [obb], False)

            pend = None
            stepn = 0
            for h in range(HPG):
                for qt in range(NQT):
                    A_ = NQT + qt
                    qs = slice(qt * 128, (qt + 1) * 128)
                    ob, obb = banks[4 + qt % 4]
                    for wi_ in range(5):
                        kt = A_ - 4 + wi_
                        bk, bkb = banks[stepn % 2]
                        stepn += 1
                        extra = []
                        if wi_ == 0:
                            extra.append(wlo)
                        if wi_ == 4:
                            extra.append(whi)
                        if kt < NQT:
                            extra.append(wctx)
                        fns = [lambda t: t.matmul(bk[:, 0:128], kwT[:, kt * 128:(kt + 1) * 128], qT[:, h, qs],
                                                  start=True, stop=(len(extra) == 0))]
                        for xi, xm in enumerate(extra):
                            fns.append(lambda t, xm=xm, l=(xi == len(extra) - 1): t.matmul(
                                bk[:, 0:128], ident, xm, start=False, stop=l))
                        S.group(fns, BK + BKV + [b_ident], [bkb])
                        e, b_e = erot.next()
                        S.op("act", lambda a: a.activation(out=e[:, 0:128], in_=bk[:, 0:128], func=AF.Exp),
                             [bkb], [b_e])
                        if pend is not None:
                            n5_pv(*pend)
                        pend = (h, qt, wi_, kt, e, b_e, ob, obb)
            n5_pv(*pend)
            ybb = sb([128, HPG * 128], BF16)
            b_ybb = nb("ybb")
            for qt in range(NQT):
                S.op("act", lambda a, qt=qt: a.activation(out=ybb, in_=yb[:, qt, :], func=AF.Copy),
                     [b_yb[qt]], [b_ybb])
                bk, bkb = banks[qt % 2]
                pst = bk[:].bitcast(BF16)
                for h in range(HPG):
                    S.group([lambda t, pst=pst, h=h: t.transpose(out=pst[:, h * 128:(h + 1) * 128],
                                                                 in_=ybb[:, h * 128:(h + 1) * 128], identity=ident)],
                            [b_ybb, b_ident], [bkb])
                st, stb = stg_bf.next()
                S.op("dve", lambda v, st=st, pst=pst: v.tensor_copy(out=st[:, 0:HPG * 128], in_=pst[:, 0:HPG * 128]),
                     [bkb], [stb])
                r0 = g * HPG * 128
                S.dma("sp", s_ybT[r0:r0 + HPG * 128, qt * 128:(qt + 1) * 128].rearrange("(h d) t -> d h t", d=128),
                      st[:, 0:HPG * 128].rearrange("d (h t) -> d h t", t=128), [stb], [db["ybT"]], stb)
            S.barrier()

    if "N" in phases:
        phaseN()
    S.barrier()
    AR.off = persist_mark


    def phaseC():
        make_wslots(3, 8192)
        bglu = sb([128, KS], F32)
        b_bg = nb("bglu")
        S.dma("sp", bglu, i_bglu, [], [b_bg], b_bg)
        ya = sb([128, KS, TT], BF16)
        b_ya = nb("ya")
        ya2 = sb([128, KS, TT], BF16)
        b_ya2 = nb("ya2")
        ybt = sb([128, KA, TT], BF16)
        b_ybt = nb("ybt")
        mg = sb([128, KD, TT], BF16)
        b_mg = nb("mg")
        yin = Rot([(sb([128, TT], F32), nb("yin")) for _ in range(2)])
        tmp = Rot([(sb([128, TT], F32), nb("ctmp")) for _ in range(2)])
        gat = Rot([(sb([128, TT], BF16), nb("gate")) for _ in range(3)])
        w_glu3, w_pa3, w_pb3, w_out3 = wview(w_glu), wview(w_pa), wview(w_pb), wview(w_out)
        for ti in range(NTO):
            tok0 = ti * TT
            ts_ = slice(tok0, tok0 + TT)
            S.dma("sp", ybt, s_ybT[:, ts_].rearrange("(k p) t -> p k t", p=128), [db["ybT"]], [b_ybt], b_ybt)
            for k in range(KS):
                yi, b_yi = yin.next()
                t1, b_t1 = tmp.next()
                S.dma("sp", yi, s_yaT[k * 128:(k + 1) * 128, ts_], [db["yaT"]], [b_yi], b_yi)
                S.op("dve", lambda v, yi=yi, t1=t1: v.tensor_tensor(out=t1, in0=yi, in1=yi, op=ALU.mult), [b_yi], [b_t1])
                S.op("dve", lambda v, t1=t1: v.tensor_scalar(out=t1, in0=t1, scalar1=0.044715, scalar2=1.0,
                                                              op0=ALU.mult, op1=ALU.add), [b_t1], [b_t1])
                S.op("pool", lambda v, yi=yi, t1=t1: v.tensor_tensor(out=t1, in0=t1, in1=yi, op=ALU.mult),
                     [b_yi, b_t1], [b_t1])
                S.op("act", lambda a, t1=t1: a.activation(out=t1, in_=t1, func=AF.Sigmoid, scale=1.5957691216),
                     [b_t1], [b_t1])
                S.op("pool", lambda v, yi=yi, t1=t1, k=k: v.tensor_tensor(out=ya[:, k, :], in0=yi, in1=t1, op=ALU.mult),
                     [b_yi, b_t1], [b_ya])

            def glu_post(r0, mw, bk, bkb):
                oc = r0 // 128
                t1, b_t1 = tmp.next()
                S.op("act", lambda a: a.activation(out=t1[0:mw, :], in_=bk[0:mw, 0:TT], func=AF.Sigmoid,
                                                   bias=bglu[0:mw, oc:oc + 1]), [bkb, b_bg], [b_t1])
                S.op("dve", lambda v: v.tensor_tensor(out=ya2[0:mw, oc, :], in0=ya[0:mw, oc, :], in1=t1[0:mw, :],
                                                      op=ALU.mult), [b_t1, b_ya], [b_ya2])

            wst["ck"] = ("glu", (SW + 511) // 512)
            projF(w_glu3, 0, SW, KS, ya, b_ya, None, None, post=glu_post)

            def merge_post(first, gsrc, gbuf):
                def post(r0, mw, bk, bkb):
                    oc = r0 // 128
                    gt_, b_gt = gat.next()
                    S.dma("sp", gt_[0:mw, :], gsrc[r0:r0 + mw, ts_], [gbuf], [b_gt], b_gt)
                    if first:
                        S.op("dve", lambda v: v.tensor_tensor(out=mg[0:mw, oc, :], in0=bk[0:mw, 0:TT], in1=gt_[0:mw, :],
                                                              op=ALU.mult), [bkb, b_gt], [b_mg])
                    else:
                        t1, b_t1 = tmp.next()
                        S.op("dve", lambda v: v.tensor_tensor(out=t1[0:mw, :], in0=bk[0:mw, 0:TT], in1=gt_[0:mw, :],
                                                              op=ALU.mult), [bkb, b_gt], [b_t1])
                        S.op("pool", lambda v: v.tensor_tensor(out=mg[0:mw, oc, :], in0=mg[0:mw, oc, :],
                                                               in1=t1[0:mw, :], op=ALU.add), [b_t1], [b_mg])
                return post

            wst["ck"] = ("pa", (D + 511) // 512)
            projF(w_pa3, 0, D, KS, ya2, b_ya2, None, None, post=merge_post(True, s_gaT, db["gaT"]))
            wst["ck"] = ("pb", (D + 511) // 512)
            projF(w_pb3, 0, D, KA, ybt, b_ybt, None, None, post=merge_post(False, s_gbT, db["gbT"]))
            wst["ck"] = ("out", ((D + 511) // 512) * ((KD + 15) // 16))
            projT(w_out3, 0, D, KD, mg, b_mg,
                  lambda sub, cb, cw: s_otok[tok0 + sub * 128:tok0 + (sub + 1) * 128, cb:cb + cw], db["otok"],
                  f32out=True, kchunk=16)

    if "C" in phases:
        phaseC()
    S.barrier()
    AR.off = persist_mark

    def phaseD():
        make_wslots(4, 8192)
        wst["ck"] = None
        hnT = sb([128, KD, TT], BF16)
        b_hnT = nb("hnT")
        g3T = sb([128, KD], F32)
        b_g3 = nb("g3T")
        S.dma("sp", g3T, i_g3T, [], [b_g3], b_g3)
        a0 = AR.off
        actT = sb([128, KF, TT], BF16)
        b_act = nb("actT")
        a1 = AR.off
        AR.off = a0
        ot = sb([128, D], F32)
        xt = sb([128, D], F32)
        grep = sb([128, D], F32)
        xn = sb([128, 4, D], BF16)
        assert AR.off <= a1 or True
        AR.off = max(AR.off, a1)
        b_ot, b_xt, b_gr = nb("ot"), nb("xt"), nb("grep")
        b_ot2, b_xt2 = nb("ot2"), nb("xt2")
        b_xn = [nb("xn") for _ in range(4)]
        nd = dict(xn=xn, b_xn=b_xn, hnT=hnT, b_hnT=b_hnT, gT=g3T, b_g=b_g3)
        sgr = Rot([(sb([128, TT], F32), nb("sg")) for _ in range(2)])
        w_fg3, w_fu3, w_fd3 = wview(w_fg), wview(w_fu), wview(w_fd)
        for ti in range(NTO):
            tok0 = ti * TT
            S.dma("sp", grep, i_g2rep, [], [b_gr], b_gr)
            hnf = hnT.rearrange("p k t -> p (k t)").bitcast(F32)
            alt0 = [(ot, b_ot, xt, b_xt), (hnf[:, 0:D], b_ot2, hnf[:, D:2 * D], b_xt2)]
            for sub in range(4):
                rows = slice(tok0 + sub * 128, tok0 + (sub + 1) * 128)
                o_, bo_, x_, bx_ = alt0[sub % 2]
                S.dma("sp", o_, s_otok[rows, :], [db["otok"]], [bo_], bo_)
                S.dma("sp", x_, x_own[rows, :], [], [bx_], bx_)
                rs = rstd_of(nd, o_, bo_, xn[:, sub, :], b_xn[sub])
                S.op("dve", lambda v: v.scalar_tensor_tensor(out=o_, in0=o_, scalar=rs, in1=grep, op0=ALU.mult,
                                                             op1=ALU.mult), [b_stat, b_gr], [bo_])
                S.op("dve", lambda v: v.tensor_tensor(out=x_, in0=x_, in1=o_, op=ALU.add), [bo_], [bx_])
                S.dma("sp", s_h1[rows, :], x_, [bx_], [db["h1"]], bx_)
                rs2 = rstd_of(nd, x_, bx_, xn[:, sub, :], b_xn[sub])
                S.op("dve", lambda v: v.tensor_scalar(out=xn[:, sub, :], in0=x_, scalar1=rs2, scalar2=None,
                                                      op0=ALU.mult), [bx_, b_stat], [b_xn[sub]])
            S.barrier()
            transposes_to_hnT(nd)
            S.barrier()
            for cb in range(0, c.DFF, 256):
                cw = min(256, c.DFF - cb)
                wg, wgb = load_w(w_fg3, 0, KD, cb, cw)
                wu, wub = load_w(w_fu3, 0, KD, cb, cw)
                for m0 in range(0, cw, 128):
                    fc = (cb + m0) // 128
                    bg, bgb = mm_rot.next()
                    bu, bub = mm_rot.next()
                    S.group([lambda t, k=k, m0=m0, bg=bg, wg=wg: t.matmul(bg[:, 0:TT], wg[:, k, m0:m0 + 128], hnT[:, k, :],
                                                                          start=(k == 0), stop=(k == KD - 1))
                             for k in range(KD)], [wgb, b_hnT], [bgb])
                    S.group([lambda t, k=k, m0=m0, bu=bu, wu=wu: t.matmul(bu[:, 0:TT], wu[:, k, m0:m0 + 128], hnT[:, k, :],
                                                                          start=(k == 0), stop=(k == KD - 1))
                             for k in range(KD)], [wub, b_hnT], [bub])
                    sg, b_sg = sgr.next()
                    S.op("act", lambda a, sg=sg, bg=bg: a.activation(out=sg, in_=bg[:, 0:TT], func=AF.Silu),
                         [bgb], [b_sg])
                    S.op("dve", lambda v, sg=sg, bu=bu, fc=fc: v.tensor_tensor(out=actT[:, fc, :], in0=sg,
                                                                               in1=bu[:, 0:TT], op=ALU.mult),
                         [b_sg, bub], [b_act])
            wst["ck"] = None
            projT(w_fd3, 0, D, KF, actT, b_act,
                  lambda sub, cb, cw: s_ftok[tok0 + sub * 128:tok0 + (sub + 1) * 128, cb:cb + cw], db["ftok"],
                  f32out=True, kchunk=16)
            S.barrier()
            S.dma("sp", grep, i_g4rep, [], [b_gr], b_gr)
            xnf = xn.rearrange("p a d -> p (a d)").bitcast(F32)
            alt = [(ot, b_ot, xt, b_xt), (xnf[:, 0:D], b_ot2, xnf[:, D:2 * D], b_xt2)]
            junk3 = hnT.rearrange("p k t -> p (k t)")[:, 0:D]
            for sub in range(4):
                rows = slice(tok0 + sub * 128, tok0 + (sub + 1) * 128)
                o_, bo_, x_, bx_ = alt[sub % 2]
                S.dma("sp", o_, s_ftok[rows, :], [db["ftok"]], [bo_], bo_)
                S.dma("sp", x_, s_h1[rows, :], [db["h1"]], [bx_], bx_)
                rs = rstd_of(nd, o_, bo_, junk3, b_hnT)
                S.op("dve", lambda v: v.scalar_tensor_tensor(out=o_, in0=o_, scalar=rs, in1=grep, op0=ALU.mult,
                                                             op1=ALU.mult), [b_stat, b_gr], [bo_])
                S.op("dve", lambda v: v.tensor_tensor(out=x_, in0=x_, in1=o_, op=ALU.add), [bo_], [bx_])
                S.dma("sp", y_out[rows, :], x_, [bx_], [], bx_)
            S.barrier()

    if "D" in phases:
        phaseD()
    S.emit()
    return nc, es


def _masks(c, s):
    SH, NB, NC_, NCT, NKT = c.SH, c.NB, c.NC, c.NCT, c.NKT
    SHb, SHc = SH // 64, SH // 16
    i = np.arange(SH)
    tg = s * SH + i
    n = np.arange(NCT * 128)
    if s == 1:
        ng = n.copy()
        nvalid = n < NC_
    else:
        ng = n - SHc
        nvalid = (n >= SHc) & (n < NC_)
    valid = nvalid[:, None] & ((16 * ng[:, None] + 31) <= tg[None, :])
    cmpbias = np.where(valid, 0.0, NEG).astype(np.float32)
    j = np.arange(NB)
    if s == 1:
        jg = j.copy()
        jvalid = np.ones(NB, bool)
    else:
        jg = j - SHb
        jvalid = j >= SHb
    ov = (16 * ng[:, None] < 64 * (jg[None, :] + 1)) & (16 * ng[:, None] + 32 > 64 * jg[None, :])
    ovl = (ov & nvalid[:, None] & jvalid[None, :]).astype(np.float32)
    cur = tg // 64
    allowed = jvalid[None, :] & (jg[None, :] * 64 <= tg[:, None])
    forced = allowed & ((jg[None, :] == 0) | (jg[None, :] == cur[:, None]) | (jg[None, :] == cur[:, None] - 1))
    selA = (allowed & ~forced).astype(np.float32)
    selB = np.where(forced, 1e9, np.where(allowed, 0.0, -1e30)).astype(np.float32)
    selM = allowed.astype(np.float32)
    m = np.arange(NKT * 128)
    expand = (j[:, None] == (m[None, :] // 64)).astype(np.float32)
    sl = np.arange(128)
    tl = np.arange(512)
    caus = np.concatenate([np.where((128 * v + sl[:, None]) <= tl[None, :], 0.0, NEG) for v in range(4)], 0)
    t1 = np.arange(128)
    wlo = np.where(sl[:, None] > t1[None, :], 0.0, NEG)
    whi = np.where(sl[:, None] <= t1[None, :], 0.0, NEG)
    wctx = np.full((128, 128), 0.0 if s == 1 else NEG)
    f = lambda a: np.ascontiguousarray(a, dtype=np.float32)
    return dict(cmpbias=f(cmpbias), ovl=f(ovl), selA=f(selA), selB=f(selB), selM=f(selM), expand=f(expand),
                caus=f(caus), wlo=f(wlo), whi=f(whi), wctx=f(wctx), ident=f(np.eye(128)))


def _shared(c, inp):
    f = lambda a: np.ascontiguousarray(a, dtype=np.float32)
    KD, NG, KS, D = c.KD, c.NG, c.KS, c.D
    m = {}
    m["w_in"] = f(inp["w_in"][0])
    m["w_glu"] = f(inp["ssm_w_glu"][0])
    m["w_pa"] = f(inp["w_proj_a"][0])
    m["w_pb"] = f(inp["w_proj_b"][0])
    m["w_out"] = f(inp["w_out"][0])
    m["w_fg"] = f(inp["w_ffn_gate"][0])
    m["w_fu"] = f(inp["w_ffn_up"][0])
    m["w_fd"] = f(inp["w_ffn_down"][0])
    m["g1T"] = f(np.asarray(inp["norm_mix_pre"][0]).reshape(KD, 128).T)
    m["g3T"] = f(np.asarray(inp["norm_ffn_pre"][0]).reshape(KD, 128).T)
    m["g2rep"] = f(np.broadcast_to(np.asarray(inp["norm_mix_post"][0])[None, :], (128, D)))
    m["g4rep"] = f(np.broadcast_to(np.asarray(inp["norm_ffn_post"][0])[None, :], (128, D)))
    are, aim, ldt = np.asarray(inp["ssm_a_re"][0]), np.asarray(inp["ssm_a_im"][0]), np.asarray(inp["ssm_log_dt"][0])
    m["are_pg"] = f(np.concatenate([are.T, are.T], 0))
    m["aim_pg"] = f(np.concatenate([aim.T, aim.T], 0))
    m["ldt_pg"] = f(np.broadcast_to(ldt[None, :], (128, NG)))
    m["are_gp"] = f(are)
    m["aim_gp"] = f(aim)
    m["ldt_gp"] = f(np.broadcast_to(ldt[:, None], (NG, 64)))
    m["bre_cg"] = f(np.asarray(inp["ssm_b_re"][0]).transpose(2, 0, 1).reshape(16, NG * 64))
    m["bim_cg"] = f(np.asarray(inp["ssm_b_im"][0]).transpose(2, 0, 1).reshape(16, NG * 64))
    creT = np.asarray(inp["ssm_c_re"][0]).transpose(2, 0, 1).reshape(64, NG * 16)
    cimT = np.asarray(inp["ssm_c_im"][0]).transpose(2, 0, 1).reshape(64, NG * 16)
    m["cc1"] = f(np.concatenate([creT, cimT], 0))
    m["cc2"] = f(np.concatenate([cimT, creT], 0))
    m["dskip"] = f(np.asarray(inp["ssm_d"][0]).reshape(NG, 16).T)
    m["bgluT"] = f(np.asarray(inp["ssm_b_glu"][0]).reshape(KS, 128).T)
    m["w1k"] = f(inp["cmp_w1_k"][0])
    m["w1v"] = f(inp["cmp_w1_v"][0])
    m["w2k"] = f(inp["cmp_w2_k"][0])
    m["w2v"] = f(inp["cmp_w2_v"][0])
    m["pekT"] = f(np.asarray(inp["cmp_pe_k"][0]).T)
    m["pevT"] = f(np.asarray(inp["cmp_pe_v"][0]).T)
    return m


def make_in_maps(c, inp):
    shared = _shared(c, inp)
    masks = [_masks(c, 0), _masks(c, 1)]
    x = np.asarray(inp["x"], dtype=np.float32)
    maps = []
    for b in range(c.B):
        for s in range(2):
            m = dict(shared)
            m.update(masks[s])
            m["x_own"] = np.ascontiguousarray(x[b, s * c.SH:(s + 1) * c.SH])
            m["x_ctx"] = np.ascontiguousarray(x[b, (1 - s) * c.SH:(2 - s) * c.SH])
            m["flag"] = np.full((128, 1), float(s), np.float32)
            maps.append(m)
    return maps


_CACHE = {}


def kernel(**inputs):
    c = Cfg()
    if "nc" not in _CACHE:
        _CACHE["nc"] = build(c)
    nc, es = _CACHE["nc"]
    maps = make_in_maps(c, inputs)
    res = run_bass_kernel_spmd(nc, maps, core_ids=list(range(2 * c.B)))
    out = np.empty((c.B, c.S, c.D), np.float32)
    for b in range(c.B):
        for s in range(2):
            out[b, s * c.SH:(s + 1) * c.SH] = res.results[b * 2 + s]["y"]
    return out
```

```python
import math
import types
from contextlib import ExitStack

import numpy as np
import concourse.bass as bass
import concourse.mybir as mybir
from concourse.bass_utils import run_bass_kernel_spmd

F32 = mybir.dt.float32
BF16 = mybir.dt.bfloat16
AF = mybir.ActivationFunctionType
ALU = mybir.AluOpType
NEG = -30000.0
EPS = 1e-6


class Cfg:
    def __init__(self, B=4, S=4096, D=4096, SW=2048, NH=16, NKV=4, DFF=11008):
        self.B, self.S, self.D, self.SW, self.NH, self.NKV, self.DFF = B, S, D, SW, NH, NKV, DFF
        self.SH = S // 2
        self.KD = D // 128
        self.NG = SW // 16
        self.KS = SW // 128
        self.HPG = NH // NKV
        self.AW = NH * 128
        self.KA = self.AW // 128
        self.KVW = NKV * 128
        self.KF = DFF // 128
        self.NB = S // 64
        self.NC = (S - 32) // 16 + 1
        self.NCT = (self.NC + 127) // 128
        self.NKT = S // 128
        self.NQT = self.SH // 128
        self.NQC = self.SH // 512
        self.INW = SW + self.AW + 6 * self.KVW + 3 * NH + 2 * D
        self.NK = int(math.log2(S))
        assert (1 << self.NK) == S


def _freeze(fn):
    if fn.__closure__ is None:
        return fn
    cells = []
    for cl in fn.__closure__:
        try:
            cells.append(types.CellType(cl.cell_contents))
        except ValueError:
            cells.append(cl)
    return types.FunctionType(fn.__code__, fn.__globals__, fn.__name__, fn.__defaults__, tuple(cells))


class Tok:
    __slots__ = ("sem", "val")

    def __init__(self, sem, val):
        self.sem, self.val = sem, val


class Buf:
    def __init__(self, name):
        self.name = name
        self.w = None
        self.r = {}
        self.dsem = None
        self.dcnt = 0


class Sch:
    ENG = ("pe", "act", "dve", "pool", "sp")

    def __init__(self, nc, es):
        self.nc, self.es = nc, es
        self.q = {e: [] for e in self.ENG}
        self.sem = {e: es.enter_context(nc.semaphore("c_" + e)) for e in ("pe", "act", "dve", "pool")}
        self.cnt = {e: 0 for e in self.ENG}
        self.seen = {e: {} for e in self.ENG}
        self.nsem = 4
        self.dma_toks = []

    def _wait(self, e, tok):
        if tok is None:
            return
        if e == "pe" and tok.sem is self.sem["pe"]:
            return
        k = id(tok.sem)
        if self.seen[e].get(k, 0) >= tok.val:
            return
        self.seen[e][k] = tok.val
        self.q[e].append(("w", tok.sem, tok.val))

    def _deps(self, e, reads, writes):
        for b in reads:
            self._wait(e, b.w)
        for b in writes:
            self._wait(e, b.w)
            for t in list(b.r.values()):
                self._wait(e, t)

    def _mark(self, tok, reads, writes):
        for b in reads:
            b.r[id(tok.sem)] = tok
        for b in writes:
            b.w = tok
            b.r = {}

    def op(self, e, fn, reads=(), writes=()):
        self._deps(e, reads, writes)
        self.cnt[e] += 1
        tok = Tok(self.sem[e], self.cnt[e])
        self.q[e].append(("o", _freeze(fn)))
        self._mark(tok, reads, writes)
        return tok

    def group(self, fns, reads=(), writes=()):
        self._deps("pe", reads, writes)
        for f in fns[:-1]:
            self.q["pe"].append(("n", _freeze(f)))
        self.cnt["pe"] += 1
        tok = Tok(self.sem["pe"], self.cnt["pe"])
        self.q["pe"].append(("o", _freeze(fns[-1])))
        self._mark(tok, reads, writes)
        return tok

    def dma(self, e, out, in_, reads, writes, slot, **kw):
        self._deps(e, reads, writes)
        if slot.dsem is None:
            slot.dsem = self.es.enter_context(self.nc.semaphore("d_" + slot.name))
            self.nsem += 1
        slot.dcnt += 16
        tok = Tok(slot.dsem, slot.dcnt)
        self.q[e].append(("d", out, in_, slot.dsem, kw))
        self._mark(tok, reads, writes)
        self.dma_toks.append(tok)
        return tok

    def barrier(self):
        last = {}
        for t in self.dma_toks:
            last[id(t.sem)] = t
        toks = list(last.values()) + [Tok(self.sem[x], self.cnt[x]) for x in ("pe", "act", "dve", "pool") if self.cnt[x]]
        for e in self.ENG:
            for t in toks:
                if e in self.sem and t.sem is self.sem[e]:
                    if e != "pe":
                        self._wait(e, t)
                    continue
                self._wait(e, t)
        self.dma_toks = list(last.values())

    def emit(self):
        nc = self.nc
        last = {}
        for t in self.dma_toks:
            last[id(t.sem)] = t
        for t in last.values():
            self._wait("sp", t)

        def replay(eng, items, mysem):
            for it in items:
                if it[0] == "w":
                    eng.wait_ge(it[1], it[2])
                elif it[0] == "o":
                    it[1](eng).then_inc(mysem, 1)
                elif it[0] == "n":
                    it[1](eng)
                else:
                    eng.dma_start(out=it[1], in_=it[2], **it[4]).then_inc(it[3], 16)

        with nc.Block() as block:
            @block.tensor
            def _(t):
                replay(t, self.q["pe"], self.sem["pe"])

            @block.scalar
            def _(a):
                replay(a, self.q["act"], self.sem["act"])

            @block.vector
            def _(v):
                replay(v, self.q["dve"], self.sem["dve"])

            @block.gpsimd
            def _(g):
                replay(g, self.q["pool"], self.sem["pool"])

            @block.sync
            def _(s):
                replay(s, self.q["sp"], None)


class Rot:
    def __init__(self, items):
        self.items = items
        self.i = 0

    def next(self):
        it = self.items[self.i % len(self.items)]
        self.i += 1
        return it


class Arena:
    def __init__(self, t, nelem):
        self.t, self.n, self.off = t, nelem, 0

    def alloc(self, shape, dt):
        p = shape[0]
        n = int(np.prod(shape[1:]))
        ne = n * (2 if dt == F32 else 1)
        self.off = (self.off + 1) // 2 * 2
        assert self.off + ne <= self.n, ("SBUF arena overflow", self.off, ne, self.n)
        ap = self.t[0:p, self.off:self.off + ne]
        self.off += ne
        if dt == F32:
            ap = ap.bitcast(F32)
        if len(shape) == 3:
            ap = ap.rearrange("p (a b) -> p a b", b=shape[2])
        elif len(shape) == 4:
            ap = ap.rearrange("p (a b c) -> p a b c", b=shape[2], c=shape[3])
        return ap

def build(cfg, debug_outs=(), phases="ABNCD"):
    c = cfg
    nc = bass.Bass("TRN2", target_bir_lowering=False)
    es = ExitStack()
    S = Sch(nc, es)
    D, SH, KD, SW, NG, KS, AW, KA, KVW, KF, NB, NC_, NCT, NKT, NQT, NQC, NH, NKV, HPG = (
        c.D, c.SH, c.KD, c.SW, c.NG, c.KS, c.AW, c.KA, c.KVW, c.KF, c.NB, c.NC, c.NCT, c.NKT, c.NQT,
        c.NQC, c.NH, c.NKV, c.HPG)
    SEQ = c.S
    TT = 512
    NTO = SH // TT

    def din(name, shape, dt=F32):
        return nc.dram_tensor(name, list(shape), dt, kind="ExternalInput").ap()

    def dscr(name, shape, dt):
        kind = "ExternalOutput" if name in debug_outs else "Internal"
        return nc.dram_tensor(name, list(shape), dt, kind=kind).ap()

    x_own = din("x_own", [SH, D])
    x_ctx = din("x_ctx", [SH, D])
    w_in = din("w_in", [D, c.INW])
    w_glu = din("w_glu", [SW, SW])
    w_pa = din("w_pa", [SW, D])
    w_pb = din("w_pb", [AW, D])
    w_out = din("w_out", [D, D])
    w_fg = din("w_fg", [D, c.DFF])
    w_fu = din("w_fu", [D, c.DFF])
    w_fd = din("w_fd", [c.DFF, D])
    i_g1T = din("g1T", [128, KD])
    i_g3T = din("g3T", [128, KD])
    i_g2rep = din("g2rep", [128, D])
    i_g4rep = din("g4rep", [128, D])
    i_are_pg = din("are_pg", [128, NG])
    i_aim_pg = din("aim_pg", [128, NG])
    i_ldt_pg = din("ldt_pg", [128, NG])
    i_are_gp = din("are_gp", [NG, 64])
    i_aim_gp = din("aim_gp", [NG, 64])
    i_ldt_gp = din("ldt_gp", [NG, 64])
    i_bre = din("bre_cg", [16, NG * 64])
    i_bim = din("bim_cg", [16, NG * 64])
    i_cc1 = din("cc1", [128, NG * 16])
    i_cc2 = din("cc2", [128, NG * 16])
    i_dsk = din("dskip", [16, NG])
    i_bglu = din("bgluT", [128, KS])
    i_flag = din("flag", [128, 1])
    i_w1k = din("w1k", [32 * 128, 128])
    i_w1v = din("w1v", [32 * 128, 128])
    i_w2k = din("w2k", [128, 128])
    i_w2v = din("w2v", [128, 128])
    i_pek = din("pekT", [128, 32])
    i_pev = din("pevT", [128, 32])
    i_cmpb = din("cmpbias", [NCT * 128, SH])
    i_ovl = din("ovl", [NCT * 128, NB])
    i_selA = din("selA", [SH, NB])
    i_selB = din("selB", [SH, NB])
    i_selM = din("selM", [SH, NB])
    i_exp = din("expand", [NB, NKT * 128])
    i_caus = din("caus", [4 * 128, 512])
    i_wlo = din("wlo", [128, 128])
    i_whi = din("whi", [128, 128])
    i_wctx = din("wctx", [128, 128])
    i_ident = din("ident", [128, 128])
    y_out = nc.dram_tensor("y", [SH, D], F32, kind="ExternalOutput").ap()

    s_uT = dscr("s_uT", [SW, SEQ], BF16)
    s_kT = dscr("s_kT", [3, KVW, SEQ], BF16)
    s_vcT = dscr("s_vcT", [KVW, SEQ], BF16)
    s_vt = dscr("s_vt", [2, SEQ, KVW], BF16)
    s_qT = dscr("s_qT", [AW, SH], BF16)
    s_gn = dscr("s_gn", [SH, 3 * NH], F32)
    s_gaT = dscr("s_gaT", [D, SH], BF16)
    s_gbT = dscr("s_gbT", [D, SH], BF16)
    s_bbar = dscr("s_bbar", [NG, SEQ // TT, 16, 256], BF16)
    s_zs = dscr("s_zs", [SEQ // TT, 2, NG * 64], F32)
    s_tab = dscr("s_tab", [128, NG, 64 + 2 * (TT // 32)], F32)
    s_yaT = dscr("s_yaT", [SW, SH], F32)
    s_ybT = dscr("s_ybT", [AW, SH], BF16)
    s_otok = dscr("s_otok", [SH, D], F32)
    s_h1 = dscr("s_h1", [SH, D], F32)
    s_ftok = dscr("s_ftok", [SH, D], F32)
    db = {n: Buf(n) for n in ("uT", "kT", "vcT", "vt", "qT", "gn", "gaT", "gbT", "bbar", "zs", "yaT", "ybT",
                              "otok", "h1", "ftok", "tab")}

    ARENA_N = 104000
    arena_t = es.enter_context(nc.sbuf_tensor("arena", [128, ARENA_N], BF16))
    AR = Arena(arena_t, ARENA_N)

    def sb(shape, dt):
        return AR.alloc(list(shape), dt)

    bufn = [0]

    def nb(prefix="b"):
        bufn[0] += 1
        return Buf("%s%d" % (prefix, bufn[0]))

    banks = []
    for i in range(8):
        t = es.enter_context(nc.psum_tensor("bank%d" % i, [128, 512], F32))
        banks.append((t, Buf("bank%d" % i)))

    ident = sb([128, 128], BF16)
    b_ident = nb("ident")
    S.dma("pool", ident, i_ident, [], [b_ident], b_ident)
    stg_bf = Rot([(sb([128, 512], BF16), nb("stgb")) for i in range(4)])
    stg_f = Rot([(sb([128, 512], F32), nb("stgf")) for i in range(3)])
    wst = {"rot": None, "sz": 0}

    def make_wslots(n, size):
        wst["rot"] = Rot([(sb([128, size], BF16), nb("wslot")) for i in range(n)])
        wst["sz"] = size
    stat = sb([128, 8], F32)
    b_stat = nb("stat")
    persist_mark = AR.off
    evac_i = [0]

    def evac_eng():
        evac_i[0] += 1
        return "act" if evac_i[0] % 2 else "dve"

    def evac(eng, out, in_, reads, writes, func=None, scale=1.0):
        if func is not None or eng == "act":
            f = func if func is not None else AF.Copy
            return S.op("act", lambda a: a.activation(out=out, in_=in_, func=f, scale=float(scale)), reads, writes)
        if scale != 1.0:
            return S.op("dve", lambda v: v.tensor_scalar(out=out, in0=in_, scalar1=float(scale), scalar2=None,
                                                         op0=ALU.mult), reads, writes)
        return S.op("dve", lambda v: v.tensor_copy(out=out, in_=in_), reads, writes)

    def wview(w_ap):
        return w_ap.rearrange("(kc p) n -> p kc n", p=128)

    wcache = {}

    def load_w(w3, kc0, nk, c0, cw, ck=None, ntiles=0):
        if ck is None and wst.get("ck"):
            ck, ntiles = wst["ck"]
        wt, wb = wst["rot"].next()
        sz = wst["sz"]
        assert nk * cw <= sz, (nk, cw, sz)
        flat = wt[:, 0:nk * cw]
        dst = flat.rearrange("p (k n) -> p k n", n=cw)
        if ck is None:
            S.dma("pool", dst, w3[:, kc0:kc0 + nk, c0:c0 + cw], [], [wb], wb)
            return dst, wb
        if ck not in wcache:
            wcache[ck] = dict(ap=nc.dram_tensor("wc_" + ck, [ntiles, 128, sz], BF16, kind="Internal").ap(),
                              buf=Buf("wc_" + ck), idx={})
        ent = wcache[ck]
        key = (kc0, nk, c0, cw)
        if key not in ent["idx"]:
            i = len(ent["idx"])
            assert i < ntiles, (ck, i, ntiles)
            ent["idx"][key] = i
            S.dma("pool", dst, w3[:, kc0:kc0 + nk, c0:c0 + cw], [], [wb], wb)
            flush_spill()
            wst["pend"] = (ent["ap"][i, :, 0:nk * cw], flat, wb, ent["buf"])
        else:
            i = ent["idx"][key]
            S.dma("pool", flat, ent["ap"][i, :, 0:nk * cw], [ent["buf"]], [wb], wb)
            flush_spill()
        return dst, wb

    def flush_spill():
        p = wst.get("pend")
        if p is not None:
            wst["pend"] = None
            S.dma("pool", p[0], p[1], [p[2]], [p[3]], p[2])

    _orig_barrier = S.barrier

    def _barrier():
        flush_spill()
        _orig_barrier()

    S.barrier = _barrier

    mm_rot = Rot(banks[0:4])
    mm8_rot = Rot(banks[0:8])

    def projF(w3, c0, width, nk, act_tile, b_act, dst_fn, dst_buf, func=None, scale=1.0, post=None, ck=None, nt=0):
        for cb in range(0, width, 512):
            cw = min(512, width - cb)
            wt, wb = load_w(w3, 0, nk, c0 + cb, cw, ck, nt)
            for m0 in range(0, cw, 128):
                mw = min(128, cw - m0)
                bk, bkb = mm_rot.next()
                fns = [lambda t, k=k, m0=m0, mw=mw, bk=bk, wt=wt: t.matmul(
                    bk[0:mw, 0:TT], wt[:, k, m0:m0 + mw], act_tile[:, k, :], start=(k == 0), stop=(k == nk - 1))
                    for k in range(nk)]
                S.group(fns, [wb, b_act], [bkb])
                if post is not None:
                    post(cb + m0, mw, bk, bkb)
                    continue
                st, stb = stg_bf.next()
                evac(evac_eng(), st[0:mw, 0:TT], bk[0:mw, 0:TT], [bkb], [stb], func=func, scale=scale)
                S.dma("sp", dst_fn(cb + m0, mw), st[0:mw, 0:TT], [stb], [dst_buf], stb)

    def projT(w3, c0, width, nk, act_tile, b_act, dst_fn, dst_buf, func=None, f32out=False, kchunk=None, ck=None, nt=0,
              wide=False):
        kch = kchunk or nk
        for cb in range(0, width, 512):
            cw = min(512, width - cb)
            bks = [(mm8_rot if wide else mm_rot).next() for _ in range(4)]
            for k0 in range(0, nk, kch):
                kn = min(kch, nk - k0)
                wt, wb = load_w(w3, k0, kn, c0 + cb, cw, ck, nt)
                for sub in range(4):
                    bk, bkb = bks[sub]
                    fns = [lambda t, k=k, k0=k0, sub=sub, bk=bk, wt=wt, cw=cw: t.matmul(
                        bk[:, 0:cw], act_tile[:, k0 + k, sub * 128:(sub + 1) * 128], wt[:, k, 0:cw],
                        start=(k0 + k == 0), stop=(k0 + k == nk - 1)) for k in range(kn)]
                    S.group(fns, [wb, b_act], [bkb])
            for sub in range(4):
                bk, bkb = bks[sub]
                st, stb = (stg_f if f32out else stg_bf).next()
                evac(evac_eng(), st[:, 0:cw], bk[:, 0:cw], [bkb], [stb], func=func)
                S.dma("sp", dst_fn(sub, cb, cw), st[:, 0:cw], [stb], [dst_buf], stb)

    def alloc_norm():
        d = {}
        d["xrot"] = Rot([(sb([128, D], F32), nb("xin")) for i in range(2)])
        d["xn"] = sb([128, 4, D], BF16)
        d["b_xn"] = [nb("xn") for i in range(4)]
        d["hnT"] = sb([128, KD, TT], BF16)
        d["b_hnT"] = nb("hnT")
        d["gT"] = sb([128, KD], F32)
        d["b_g"] = nb("gT")
        return d

    def rstd_of(nd, src_ap, bsrc, junk, b_junk, col=0):
        S.op("act", lambda a: a.activation(out=junk, in_=src_ap, func=AF.Square,
                                           accum_out=stat[:, col:col + 1]), [bsrc], [b_junk, b_stat])
        S.op("dve", lambda v: v.tensor_scalar(out=stat[:, col + 1:col + 2], in0=stat[:, col:col + 1],
                                              scalar1=1.0 / D, scalar2=EPS, op0=ALU.mult, op1=ALU.add),
             [b_stat], [b_stat])
        S.op("act", lambda a: a.activation(out=stat[:, col + 1:col + 2], in_=stat[:, col + 1:col + 2], func=AF.Sqrt),
             [b_stat], [b_stat])
        S.op("dve", lambda v: v.reciprocal(out=stat[:, col + 2:col + 3], in_=stat[:, col + 1:col + 2]),
             [b_stat], [b_stat])
        return stat[:, col + 2:col + 3]

    def transposes_to_hnT(nd):
        xn, hnT, gT = nd["xn"], nd["hnT"], nd["gT"]
        for kc in range(KD):
            bk, bkb = banks[6 + kc % 2]
            pst = bk[:].bitcast(BF16)
            for sub in range(4):
                S.group([lambda t, sub=sub, kc=kc, pst=pst: t.transpose(
                    out=pst[:, sub * 128:(sub + 1) * 128], in_=xn[:, sub, kc * 128:(kc + 1) * 128], identity=ident)],
                    nd["b_xn"] + [b_ident], [bkb])
            S.op("dve", lambda v, kc=kc, pst=pst: v.tensor_scalar(
                out=hnT[:, kc, :], in0=pst[:, 0:TT], scalar1=gT[:, kc:kc + 1], scalar2=None, op0=ALU.mult),
                 [bkb, nd["b_g"]], [nd["b_hnT"]])

    def phaseA():
        make_wslots(3, 16384 if KD * 512 <= 16384 else KD * 512)
        wst["ck"] = None
        nd = alloc_norm()
        S.dma("sp", nd["gT"], i_g1T, [], [nd["b_g"]], nd["b_g"])
        hnT, b_hnT = nd["hnT"], nd["b_hnT"]
        w_in3 = wview(w_in)
        HS = 128 ** -0.5
        o = 0
        segs = {}
        for nm, wd in (("u", SW), ("q", AW), ("kc", KVW), ("vc", KVW), ("ks", KVW), ("vs", KVW), ("kw", KVW),
                       ("vw", KVW), ("gn", 3 * NH), ("ga", D), ("gb", D)):
            segs[nm] = (o, wd)
            o += wd

        def tile(xsrc, ti, own):
            tok0 = ti * TT
            apos = (SH if own else 0) + tok0
            for sub in range(4):
                xt, xb = nd["xrot"].next()
                S.dma("sp", xt, xsrc[tok0 + sub * 128:tok0 + (sub + 1) * 128, :], [], [xb], xb)
                rs = rstd_of(nd, xt, xb, nd["xn"][:, sub, :], nd["b_xn"][sub])
                S.op("dve", lambda v, xt=xt, sub=sub, rs=rs: v.tensor_scalar(
                    out=nd["xn"][:, sub, :], in0=xt, scalar1=rs, scalar2=None, op0=ALU.mult),
                     [xb, b_stat], [nd["b_xn"][sub]])
            transposes_to_hnT(nd)
            win = ["kw", "vw"] if (own or ti == NTO - 1) else []
            names = ["u", "kc", "vc", "ks", "vs"] + win + (["q", "gn", "ga", "gb"] if own else [])
            for nm in names:
                c0, wd = segs[nm]
                if nm == "u":
                    projF(w_in3, c0, wd, KD, hnT, b_hnT, lambda r, n: s_uT[r:r + n, apos:apos + TT], db["uT"])
                elif nm in ("kc", "ks", "kw"):
                    ki = ("kc", "ks", "kw").index(nm)
                    projF(w_in3, c0, wd, KD, hnT, b_hnT, lambda r, n, ki=ki: s_kT[ki, r:r + n, apos:apos + TT],
                          db["kT"])
                elif nm == "vc":
                    projF(w_in3, c0, wd, KD, hnT, b_hnT, lambda r, n: s_vcT[r:r + n, apos:apos + TT], db["vcT"])
                elif nm in ("vs", "vw"):
                    vi = ("vs", "vw").index(nm)
                    projT(w_in3, c0, wd, KD, hnT, b_hnT,
                          lambda sub, cb, cw, vi=vi: s_vt[vi, apos + sub * 128:apos + (sub + 1) * 128, cb:cb + cw],
                          db["vt"])
                elif nm == "q":
                    projF(w_in3, c0, wd, KD, hnT, b_hnT, lambda r, n: s_qT[r:r + n, tok0:tok0 + TT], db["qT"],
                          scale=HS)
                elif nm == "gn":
                    projT(w_in3, c0, wd, KD, hnT, b_hnT,
                          lambda sub, cb, cw: s_gn[tok0 + sub * 128:tok0 + (sub + 1) * 128, cb:cb + cw], db["gn"],
                          func=AF.Sigmoid, f32out=True)
                elif nm == "ga":
                    projF(w_in3, c0, wd, KD, hnT, b_hnT, lambda r, n: s_gaT[r:r + n, tok0:tok0 + TT], db["gaT"],
                          func=AF.Sigmoid)
                elif nm == "gb":
                    projF(w_in3, c0, wd, KD, hnT, b_hnT, lambda r, n: s_gbT[r:r + n, tok0:tok0 + TT], db["gbT"],
                          func=AF.Sigmoid)

        for ti in range(NTO):
            tile(x_ctx, ti, False)
        for ti in range(NTO):
            tile(x_own, ti, True)

    if "A" in phases:
        phaseA()
    S.barrier()
    AR.off = persist_mark

    def phaseB():
        NK = c.NK

        def tt(eng, out, a, b, op, reads, writes):
            return S.op(eng, lambda v: v.tensor_tensor(out=out, in0=a, in1=b, op=op), reads, writes)

        def ts(eng, out, a, s1, s2, op0, op1, reads, writes):
            if op1 is None:
                return S.op(eng, lambda v: v.tensor_scalar(out=out, in0=a, scalar1=s1, scalar2=None, op0=op0),
                            reads, writes)
            return S.op(eng, lambda v: v.tensor_scalar(out=out, in0=a, scalar1=s1, scalar2=s2, op0=op0, op1=op1),
                        reads, writes)

        def consts(P, Fd, i_ar, i_ai, i_ld, npow):
            bb = nb("s5c")
            B = [bb]
            T = lambda: sb([P, Fd], F32)
            ar, ai, ld = T(), T(), T()
            S.dma("sp", ar, i_ar, [], B, bb)
            S.dma("sp", ai, i_ai, [], B, bb)
            S.dma("sp", ld, i_ld, [], B, bb)
            dt_, lam, th, dec, x2, ps_, pc_, s_, c_, t1, t2 = (T() for _ in range(11))
            S.op("act", lambda a: a.activation(out=dt_, in_=ld, func=AF.Exp), B, B)
            tt("dve", lam, dt_, ar, ALU.mult, B, B)
            tt("dve", th, dt_, ai, ALU.mult, B, B)
            S.op("act", lambda a: a.activation(out=dec, in_=lam, func=AF.Exp), B, B)
            ts("dve", th, th, 1.0 / 64, None, ALU.mult, None, B, B)
            tt("dve", x2, th, th, ALU.mult, B, B)

            def horner(out, coeffs):
                ts("dve", out, x2, coeffs[0], coeffs[1], ALU.mult, ALU.add, B, B)
                for cf in coeffs[2:]:
                    tt("dve", out, out, x2, ALU.mult, B, B)
                    ts("dve", out, out, cf, None, ALU.add, None, B, B)

            horner(ps_, [1.0 / 362880, -1.0 / 5040, 1.0 / 120, -1.0 / 6, 1.0])
            tt("dve", s_, ps_, th, ALU.mult, B, B)
            horner(c_, [1.0 / 40320, -1.0 / 720, 1.0 / 24, -0.5, 1.0])

            def dbl(co, so, ci, si):
                tt("dve", t1, si, si, ALU.mult, B, B)
                tt("dve", t2, ci, ci, ALU.mult, B, B)
                S.op("dve", lambda v: v.scalar_tensor_tensor(out=so, in0=si, scalar=2.0, in1=ci, op0=ALU.mult,
                                                             op1=ALU.mult), B, B)
                tt("dve", co, t2, t1, ALU.subtract, B, B)

            def renorm(cc, ss):
                tt("dve", t1, ss, ss, ALU.mult, B, B)
                tt("dve", t2, cc, cc, ALU.mult, B, B)
                tt("dve", t1, t1, t2, ALU.add, B, B)
                ts("dve", t1, t1, -0.5, 1.5, ALU.mult, ALU.add, B, B)
                tt("dve", cc, cc, t1, ALU.mult, B, B)
                tt("dve", ss, ss, t1, ALU.mult, B, B)

            c2, s2 = T(), T()
            cur = (c_, s_)
            oth = (c2, s2)
            for i in range(6):
                dbl(oth[0], oth[1], cur[0], cur[1])
                cur, oth = oth, cur
            renorm(cur[0], cur[1])
            wr, wi = [cur[0]], [cur[1]]
            for k in range(1, npow):
                a, b = T(), T()
                dbl(a, b, wr[-1], wi[-1])
                renorm(a, b)
                wr.append(a)
                wi.append(b)
            abr, abi, den, m, zr, zi = (T() for _ in range(6))
            tt("dve", abr, dec, wr[0], ALU.mult, B, B)
            tt("dve", abi, dec, wi[0], ALU.mult, B, B)
            tt("dve", t1, ar, ar, ALU.mult, B, B)
            tt("dve", t2, ai, ai, ALU.mult, B, B)
            tt("dve", den, t1, t2, ALU.add, B, B)
            S.op("dve", lambda v: v.reciprocal(out=den, in_=den), B, B)
            ts("dve", m, abr, -1.0, None, ALU.add, None, B, B)
            tt("dve", t1, m, ar, ALU.mult, B, B)
            tt("dve", t2, abi, ai, ALU.mult, B, B)
            tt("dve", t1, t1, t2, ALU.add, B, B)
            tt("dve", zr, t1, den, ALU.mult, B, B)
            tt("dve", t1, abi, ar, ALU.mult, B, B)
            tt("dve", t2, m, ai, ALU.mult, B, B)
            tt("dve", t1, t1, t2, ALU.subtract, B, B)
            tt("dve", zi, t1, den, ALU.mult, B, B)
            return dict(dec=dec, wr=wr, wi=wi, zr=zr, zi=zi, buf=bb)

        NT = SEQ // TT
        LT = int(math.log2(TT))
        mark0 = AR.off
        cg = consts(NG, 64, i_are_gp, i_aim_gp, i_ldt_gp, LT + 1)
        BG = [cg["buf"]]
        zr, zi, cT, sT = cg["zr"], cg["zi"], cg["wr"][LT], cg["wi"][LT]
        z2r, z2i, zt = sb([NG, 64], F32), sb([NG, 64], F32), sb([NG, 64], F32)
        cur, oth = (zr, zi), (z2r, z2i)
        for j in range(NT):
            S.dma("sp", s_zs[j, 0].rearrange("(g p) -> g p", p=64), cur[0], BG, [db["zs"]], db["zs"])
            S.dma("sp", s_zs[j, 1].rearrange("(g p) -> g p", p=64), cur[1], BG, [db["zs"]], db["zs"])
            if j == NT - 1:
                break
            tt("dve", zt, cur[1], sT, ALU.mult, BG, BG)
            tt("dve", oth[0], cur[0], cT, ALU.mult, BG, BG)
            tt("dve", oth[0], oth[0], zt, ALU.add, BG, BG)
            tt("dve", zt, cur[0], sT, ALU.mult, BG, BG)
            tt("dve", oth[1], cur[1], cT, ALU.mult, BG, BG)
            tt("dve", oth[1], oth[1], zt, ALU.subtract, BG, BG)
            cur, oth = oth, cur
        S.barrier()
        AR.off = mark0
        cp = consts(128, NG, i_are_pg, i_aim_pg, i_ldt_pg, LT + 1)
        b_cp = cp["buf"]
        BP = [b_cp]
        cT, sT = cp["wr"][LT], cp["wi"][LT]
        cosP = sb([128, NT, NG], F32)
        sinP = sb([128, NT, NG], F32)
        ztp = sb([128, NG], F32)
        S.op("dve", lambda v: v.memset(cosP[:, 0, :], 1.0), [], BP)
        S.op("dve", lambda v: v.memset(sinP[:, 0, :], 0.0), [], BP)
        for j in range(1, NT):
            tt("dve", ztp, sinP[:, j - 1, :], sT, ALU.mult, BP, BP)
            tt("dve", cosP[:, j, :], cosP[:, j - 1, :], cT, ALU.mult, BP, BP)
            tt("dve", cosP[:, j, :], cosP[:, j, :], ztp, ALU.subtract, BP, BP)
            tt("dve", ztp, cosP[:, j - 1, :], sT, ALU.mult, BP, BP)
            tt("dve", sinP[:, j, :], sinP[:, j - 1, :], cT, ALU.mult, BP, BP)
            tt("dve", sinP[:, j, :], sinP[:, j, :], ztp, ALU.add, BP, BP)
        NA_ = TT // 32
        mt = AR.off
        for (nm, nlen, k0) in (("B", 32, 0), ("A", NA_, 5)):
            Tc = sb([128, NG, nlen], F32)
            Ts = sb([128, NG, nlen], F32)
            q1 = sb([128, NG, nlen // 2], F32)
            q2 = sb([128, NG, nlen // 2], F32)
            b_T = nb("T2")
            BT = [b_T]
            S.op("dve", lambda v: v.memset(Tc[:, :, 0:1], 1.0), [], BT)
            S.op("dve", lambda v: v.memset(Ts[:, :, 0:1], 0.0), [], BT)
            for k in range(int(math.log2(nlen))):
                n_ = 1 << k
                wrb = cp["wr"][k0 + k].unsqueeze(2).to_broadcast([128, NG, n_])
                wib = cp["wi"][k0 + k].unsqueeze(2).to_broadcast([128, NG, n_])
                tt("dve", q1[:, :, 0:n_], Ts[:, :, 0:n_], wib, ALU.mult, BT + BP, BT)
                tt("dve", q2[:, :, 0:n_], Tc[:, :, 0:n_], wrb, ALU.mult, BT + BP, BT)
                tt("dve", Tc[:, :, n_:2 * n_], q2[:, :, 0:n_], q1[:, :, 0:n_], ALU.subtract, BT, BT)
                tt("dve", q1[:, :, 0:n_], Tc[:, :, 0:n_], wib, ALU.mult, BT + BP, BT)
                tt("dve", q2[:, :, 0:n_], Ts[:, :, 0:n_], wrb, ALU.mult, BT + BP, BT)
                tt("dve", Ts[:, :, n_:2 * n_], q1[:, :, 0:n_], q2[:, :, 0:n_], ALU.add, BT, BT)
            o_ = 0 if nm == "B" else 64
            S.dma("sp", s_tab[:, :, o_:o_ + nlen], Tc, BT, [db["tab"]], b_T)
            S.dma("sp", s_tab[:, :, o_ + nlen:o_ + 2 * nlen], Ts, BT, [db["tab"]], b_T)
        S.barrier()
        AR.off = mt
        W1 = sb([128, NT // 2, NG * 16], BF16)
        W2 = sb([128, NT // 2, NG * 16], BF16)
        b_W = nb("W12")
        sgn = sb([128, 1], F32)
        S.op("dve", lambda v: v.memset(sgn[0:64, :], 1.0), [], [b_W])
        S.op("dve", lambda v: v.memset(sgn[64:128, :], -1.0), [], [b_W])
        dsk = sb([16, NG], F32)
        flag = sb([128, 1], F32)
        diagd = sb([16, NG, 16], BF16)
        identf = sb([16, 16], F32)
        b_misc = nb("misc")
        S.dma("sp", dsk, i_dsk, [], [b_misc], b_misc)
        S.dma("sp", flag, i_flag, [], [b_misc], b_misc)
        S.op("dve", lambda v: v.tensor_copy(out=identf, in_=ident[0:16, 0:16]), [b_ident], [b_misc])
        S.op("dve", lambda v: v.tensor_tensor(out=diagd, in0=identf.unsqueeze(1).to_broadcast([16, NG, 16]),
                                              in1=dsk.unsqueeze(2).to_broadcast([16, NG, 16]), op=ALU.mult),
             [b_misc], [b_misc])
        mark1 = AR.off
        cc1 = sb([128, NG * 16], F32)
        cc2 = sb([128, NG * 16], F32)
        wa = sb([128, NG * 16], F32)
        wb_ = sb([128, NG * 16], F32)
        b_cc = nb("cc")
        BCC = [b_cc]
        S.dma("sp", cc1, i_cc1, [], BCC, b_cc)
        S.dma("sp", cc2, i_cc2, [], BCC, b_cc)
        v3w = lambda a: a.rearrange("p (g c) -> p g c", c=16)
        for j in range(NT // 2, NT):
            cb = cosP[:, j, :].unsqueeze(2).to_broadcast([128, NG, 16])
            sb_ = sinP[:, j, :].unsqueeze(2).to_broadcast([128, NG, 16])
            tt("dve", v3w(wa), v3w(cc1), cb, ALU.mult, BCC + BP, BCC)
            tt("dve", v3w(wb_), v3w(cc2), sb_, ALU.mult, BCC + BP, BCC)
            S.op("dve", lambda v, j=j: v.scalar_tensor_tensor(out=W1[:, j - NT // 2, :], in0=wa, scalar=sgn[:, 0:1], in1=wb_,
                                                              op0=ALU.mult, op1=ALU.subtract), BCC + [b_W], [b_W])
            tt("dve", v3w(wa), v3w(cc2), cb, ALU.mult, BCC + BP, BCC)
            tt("dve", v3w(wb_), v3w(cc1), sb_, ALU.mult, BCC + BP, BCC)
            S.op("dve", lambda v: v.scalar_tensor_tensor(out=wa, in0=wb_, scalar=sgn[:, 0:1], in1=wa,
                                                         op0=ALU.mult, op1=ALU.add), BCC + [b_W], BCC)
            S.op("dve", lambda v, j=j: v.tensor_scalar(out=W2[:, j - NT // 2, :], in0=wa, scalar1=-1.0, scalar2=None,
                                                       op0=ALU.mult), BCC, [b_W])
        S.barrier()
        AR.off = mark1
        GC = min(16, NG)
        NGC = NG // GC
        PB = NGC * 16
        n = GC * 64
        zrb, zib, bre, bim, t1, t2 = (sb([PB, n], F32) for _ in range(6))
        obs = [(sb([PB, GC, 256], BF16), nb("ob")) for _ in range(2)]
        bz, bbi = nb("bz"), nb("bbi")
        v3 = lambda a: a.rearrange("c (g p) -> c g p", p=64)
        for gc in range(NGC):
            ps_ = slice(gc * 16, (gc + 1) * 16)
            S.dma("sp", bre[ps_, :], i_bre[:, gc * n:(gc + 1) * n], [], [bbi], bbi)
            S.dma("sp", bim[ps_, :], i_bim[:, gc * n:(gc + 1) * n], [], [bbi], bbi)
        for j in range(NT):
            for gc in range(NGC):
                ps_ = slice(gc * 16, (gc + 1) * 16)
                S.dma("sp", zrb[ps_, :], s_zs[j, 0, gc * n:(gc + 1) * n].partition_broadcast(16), [db["zs"]], [bz], bz)
                S.dma("sp", zib[ps_, :], s_zs[j, 1, gc * n:(gc + 1) * n].partition_broadcast(16), [db["zs"]], [bz], bz)
            ob, b_ob = obs[j % 2]
            B = [bz]
            tt("dve", t1, zrb, bre, ALU.mult, B + [bbi], B)
            tt("dve", t2, zib, bim, ALU.mult, B + [bbi], B)
            tt("dve", t1, t1, t2, ALU.subtract, B, B)
            tt("dve", t2, zrb, bim, ALU.mult, B + [bbi], B)
            tt("dve", zrb, zib, bre, ALU.mult, B + [bbi], B)
            tt("dve", t2, t2, zrb, ALU.add, B, B)
            S.op("dve", lambda v: v.tensor_copy(out=ob[:, :, 0:64], in_=v3(t1)), B, [b_ob])
            S.op("dve", lambda v: v.tensor_copy(out=ob[:, :, 64:128], in_=v3(t2)), B, [b_ob])
            S.op("dve", lambda v: v.tensor_copy(out=ob[:, :, 128:192], in_=v3(t2)), B, [b_ob])
            S.op("dve", lambda v: v.tensor_scalar(out=ob[:, :, 192:256], in0=v3(t1), scalar1=-1.0, scalar2=None,
                                                  op0=ALU.mult), B, [b_ob])
            for gc in range(NGC):
                ps_ = slice(gc * 16, (gc + 1) * 16)
                S.dma("sp", s_bbar[gc * GC:(gc + 1) * GC, j].rearrange("g c f -> c g f"), ob[ps_], [b_ob],
                      [db["bbar"]], b_ob)
        S.barrier()
        AR.off = mark1

        NTAB = NUG = (4 if NT >= 5 else 5)
        tabs = [(sb([128, TT], F32), sb([128, TT], F32), nb("tab")) for _ in range(NTAB)]
        ugs = [(sb([16, SEQ], BF16), sb([16, NT, 256], BF16), nb("ug")) for _ in range(NUG)]
        tmpA = (sb([128, TT], F32), nb("tmpA"))
        tmpB = (sb([128, TT], F32), nb("tmpB"))
        tls = [(sb([128, 64 + 2 * NA_], F32), nb("tl")) for _ in range(NTAB)]
        t1s = [(sb([128, TT], F32), nb("t1s")) for _ in range(2)]
        t2s = [(sb([128, TT], F32), nb("t2s")) for _ in range(2)]
        bps = [(sb([128, TT], F32), nb("bp")) for _ in range(2)]
        gts = [(sb([128, TT], F32), nb("gt")) for _ in range(3)]
        X1s = [(sb([128, TT], BF16), nb("X1")) for _ in range(2)]
        X2s = [(sb([128, TT], BF16), nb("X2")) for _ in range(2)]
        ysts = [(sb([16, TT], F32), nb("yst")) for _ in range(3)]
        init = sb([128, 1], F32)
        b_init = nb("init")
        NTOT = NG * NT

        def load_group(g):
            ug, bbt, b_ug = ugs[g % NUG]
            S.dma("sp", ug, s_uT[g * 16:(g + 1) * 16, :], [db["uT"]], [b_ug], b_ug)
            S.dma("sp", bbt, s_bbar[g].rearrange("j c f -> c j f"), [db["bbar"]], [b_ug], b_ug)

        def tab_level(g, lvl):
            Ct, St, b_tab = tabs[g % NTAB]
            tl, b_tl = tls[g % NTAB]
            ta, b_ta = tmpA
            tb, b_tb = tmpB
            shp = [128, NA_, 32]
            Bc = tl[:, 0:32].unsqueeze(1).to_broadcast(shp)
            Bs = tl[:, 32:64].unsqueeze(1).to_broadcast(shp)
            Ac = tl[:, 64:64 + NA_].unsqueeze(2).to_broadcast(shp)
            As = tl[:, 64 + NA_:64 + 2 * NA_].unsqueeze(2).to_broadcast(shp)
            v3t = lambda a: a.rearrange("p (a b) -> p a b", b=32)
            if lvl == 0:
                S.dma("sp", tl, s_tab[:, g, :], [db["tab"]], [b_tl], b_tl)
                tt("dve", v3t(ta), Ac, Bc, ALU.mult, [b_tl], [b_ta])
            elif lvl == 1:
                tt("dve", v3t(tb), As, Bs, ALU.mult, [b_tl], [b_tb])
            elif lvl == 2:
                tt("dve", Ct, ta, tb, ALU.subtract, [b_ta, b_tb], [b_tab])
            elif lvl == 3:
                tt("dve", v3t(ta), As, Bc, ALU.mult, [b_tl], [b_ta])
            elif lvl == 4:
                tt("dve", v3t(tb), Ac, Bs, ALU.mult, [b_tl], [b_tb])
            elif lvl == 5:
                tt("dve", St, ta, tb, ALU.add, [b_ta, b_tb], [b_tab])

        NLV = 6
        for g in range(min(2, NG)):
            load_group(g)
            for lvl in range(NLV):
                tab_level(g, lvl)
        lv_per_it = (NLV + NT - 1) // NT
        prevgt = {}
        for it in range(NTOT + 7):
            gi, ji = divmod(it, NT)
            if it < NTOT:
                if ji == 0 and gi + 2 < NG:
                    load_group(gi + 2)
                if gi + 2 < NG:
                    for lvl in range(ji * lv_per_it, min(NLV, (ji + 1) * lv_per_it)):
                        tab_level(gi + 2, lvl)
            i = it
            if 0 <= i < NTOT:
                g, j = divmod(i, NT)
                ug, bbt, b_ug = ugs[g % NUG]
                js = slice(j * TT, (j + 1) * TT)
                bA, bAb = banks[(i % 2) * 2]
                bB, bBb = banks[(i % 2) * 2 + 1]
                S.group([lambda t: t.matmul(bA[:, :], bbt[:, j, 0:128], ug[:, js], start=True, stop=True)], [b_ug], [bAb])
                S.group([lambda t: t.matmul(bB[:, :], bbt[:, j, 128:256], ug[:, js], start=True, stop=True)], [b_ug], [bBb])
            i = it - 1
            if 0 <= i < NTOT:
                g, j = divmod(i, NT)
                Ct, St, b_tab = tabs[g % NTAB]
                bA, bAb = banks[(i % 2) * 2]
                bB, bBb = banks[(i % 2) * 2 + 1]
                t1, b_t1 = t1s[i % 2]
                t2, b_t2 = t2s[i % 2]
                tt("dve", t1, bA[:, :], Ct, ALU.mult, [bAb, b_tab], [b_t1])
                tt("dve", t2, bB[:, :], St, ALU.mult, [bBb, b_tab], [b_t2])
            i = it - 2
            if 0 <= i < NTOT:
                t1, b_t1 = t1s[i % 2]
                t2, b_t2 = t2s[i % 2]
                bp, b_bp = bps[i % 2]
                tt("pool", bp, t1, t2, ALU.add, [b_t1, b_t2], [b_bp])
            i = it - 3
            if 0 <= i < NTOT:
                g, j = divmod(i, NT)
                bp, b_bp = bps[i % 2]
                gt, b_gt = gts[i % 3]
                rbc = cp["dec"][:, g:g + 1].to_broadcast([128, TT])
                if j == 0:
                    ini, rd = 0.0, []
                elif j == NT // 2:
                    pgt = prevgt[i - 1]
                    S.op("dve", lambda v: v.tensor_tensor(out=init, in0=pgt[0][:, TT - 1:TT], in1=flag, op=ALU.mult),
                         [pgt[1], b_misc], [b_init])
                    ini, rd = init, [b_init]
                else:
                    pgt = prevgt[i - 1]
                    ini, rd = pgt[0][:, TT - 1:TT], [pgt[1]]
                S.op("dve", lambda v: v.tensor_tensor_scan(out=gt, data0=rbc, data1=bp, initial=ini, op0=ALU.mult,
                                                           op1=ALU.add), [b_bp, b_cp] + rd, [b_gt])
                prevgt[i] = (gt, b_gt)
                prevgt.pop(i - 2, None)
            i = it - 4
            if 0 <= i < NTOT and (i % NT) >= NT // 2:
                g, j = divmod(i, NT)
                Ct, St, b_tab = tabs[g % NTAB]
                gt, b_gt = gts[i % 3]
                X1, b_X1 = X1s[i % 2]
                X2, b_X2 = X2s[i % 2]
                tt("pool", X1, gt, Ct, ALU.mult, [b_gt, b_tab], [b_X1])
                tt("pool", X2, gt, St, ALU.mult, [b_gt, b_tab], [b_X2])
            i = it - 5
            if 0 <= i < NTOT and (i % NT) >= NT // 2:
                g, j = divmod(i, NT)
                ug, bbt, b_ug = ugs[g % NUG]
                js = slice(j * TT, (j + 1) * TT)
                X1, b_X1 = X1s[i % 2]
                X2, b_X2 = X2s[i % 2]
                bY, bYb = banks[4 + i % 2]
                gsl = slice(g * 16, (g + 1) * 16)
                S.group([lambda t: t.matmul(bY[0:16, :], W1[:, j - NT // 2, gsl], X1, start=True, stop=False),
                         lambda t: t.matmul(bY[0:16, :], W2[:, j - NT // 2, gsl], X2, start=False, stop=False),
                         lambda t: t.matmul(bY[0:16, :], diagd[:, g, :], ug[:, js], start=False, stop=True)],
                        [b_W, b_X1, b_X2, b_ug, b_misc], [bYb])
            i = it - 6
            if 0 <= i < NTOT and (i % NT) >= NT // 2:
                g, j = divmod(i, NT)
                bY, bYb = banks[4 + i % 2]
                yst, b_yst = ysts[i % 3]
                S.op("act", lambda a: a.activation(out=yst, in_=bY[0:16, :], func=AF.Copy), [bYb], [b_yst])
                o0 = (j - NT // 2) * TT
                S.dma("sp", s_yaT[g * 16:(g + 1) * 16, o0:o0 + TT], yst, [b_yst], [db["yaT"]], b_yst)

    if "B" in phases:
        phaseB()
    S.barrier()
    AR.off = persist_mark

    def phaseN():
        NQ4 = NQC
        b_k = nb("ncon")
        BK = [b_k]

        def cload(shape, src, dt=BF16, q="pool", **kw):
            t = sb(shape, dt)
            S.dma(q, t, src, [], BK, b_k, **kw)
            return t

        w1k = cload([128, 32, 128], i_w1k.rearrange("(l d) h -> d l h", d=128))
        w1v = cload([128, 32, 128], i_w1v.rearrange("(l d) h -> d l h", d=128))
        w2k = cload([128, 128], i_w2k)
        w2v = cload([128, 128], i_w2v)
        pek = cload([128, 32], i_pek)
        pev = cload([128, 32], i_pev)
        cmpb = cload([128, NCT, SH], i_cmpb.rearrange("(a p) t -> p a t", p=128))
        expd = cload([NB, NKT * 128], i_exp, max_dma_last_dim=4096).rearrange("j (k m) -> j k m", m=128)
        caus = cload([128, 4, 512], i_caus.rearrange("(v p) t -> p v t", p=128))
        wlo = cload([128, 128], i_wlo)
        whi = cload([128, 128], i_whi)
        wctx = cload([128, 128], i_wctx)
        selA = cload([128, NQT, NB], i_selA.rearrange("(q p) j -> p q j", p=128), F32, "sp")
        selB = cload([128, NQT, NB], i_selB.rearrange("(q p) j -> p q j", p=128), F32, "sp")
        selM = cload([128, NQT, NB], i_selM.rearrange("(q p) j -> p q j", p=128), F32, "sp")
        gnt = sb([128, NQT, 3 * NH], F32)
        S.dma("sp", gnt, s_gn.rearrange("(q p) j -> p q j", p=128), [db["gn"]], BK, b_k)
        NW = 128 + 1 + NB
        pebias = sb([128, 2], F32)
        for i, (w1, pe) in enumerate(((w1k, pek), (w1v, pev))):
            bk, bkb = banks[4 + i]
            S.group([lambda t, l=l, w1=w1, pe=pe, bk=bk: t.matmul(bk[:, 0:1], w1[:, l, :], pe[:, l:l + 1],
                                                                 start=(l == 0), stop=(l == 31)) for l in range(32)],
                    BK, [bkb])
            S.op("dve", lambda v, i=i, bk=bk: v.tensor_copy(out=pebias[:, i:i + 1], in_=bk[:, 0:1]), [bkb], BK)
        mark = AR.off
        for g in range(NKV):
            AR.off = mark
            b_kv = nb("kv")
            BKV = [b_kv]
            kcT = sb([128, SEQ], BF16)
            vcT = sb([128, SEQ], BF16)
            ksT = sb([128, SEQ], BF16)
            kwT = sb([128, SEQ], BF16)
            vs = sb([128, NKT, 132], BF16)
            vw = sb([128, NKT, 132], BF16)
            qT = sb([128, HPG, SH], BF16)
            gs = slice(g * 128, (g + 1) * 128)
            S.dma("sp", kcT, s_kT[0, gs, :], [db["kT"]], BKV, b_kv)
            S.dma("sp", ksT, s_kT[1, gs, :], [db["kT"]], BKV, b_kv)
            w0 = (NQT - 4) * 128
            S.dma("sp", kwT[:, w0:], s_kT[2, gs, w0:], [db["kT"]], BKV, b_kv)
            S.dma("sp", vcT, s_vcT[gs, :], [db["vcT"]], BKV, b_kv)
            S.dma("sp", vs[:, :, 0:128], s_vt[0, :, gs].rearrange("(k p) d -> p k d", p=128), [db["vt"]], BKV, b_kv)
            S.dma("sp", vw[:, NQT - 4:, 0:128], s_vt[1, w0:, gs].rearrange("(k p) d -> p k d", p=128), [db["vt"]], BKV,
                  b_kv)
            S.dma("sp", qT, s_qT[g * HPG * 128:(g + 1) * HPG * 128, :].rearrange("(h d) t -> d h t", d=128),
                  [db["qT"]], BKV, b_kv)
            S.op("pool", lambda v, vs=vs: v.memset(vs[:, :, 128:129], 1.0), [], BKV)
            S.op("pool", lambda v, vw=vw: v.memset(vw[:, :, 128:129], 1.0), [], BKV)
            hT = sb([128, 2, NC_], BF16)
            kcmpT = sb([128, NC_], BF16)
            vext = sb([128, NCT, NW], BF16)
            b_cmp = nb("cmp")
            BC = [b_cmp]
            S.op("pool", lambda v, vext=vext: v.memset(vext[:, :, 128:129], 1.0), [], BC)
            S.dma("pool", vext[:, :, 129:NW], i_ovl.rearrange("(a p) j -> p a j", p=128), [], BC, b_cmp)
            gtmp = sb([128, 3, NC_], F32)
            for i, (srcT, w1) in enumerate(((kcT, w1k), (vcT, w1v))):
                bk, bkb = banks[i]
                S.group([lambda t, l=l, w1=w1, srcT=srcT, bk=bk: t.matmul(
                    bk[:, 0:NC_], w1[:, l, :], srcT[:, l:l + 16 * (NC_ - 1) + 1:16], start=(l == 0), stop=(l == 31))
                    for l in range(32)], BK + BKV, [bkb])
                xx, x3, sg = gtmp[:, 0, :], gtmp[:, 1, :], gtmp[:, 2, :]
                S.op("dve", lambda v, xx=xx, bk=bk, i=i: v.tensor_scalar(
                    out=xx, in0=bk[:, 0:NC_], scalar1=pebias[:, i:i + 1], scalar2=None, op0=ALU.add), [bkb] + BK, BC)
                S.op("dve", lambda v, xx=xx, x3=x3: v.tensor_tensor(out=x3, in0=xx, in1=xx, op=ALU.mult), BC, BC)
                S.op("dve", lambda v, x3=x3: v.tensor_scalar(out=x3, in0=x3, scalar1=0.044715, scalar2=1.0,
                                                              op0=ALU.mult, op1=ALU.add), BC, BC)
                S.op("dve", lambda v, xx=xx, x3=x3: v.tensor_tensor(out=x3, in0=x3, in1=xx, op=ALU.mult), BC, BC)
                S.op("act", lambda a, x3=x3, sg=sg: a.activation(out=sg, in_=x3, func=AF.Sigmoid, scale=1.5957691216),
                     BC, BC)
                S.op("dve", lambda v, xx=xx, sg=sg, i=i: v.tensor_tensor(out=hT[:, i, :], in0=xx, in1=sg, op=ALU.mult),
                     BC, BC)
            bk, bkb = banks[2]
            S.group([lambda t, bk=bk: t.matmul(bk[:, 0:NC_], w2k, hT[:, 0, :], start=True, stop=True)], BK + BC,
                    [bkb])
            S.op("act", lambda a, bk=bk: a.activation(out=kcmpT, in_=bk[:, 0:NC_], func=AF.Copy), [bkb], BC)
            for a_ in range(NCT):
                na = min(128, NC_ - a_ * 128)
                bk, bkb = banks[3]
                S.group([lambda t, bk=bk, a_=a_, na=na: t.matmul(bk[0:na, 0:128], hT[:, 1, a_ * 128:a_ * 128 + na],
                                                                 w2v, start=True, stop=True)], BK + BC, [bkb])
                S.op("dve", lambda v, bk=bk, a_=a_, na=na: v.tensor_copy(out=vext[0:na, a_, 0:128],
                                                                         in_=bk[0:na, 0:128]), [bkb], BC)
            yb = sb([128, NQT, HPG * 128], F32)
            b_yb = [nb("yb") for _ in range(NQT)]
            imp = sb([128, NQT, NB], F32)
            b_imp = [nb("imp") for _ in range(NQT)]
            sc8 = sb([128, 16], F32)
            b_sc = nb("sc8")
            erot = Rot([(sb([128, 512], BF16), nb("e")) for _ in range(4)])

            def gate_scale(qt, h, br, den_ap, den_reads):
                col = br * NH + g * HPG + h
                S.op("dve", lambda v: v.tensor_scalar(out=sc8[:, 1:2], in0=den_ap, scalar1=1e-30, scalar2=None,
                                                      op0=ALU.max), den_reads, [b_sc])
                S.op("dve", lambda v: v.reciprocal(out=sc8[:, 2:3], in_=sc8[:, 1:2]), [b_sc], [b_sc])
                S.op("dve", lambda v: v.tensor_tensor(out=sc8[:, 0:1], in0=sc8[:, 2:3], in1=gnt[:, qt, col:col + 1],
                                                      op=ALU.mult), [b_sc] + BK, [b_sc])

            def accum_out(qt, h, num_ap, num_reads, first):
                dst = yb[:, qt, h * 128:(h + 1) * 128]
                if first:
                    S.op("dve", lambda v: v.tensor_scalar(out=dst, in0=num_ap, scalar1=sc8[:, 0:1], scalar2=None,
                                                          op0=ALU.mult), num_reads + [b_sc], [b_yb[qt]])
                else:
                    S.op("dve", lambda v: v.scalar_tensor_tensor(out=dst, in0=num_ap, scalar=sc8[:, 0:1], in1=dst,
                                                                 op0=ALU.mult, op1=ALU.add),
                         num_reads + [b_sc], [b_yb[qt]])

            selbT = sb([NB, SH], BF16)
            b_selT = nb("selT")
            scr = sb([128, 3, NB], F32)
            m8 = sb([128, 16], F32)
            selb = sb([128, NB], BF16)
            b_s3 = nb("s3")
            B3 = [b_s3]

            def n3(qt):
                sc_, wk_, se_ = scr[:, 0, :], scr[:, 1, :], scr[:, 2, :]
                S.op("dve", lambda v: v.tensor_tensor(out=sc_, in0=imp[:, qt, :], in1=selA[:, qt, :], op=ALU.mult),
                     [b_imp[qt]] + BK, B3)
                S.op("dve", lambda v: v.tensor_tensor(out=sc_, in0=sc_, in1=selB[:, qt, :], op=ALU.add), BK, B3)
                S.op("dve", lambda v: v.max(out=m8[:, 0:8], in_=sc_), B3, B3)
                S.op("dve", lambda v: v.match_replace(out=wk_, in_to_replace=m8[:, 0:8], in_values=sc_,
                                                      imm_value=-3.0e38), B3, B3)
                S.op("dve", lambda v: v.max(out=m8[:, 8:16], in_=wk_), B3, B3)
                S.op("dve", lambda v: v.tensor_scalar(out=se_, in0=sc_, scalar1=m8[:, 15:16], scalar2=None,
                                                      op0=ALU.is_ge), B3, B3)
                S.op("dve", lambda v: v.tensor_tensor(out=se_, in0=se_, in1=selM[:, qt, :], op=ALU.mult), BK, B3)
                S.op("dve", lambda v: v.tensor_scalar(out=selb, in0=se_, scalar1=-1.0, scalar2=-NEG, op0=ALU.add,
                                                      op1=ALU.mult), B3, B3)
                bk, bkb = banks[6 + qt % 2]
                pst = bk[:].bitcast(BF16)
                S.group([lambda t: t.transpose(out=pst[0:NB, 0:128], in_=selb, identity=ident)], B3 + [b_ident], [bkb])
                S.op("act", lambda a: a.activation(out=selbT[:, qt * 128:(qt + 1) * 128], in_=pst[0:NB, 0:128],
                                                   func=AF.Copy), [bkb], [b_selT])

            def n2_scores(h, qc):
                qs = slice(qc * 512, (qc + 1) * 512)
                es_ = []
                for a_ in range(NCT):
                    na = min(128, NC_ - a_ * 128)
                    bk, bkb = banks[a_ % 2]
                    S.group([lambda t: t.matmul(bk[0:na, :], kcmpT[:, a_ * 128:a_ * 128 + na], qT[:, h, qs],
                                                start=True, stop=False),
                             lambda t: t.matmul(bk[0:na, :], ident[0:na, 0:na], cmpb[0:na, a_, qs], start=False,
                                                stop=True)], BK + BKV + BC + [b_ident], [bkb])
                    e, b_e = erot.next()
                    S.op("act", lambda a: a.activation(out=e[0:na, :], in_=bk[0:na, :], func=AF.Exp), [bkb], [b_e])
                    es_.append((e, b_e, na))
                return es_

            def n2_pv(h, qc, es_):
                for q4 in range(4):
                    qt = qc * 4 + q4
                    bk, bkb = banks[2 + q4 % 2]
                    S.group([lambda t, a_=a_, e=es_[a_][0], na=es_[a_][2]: t.matmul(
                        bk[:, 0:NW], e[0:na, q4 * 128:(q4 + 1) * 128], vext[0:na, a_, :], start=(a_ == 0),
                        stop=(a_ == NCT - 1)) for a_ in range(NCT)], [x[1] for x in es_] + BC, [bkb])
                    gate_scale(qt, h, 0, bk[:, 128:129], [bkb])
                    accum_out(qt, h, bk[:, 0:128], [bkb], True)
                    if h == 0:
                        S.op("dve", lambda v: v.tensor_scalar(out=imp[:, qt, :], in0=bk[:, 129:NW],
                                                              scalar1=sc8[:, 2:3], scalar2=None, op0=ALU.mult),
                             [bkb, b_sc], [b_imp[qt]])
                    else:
                        S.op("dve", lambda v: v.scalar_tensor_tensor(out=imp[:, qt, :], in0=bk[:, 129:NW],
                                                                     scalar=sc8[:, 2:3], in1=imp[:, qt, :],
                                                                     op0=ALU.mult, op1=ALU.add),
                             [bkb, b_sc], [b_imp[qt]])

            pend = None
            for qc in range(NQ4):
                for h in range(HPG):
                    es_ = n2_scores(h, qc)
                    if pend is not None:
                        n2_pv(*pend)
                        if pend[0] == HPG - 1:
                            for q4 in range(4):
                                n3(pend[1] * 4 + q4)
                    pend = (h, qc, es_)
            n2_pv(*pend)
            for q4 in range(4):
                n3(pend[1] * 4 + q4)
            def n4_pv(h, qc, ki, nk_, kt, e, b_e, obk):
                for q4 in range(4):
                    ob, obb = obk[q4]
                    S.group([lambda t: t.matmul(ob[:, 0:129], e[:, q4 * 128:(q4 + 1) * 128], vs[:, kt, 0:129],
                                                start=(ki == 0), stop=(ki == nk_ - 1))], [b_e] + BKV, [obb])
                if ki == nk_ - 1:
                    for q4 in range(4):
                        qt = qc * 4 + q4
                        ob, obb = obk[q4]
                        gate_scale(qt, h, 1, ob[:, 128:129], [obb])
                        accum_out(qt, h, ob[:, 0:128], [obb], False)

            pend = None
            stepn = 0
            for h in range(HPG):
                for qc in range(NQ4):
                    qs = slice(qc * 512, (qc + 1) * 512)
                    kts = list(range(NQT)) + [NQT + i for i in range(4 * qc + 4)]
                    obk = [banks[4 + q4] for q4 in range(4)]
                    for ki, kt in enumerate(kts):
                        bk, bkb = banks[stepn % 2]
                        stepn += 1
                        fns = [lambda t: t.matmul(bk[:, :], ksT[:, kt * 128:(kt + 1) * 128], qT[:, h, qs], start=True,
                                                  stop=False)]
                        diag = kt - NQT - 4 * qc
                        last_is_sel = not (0 <= diag < 4)
                        fns.append(lambda t: t.matmul(bk[:, :], expd[:, kt, :], selbT[:, qs], start=False,
                                                      stop=last_is_sel))
                        if not last_is_sel:
                            fns.append(lambda t: t.matmul(bk[:, :], ident, caus[:, diag, :], start=False, stop=True))
                        S.group(fns, BK + BKV + [b_selT, b_ident], [bkb])
                        e, b_e = erot.next()
                        S.op("act", lambda a: a.activation(out=e, in_=bk[:, :], func=AF.Exp), [bkb], [b_e])
                        if pend is not None:
                            n4_pv(*pend)
                        pend = (h, qc, ki, len(kts), kt, e, b_e, obk)
            n4_pv(*pend)
            def n5_pv(h, qt, wi_, kt, e, b_e, ob, obb):
                S.group([lambda t: t.matmul(ob[:, 0:129], e[:, 0:128], vw[:, kt, 0:129], start=(wi_ == 0),
                                            stop=(wi_ == 4))], [b_e] + BKV, [obb])
                if wi_ == 4:
                    gate_scale(qt, h, 2, ob[:, 128:129], [obb])
                    accum_out(qt, h, ob[:, 0:128], [obb], False)

            pend = None
            stepn = 0
            for h in range(HPG):
                for qt in range(NQT):
                    A_ = NQT + qt
                    qs = slice(qt * 128, (qt + 1) * 128)
                    ob, obb = banks[4 + qt % 4]
                    for wi_ in range(5):
                        kt = A_ - 4 + wi_
                        bk, bkb = banks[stepn % 2]
                        stepn += 1
                        extra = []
                        if wi_ == 0:
                            extra.append(wlo)
                        if wi_ == 4:
                            extra.append(whi)
                        if kt < NQT:
                            extra.append(wctx)
                        fns = [lambda t: t.matmul(bk[:, 0:128], kwT[:, kt * 128:(kt + 1) * 128], qT[:, h, qs],
                                                  start=True, stop=(len(extra) == 0))]
                        for xi, xm in enumerate(extra):
                            fns.append(lambda t, xm=xm, l=(xi == len(extra) - 1): t.matmul(
                                bk[:, 0:128], ident, xm, start=False, stop=l))
                        S.group(fns, BK + BKV + [b_ident], [bkb])
                        e, b_e = erot.next()
                        S.op("act", lambda a: a.activation(out=e[:, 0:128], in_=bk[:, 0:128], func=AF.Exp),
                             [bkb], [b_e])
                        if pend is not None:
                            n5_pv(*pend)
                        pend = (h, qt, wi_, kt, e, b_e, ob, obb)
            n5_pv(*pend)
            ybb = sb([128, HPG * 128], BF16)
            b_ybb = nb("ybb")
            for qt in range(NQT):
                S.op("act", lambda a, qt=qt: a.activation(out=ybb, in_=yb[:, qt, :], func=AF.Copy),
                     [b_yb[qt]], [b_ybb])
                bk, bkb = banks[qt % 2]
                pst = bk[:].bitcast(BF16)
                for h in range(HPG):
                    S.group([lambda t, pst=pst, h=h: t.transpose(out=pst[:, h * 128:(h + 1) * 128],
                                                                 in_=ybb[:, h * 128:(h + 1) * 128], identity=ident)],
                            [b_ybb, b_ident], [bkb])
                st, stb = stg_bf.next()
                S.op("dve", lambda v, st=st, pst=pst: v.tensor_copy(out=st[:, 0:HPG * 128], in_=pst[:, 0:HPG * 128]),
                     [bkb], [stb])
                r0 = g * HPG * 128
                S.dma("sp", s_ybT[r0:r0 + HPG * 128, qt * 128:(qt + 1) * 128].rearrange("(h d) t -> d h t", d=128),
                      st[:, 0:HPG * 128].rearrange("d (h t) -> d h t", t=128), [stb], [db["ybT"]], stb)
            S.barrier()

    if "N" in phases:
        phaseN()
    S.barrier()
    AR.off = persist_mark


    def phaseC():
        make_wslots(3, 8192)
        bglu = sb([128, KS], F32)
        b_bg = nb("bglu")
        S.dma("sp", bglu, i_bglu, [], [b_bg], b_bg)
        ya = sb([128, KS, TT], BF16)
        b_ya = nb("ya")
        ya2 = sb([128, KS, TT], BF16)
        b_ya2 = nb("ya2")
        ybt = sb([128, KA, TT], BF16)
        b_ybt = nb("ybt")
        mg = sb([128, KD, TT], BF16)
        b_mg = nb("mg")
        yin = Rot([(sb([128, TT], F32), nb("yin")) for _ in range(2)])
        tmp = Rot([(sb([128, TT], F32), nb("ctmp")) for _ in range(2)])
        gat = Rot([(sb([128, TT], BF16), nb("gate")) for _ in range(3)])
        w_glu3, w_pa3, w_pb3, w_out3 = wview(w_glu), wview(w_pa), wview(w_pb), wview(w_out)
        for ti in range(NTO):
            tok0 = ti * TT
            ts_ = slice(tok0, tok0 + TT)
            S.dma("sp", ybt, s_ybT[:, ts_].rearrange("(k p) t -> p k t", p=128), [db["ybT"]], [b_ybt], b_ybt)
            for k in range(KS):
                yi, b_yi = yin.next()
                t1, b_t1 = tmp.next()
                S.dma("sp", yi, s_yaT[k * 128:(k + 1) * 128, ts_], [db["yaT"]], [b_yi], b_yi)
                S.op("dve", lambda v, yi=yi, t1=t1: v.tensor_tensor(out=t1, in0=yi, in1=yi, op=ALU.mult), [b_yi], [b_t1])
                S.op("dve", lambda v, t1=t1: v.tensor_scalar(out=t1, in0=t1, scalar1=0.044715, scalar2=1.0,
                                                              op0=ALU.mult, op1=ALU.add), [b_t1], [b_t1])
                S.op("pool", lambda v, yi=yi, t1=t1: v.tensor_tensor(out=t1, in0=t1, in1=yi, op=ALU.mult),
                     [b_yi, b_t1], [b_t1])
                S.op("act", lambda a, t1=t1: a.activation(out=t1, in_=t1, func=AF.Sigmoid, scale=1.5957691216),
                     [b_t1], [b_t1])
                S.op("pool", lambda v, yi=yi, t1=t1, k=k: v.tensor_tensor(out=ya[:, k, :], in0=yi, in1=t1, op=ALU.mult),
                     [b_yi, b_t1], [b_ya])

            def glu_post(r0, mw, bk, bkb):
                oc = r0 // 128
                t1, b_t1 = tmp.next()
                S.op("act", lambda a: a.activation(out=t1[0:mw, :], in_=bk[0:mw, 0:TT], func=AF.Sigmoid,
                                                   bias=bglu[0:mw, oc:oc + 1]), [bkb, b_bg], [b_t1])
                S.op("dve", lambda v: v.tensor_tensor(out=ya2[0:mw, oc, :], in0=ya[0:mw, oc, :], in1=t1[0:mw, :],
                                                      op=ALU.mult), [b_t1, b_ya], [b_ya2])

            wst["ck"] = ("glu", (SW + 511) // 512)
            projF(w_glu3, 0, SW, KS, ya, b_ya, None, None, post=glu_post)

            def merge_post(first, gsrc, gbuf):
                def post(r0, mw, bk, bkb):
                    oc = r0 // 128
                    gt_, b_gt = gat.next()
                    S.dma("sp", gt_[0:mw, :], gsrc[r0:r0 + mw, ts_], [gbuf], [b_gt], b_gt)
                    if first:
                        S.op("dve", lambda v: v.tensor_tensor(out=mg[0:mw, oc, :], in0=bk[0:mw, 0:TT], in1=gt_[0:mw, :],
                                                              op=ALU.mult), [bkb, b_gt], [b_mg])
                    else:
                        t1, b_t1 = tmp.next()
                        S.op("dve", lambda v: v.tensor_tensor(out=t1[0:mw, :], in0=bk[0:mw, 0:TT], in1=gt_[0:mw, :],
                                                              op=ALU.mult), [bkb, b_gt], [b_t1])
                        S.op("pool", lambda v: v.tensor_tensor(out=mg[0:mw, oc, :], in0=mg[0:mw, oc, :],
                                                               in1=t1[0:mw, :], op=ALU.add), [b_t1], [b_mg])
                return post

            wst["ck"] = ("pa", (D + 511) // 512)
            projF(w_pa3, 0, D, KS, ya2, b_ya2, None, None, post=merge_post(True, s_gaT, db["gaT"]))
            wst["ck"] = ("pb", (D + 511) // 512)
            projF(w_pb3, 0, D, KA, ybt, b_ybt, None, None, post=merge_post(False, s_gbT, db["gbT"]))
            wst["ck"] = ("out", ((D + 511) // 512) * ((KD + 15) // 16))
            projT(w_out3, 0, D, KD, mg, b_mg,
                  lambda sub, cb, cw: s_otok[tok0 + sub * 128:tok0 + (sub + 1) * 128, cb:cb + cw], db["otok"],
                  f32out=True, kchunk=16, wide=True)

    if "C" in phases:
        phaseC()
    S.barrier()
    AR.off = persist_mark

    def phaseD():
        make_wslots(4, 8192)
        wst["ck"] = None
        hnT = sb([128, KD, TT], BF16)
        b_hnT = nb("hnT")
        g3T = sb([128, KD], F32)
        b_g3 = nb("g3T")
        S.dma("sp", g3T, i_g3T, [], [b_g3], b_g3)
        a0 = AR.off
        actT = sb([128, KF, TT], BF16)
        b_act = nb("actT")
        a1 = AR.off
        AR.off = a0
        ot = sb([128, D], F32)
        xt = sb([128, D], F32)
        grep = sb([128, D], F32)
        xn = sb([128, 4, D], BF16)
        assert AR.off <= a1 or True
        AR.off = max(AR.off, a1)
        b_ot, b_xt, b_gr = nb("ot"), nb("xt"), nb("grep")
        b_ot2, b_xt2 = nb("ot2"), nb("xt2")
        b_xn = [nb("xn") for _ in range(4)]
        nd = dict(xn=xn, b_xn=b_xn, hnT=hnT, b_hnT=b_hnT, gT=g3T, b_g=b_g3)
        sgr = Rot([(sb([128, TT], F32), nb("sg")) for _ in range(2)])
        w_fg3, w_fu3, w_fd3 = wview(w_fg), wview(w_fu), wview(w_fd)
        for ti in range(NTO):
            tok0 = ti * TT
            S.dma("sp", grep, i_g2rep, [], [b_gr], b_gr)
            hnf = hnT.rearrange("p k t -> p (k t)").bitcast(F32)
            alt0 = [(ot, b_ot, xt, b_xt), (hnf[:, 0:D], b_ot2, hnf[:, D:2 * D], b_xt2)]
            for sub in range(4):
                rows = slice(tok0 + sub * 128, tok0 + (sub + 1) * 128)
                o_, bo_, x_, bx_ = alt0[sub % 2]
                S.dma("sp", o_, s_otok[rows, :], [db["otok"]], [bo_], bo_)
                S.dma("sp", x_, x_own[rows, :], [], [bx_], bx_)
                rs = rstd_of(nd, o_, bo_, xn[:, sub, :], b_xn[sub])
                S.op("dve", lambda v: v.scalar_tensor_tensor(out=o_, in0=o_, scalar=rs, in1=grep, op0=ALU.mult,
                                                             op1=ALU.mult), [b_stat, b_gr], [bo_])
                S.op("dve", lambda v: v.tensor_tensor(out=x_, in0=x_, in1=o_, op=ALU.add), [bo_], [bx_])
                S.dma("sp", s_h1[rows, :], x_, [bx_], [db["h1"]], bx_)
                rs2 = rstd_of(nd, x_, bx_, xn[:, sub, :], b_xn[sub])
                S.op("dve", lambda v: v.tensor_scalar(out=xn[:, sub, :], in0=x_, scalar1=rs2, scalar2=None,
                                                      op0=ALU.mult), [bx_, b_stat], [b_xn[sub]])
            S.barrier()
            transposes_to_hnT(nd)
            S.barrier()
            for cb in range(0, c.DFF, 256):
                cw = min(256, c.DFF - cb)
                wg, wgb = load_w(w_fg3, 0, KD, cb, cw)
                wu, wub = load_w(w_fu3, 0, KD, cb, cw)
                for m0 in range(0, cw, 128):
                    fc = (cb + m0) // 128
                    bg, bgb = mm_rot.next()
                    bu, bub = mm_rot.next()
                    S.group([lambda t, k=k, m0=m0, bg=bg, wg=wg: t.matmul(bg[:, 0:TT], wg[:, k, m0:m0 + 128], hnT[:, k, :],
                                                                          start=(k == 0), stop=(k == KD - 1))
                             for k in range(KD)], [wgb, b_hnT], [bgb])
                    S.group([lambda t, k=k, m0=m0, bu=bu, wu=wu: t.matmul(bu[:, 0:TT], wu[:, k, m0:m0 + 128], hnT[:, k, :],
                                                                          start=(k == 0), stop=(k == KD - 1))
                             for k in range(KD)], [wub, b_hnT], [bub])
                    sg, b_sg = sgr.next()
                    S.op("act", lambda a, sg=sg, bg=bg: a.activation(out=sg, in_=bg[:, 0:TT], func=AF.Silu),
                         [bgb], [b_sg])
                    S.op("dve", lambda v, sg=sg, bu=bu, fc=fc: v.tensor_tensor(out=actT[:, fc, :], in0=sg,
                                                                               in1=bu[:, 0:TT], op=ALU.mult),
                         [b_sg, bub], [b_act])
            wst["ck"] = None
            projT(w_fd3, 0, D, KF, actT, b_act,
                  lambda sub, cb, cw: s_ftok[tok0 + sub * 128:tok0 + (sub + 1) * 128, cb:cb + cw], db["ftok"],
                  f32out=True, kchunk=16, wide=True)
            S.barrier()
            S.dma("sp", grep, i_g4rep, [], [b_gr], b_gr)
            xnf = xn.rearrange("p a d -> p (a d)").bitcast(F32)
            alt = [(ot, b_ot, xt, b_xt), (xnf[:, 0:D], b_ot2, xnf[:, D:2 * D], b_xt2)]
            junk3 = hnT.rearrange("p k t -> p (k t)")[:, 0:D]
            for sub in range(4):
                rows = slice(tok0 + sub * 128, tok0 + (sub + 1) * 128)
                o_, bo_, x_, bx_ = alt[sub % 2]
                S.dma("sp", o_, s_ftok[rows, :], [db["ftok"]], [bo_], bo_)
                S.dma("sp", x_, s_h1[rows, :], [db["h1"]], [bx_], bx_)
                rs = rstd_of(nd, o_, bo_, junk3, b_hnT)
                S.op("dve", lambda v: v.scalar_tensor_tensor(out=o_, in0=o_, scalar=rs, in1=grep, op0=ALU.mult,
                                                             op1=ALU.mult), [b_stat, b_gr], [bo_])
                S.op("dve", lambda v: v.tensor_tensor(out=x_, in0=x_, in1=o_, op=ALU.add), [bo_], [bx_])
                S.dma("sp", y_out[rows, :], x_, [bx_], [], bx_)
            S.barrier()

    if "D" in phases:
        phaseD()
    S.emit()
    return nc, es


def _masks(c, s):
    SH, NB, NC_, NCT, NKT = c.SH, c.NB, c.NC, c.NCT, c.NKT
    SHb, SHc = SH // 64, SH // 16
    i = np.arange(SH)
    tg = s * SH + i
    n = np.arange(NCT * 128)
    if s == 1:
        ng = n.copy()
        nvalid = n < NC_
    else:
        ng = n - SHc
        nvalid = (n >= SHc) & (n < NC_)
    valid = nvalid[:, None] & ((16 * ng[:, None] + 31) <= tg[None, :])
    cmpbias = np.where(valid, 0.0, NEG).astype(np.float32)
    j = np.arange(NB)
    if s == 1:
        jg = j.copy()
        jvalid = np.ones(NB, bool)
    else:
        jg = j - SHb
        jvalid = j >= SHb
    ov = (16 * ng[:, None] < 64 * (jg[None, :] + 1)) & (16 * ng[:, None] + 32 > 64 * jg[None, :])
    ovl = (ov & nvalid[:, None] & jvalid[None, :]).astype(np.float32)
    cur = tg // 64
    allowed = jvalid[None, :] & (jg[None, :] * 64 <= tg[:, None])
    forced = allowed & ((jg[None, :] == 0) | (jg[None, :] == cur[:, None]) | (jg[None, :] == cur[:, None] - 1))
    selA = (allowed & ~forced).astype(np.float32)
    selB = np.where(forced, 1e9, np.where(allowed, 0.0, -1e30)).astype(np.float32)
    selM = allowed.astype(np.float32)
    m = np.arange(NKT * 128)
    expand = (j[:, None] == (m[None, :] // 64)).astype(np.float32)
    sl = np.arange(128)
    tl = np.arange(512)
    caus = np.concatenate([np.where((128 * v + sl[:, None]) <= tl[None, :], 0.0, NEG) for v in range(4)], 0)
    t1 = np.arange(128)
    wlo = np.where(sl[:, None] > t1[None, :], 0.0, NEG)
    whi = np.where(sl[:, None] <= t1[None, :], 0.0, NEG)
    wctx = np.full((128, 128), 0.0 if s == 1 else NEG)
    f = lambda a: np.ascontiguousarray(a, dtype=np.float32)
    return dict(cmpbias=f(cmpbias), ovl=f(ovl), selA=f(selA), selB=f(selB), selM=f(selM), expand=f(expand),
                caus=f(caus), wlo=f(wlo), whi=f(whi), wctx=f(wctx), ident=f(np.eye(128)))


def _shared(c, inp):
    f = lambda a: np.ascontiguousarray(a, dtype=np.float32)
    KD, NG, KS, D = c.KD, c.NG, c.KS, c.D
    m = {}
    m["w_in"] = f(inp["w_in"][0])
    m["w_glu"] = f(inp["ssm_w_glu"][0])
    m["w_pa"] = f(inp["w_proj_a"][0])
    m["w_pb"] = f(inp["w_proj_b"][0])
    m["w_out"] = f(inp["w_out"][0])
    m["w_fg"] = f(inp["w_ffn_gate"][0])
    m["w_fu"] = f(inp["w_ffn_up"][0])
    m["w_fd"] = f(inp["w_ffn_down"][0])
    m["g1T"] = f(np.asarray(inp["norm_mix_pre"][0]).reshape(KD, 128).T)
    m["g3T"] = f(np.asarray(inp["norm_ffn_pre"][0]).reshape(KD, 128).T)
    m["g2rep"] = f(np.broadcast_to(np.asarray(inp["norm_mix_post"][0])[None, :], (128, D)))
    m["g4rep"] = f(np.broadcast_to(np.asarray(inp["norm_ffn_post"][0])[None, :], (128, D)))
    are, aim, ldt = np.asarray(inp["ssm_a_re"][0]), np.asarray(inp["ssm_a_im"][0]), np.asarray(inp["ssm_log_dt"][0])
    m["are_pg"] = f(np.concatenate([are.T, are.T], 0))
    m["aim_pg"] = f(np.concatenate([aim.T, aim.T], 0))
    m["ldt_pg"] = f(np.broadcast_to(ldt[None, :], (128, NG)))
    m["are_gp"] = f(are)
    m["aim_gp"] = f(aim)
    m["ldt_gp"] = f(np.broadcast_to(ldt[:, None], (NG, 64)))
    m["bre_cg"] = f(np.asarray(inp["ssm_b_re"][0]).transpose(2, 0, 1).reshape(16, NG * 64))
    m["bim_cg"] = f(np.asarray(inp["ssm_b_im"][0]).transpose(2, 0, 1).reshape(16, NG * 64))
    creT = np.asarray(inp["ssm_c_re"][0]).transpose(2, 0, 1).reshape(64, NG * 16)
    cimT = np.asarray(inp["ssm_c_im"][0]).transpose(2, 0, 1).reshape(64, NG * 16)
    m["cc1"] = f(np.concatenate([creT, cimT], 0))
    m["cc2"] = f(np.concatenate([cimT, creT], 0))
    m["dskip"] = f(np.asarray(inp["ssm_d"][0]).reshape(NG, 16).T)
    m["bgluT"] = f(np.asarray(inp["ssm_b_glu"][0]).reshape(KS, 128).T)
    m["w1k"] = f(inp["cmp_w1_k"][0])
    m["w1v"] = f(inp["cmp_w1_v"][0])
    m["w2k"] = f(inp["cmp_w2_k"][0])
    m["w2v"] = f(inp["cmp_w2_v"][0])
    m["pekT"] = f(np.asarray(inp["cmp_pe_k"][0]).T)
    m["pevT"] = f(np.asarray(inp["cmp_pe_v"][0]).T)
    return m


def make_in_maps(c, inp):
    shared = _shared(c, inp)
    masks = [_masks(c, 0), _masks(c, 1)]
    x = np.asarray(inp["x"], dtype=np.float32)
    maps = []
    for b in range(c.B):
        for s in range(2):
            m = dict(shared)
            m.update(masks[s])
            m["x_own"] = np.ascontiguousarray(x[b, s * c.SH:(s + 1) * c.SH])
            m["x_ctx"] = np.ascontiguousarray(x[b, (1 - s) * c.SH:(2 - s) * c.SH])
            m["flag"] = np.full((128, 1), float(s), np.float32)
            maps.append(m)
    return maps


_CACHE = {}


def kernel(**inputs):
    c = Cfg()
    if "nc" not in _CACHE:
        _CACHE["nc"] = build(c)
    nc, es = _CACHE["nc"]
    maps = make_in_maps(c, inputs)
    res = run_bass_kernel_spmd(nc, maps, core_ids=list(range(2 * c.B)))
    out = np.empty((c.B, c.S, c.D), np.float32)
    for b in range(c.B):
        for s in range(2):
            out[b, s * c.SH:(s + 1) * c.SH] = res.results[b * 2 + s]["y"]
    return out
```

```python
import math
import types
from contextlib import ExitStack

import numpy as np
import concourse.bass as bass
import concourse.mybir as mybir
from concourse.bass_utils import run_bass_kernel_spmd

F32 = mybir.dt.float32
BF16 = mybir.dt.bfloat16
AF = mybir.ActivationFunctionType
ALU = mybir.AluOpType
NEG = -30000.0
EPS = 1e-6


class Cfg:
    def __init__(self, B=4, S=4096, D=4096, SW=2048, NH=16, NKV=4, DFF=11008):
        self.B, self.S, self.D, self.SW, self.NH, self.NKV, self.DFF = B, S, D, SW, NH, NKV, DFF
        self.SH = S // 2
        self.KD = D // 128
        self.NG = SW // 16
        self.KS = SW // 128
        self.HPG = NH // NKV
        self.AW = NH * 128
        self.KA = self.AW // 128
        self.KVW = NKV * 128
        self.KF = DFF // 128
        self.NB = S // 64
        self.NC = (S - 32) // 16 + 1
        self.NCT = (self.NC + 127) // 128
        self.NKT = S // 128
        self.NQT = self.SH // 128
        self.NQC = self.SH // 512
        self.INW = SW + self.AW + 6 * self.KVW + 3 * NH + 2 * D
        self.NK = int(math.log2(S))
        assert (1 << self.NK) == S


def _freeze(fn):
    if fn.__closure__ is None:
        return fn
    cells = []
    for cl in fn.__closure__:
        try:
            cells.append(types.CellType(cl.cell_contents))
        except ValueError:
            cells.append(cl)
    return types.FunctionType(fn.__code__, fn.__globals__, fn.__name__, fn.__defaults__, tuple(cells))


class Tok:
    __slots__ = ("sem", "val")

    def __init__(self, sem, val):
        self.sem, self.val = sem, val


class Buf:
    def __init__(self, name):
        self.name = name
        self.w = None
        self.r = {}
        self.dsem = None
        self.dcnt = 0


class Sch:
    ENG = ("pe", "act", "dve", "pool", "sp")

    def __init__(self, nc, es):
        self.nc, self.es = nc, es
        self.q = {e: [] for e in self.ENG}
        self.sem = {e: es.enter_context(nc.semaphore("c_" + e)) for e in ("pe", "act", "dve", "pool")}
        self.cnt = {e: 0 for e in self.ENG}
        self.seen = {e: {} for e in self.ENG}
        self.nsem = 4
        self.dma_toks = []

    def _wait(self, e, tok):
        if tok is None:
            return
        if e == "pe" and tok.sem is self.sem["pe"]:
            return
        k = id(tok.sem)
        if self.seen[e].get(k, 0) >= tok.val:
            return
        self.seen[e][k] = tok.val
        self.q[e].append(("w", tok.sem, tok.val))

    def _deps(self, e, reads, writes):
        for b in reads:
            self._wait(e, b.w)
        for b in writes:
            self._wait(e, b.w)
            for t in list(b.r.values()):
                self._wait(e, t)

    def _mark(self, tok, reads, writes):
        for b in reads:
            b.r[id(tok.sem)] = tok
        for b in writes:
            b.w = tok
            b.r = {}

    def op(self, e, fn, reads=(), writes=()):
        self._deps(e, reads, writes)
        self.cnt[e] += 1
        tok = Tok(self.sem[e], self.cnt[e])
        self.q[e].append(("o", _freeze(fn)))
        self._mark(tok, reads, writes)
        return tok

    def group(self, fns, reads=(), writes=()):
        self._deps("pe", reads, writes)
        for f in fns[:-1]:
            self.q["pe"].append(("n", _freeze(f)))
        self.cnt["pe"] += 1
        tok = Tok(self.sem["pe"], self.cnt["pe"])
        self.q["pe"].append(("o", _freeze(fns[-1])))
        self._mark(tok, reads, writes)
        return tok

    def dma(self, e, out, in_, reads, writes, slot, **kw):
        self._deps(e, reads, writes)
        if slot.dsem is None:
            slot.dsem = self.es.enter_context(self.nc.semaphore("d_" + slot.name))
            self.nsem += 1
        slot.dcnt += 16
        tok = Tok(slot.dsem, slot.dcnt)
        self.q[e].append(("d", out, in_, slot.dsem, kw))
        self._mark(tok, reads, writes)
        self.dma_toks.append(tok)
        return tok

    def barrier(self):
        last = {}
        for t in self.dma_toks:
            last[id(t.sem)] = t
        toks = list(last.values()) + [Tok(self.sem[x], self.cnt[x]) for x in ("pe", "act", "dve", "pool") if self.cnt[x]]
        for e in self.ENG:
            for t in toks:
                if e in self.sem and t.sem is self.sem[e]:
                    if e != "pe":
                        self._wait(e, t)
                    continue
                self._wait(e, t)
        self.dma_toks = list(last.values())

    def emit(self):
        nc = self.nc
        last = {}
        for t in self.dma_toks:
            last[id(t.sem)] = t
        for t in last.values():
            self._wait("sp", t)

        def replay(eng, items, mysem):
            for it in items:
                if it[0] == "w":
                    eng.wait_ge(it[1], it[2])
                elif it[0] == "o":
                    it[1](eng).then_inc(mysem, 1)
                elif it[0] == "n":
                    it[1](eng)
                else:
                    eng.dma_start(out=it[1], in_=it[2], **it[4]).then_inc(it[3], 16)

        with nc.Block() as block:
            @block.tensor
            def _(t):
                replay(t, self.q["pe"], self.sem["pe"])

            @block.scalar
            def _(a):
                replay(a, self.q["act"], self.sem["act"])

            @block.vector
            def _(v):
                replay(v, self.q["dve"], self.sem["dve"])

            @block.gpsimd
            def _(g):
                replay(g, self.q["pool"], self.sem["pool"])

            @block.sync
            def _(s):
                replay(s, self.q["sp"], None)


class Rot:
    def __init__(self, items):
        self.items = items
        self.i = 0

    def next(self):
        it = self.items[self.i % len(self.items)]
        self.i += 1
        return it


class Arena:
    def __init__(self, t, nelem):
        self.t, self.n, self.off = t, nelem, 0

    def alloc(self, shape, dt):
        p = shape[0]
        n = int(np.prod(shape[1:]))
        ne = n * (2 if dt == F32 else 1)
        self.off = (self.off + 1) // 2 * 2
        assert self.off + ne <= self.n, ("SBUF arena overflow", self.off, ne, self.n)
        ap = self.t[0:p, self.off:self.off + ne]
        self.off += ne
        if dt == F32:
            ap = ap.bitcast(F32)
        if len(shape) == 3:
            ap = ap.rearrange("p (a b) -> p a b", b=shape[2])
        elif len(shape) == 4:
            ap = ap.rearrange("p (a b c) -> p a b c", b=shape[2], c=shape[3])
        return ap

def build(cfg, debug_outs=(), phases="ABNCD"):
    c = cfg
    nc = bass.Bass("TRN2", target_bir_lowering=False)
    es = ExitStack()
    S = Sch(nc, es)
    D, SH, KD, SW, NG, KS, AW, KA, KVW, KF, NB, NC_, NCT, NKT, NQT, NQC, NH, NKV, HPG = (
        c.D, c.SH, c.KD, c.SW, c.NG, c.KS, c.AW, c.KA, c.KVW, c.KF, c.NB, c.NC, c.NCT, c.NKT, c.NQT,
        c.NQC, c.NH, c.NKV, c.HPG)
    SEQ = c.S
    TT = 512
    NTO = SH // TT

    def din(name, shape, dt=F32):
        return nc.dram_tensor(name, list(shape), dt, kind="ExternalInput").ap()

    def dscr(name, shape, dt):
        kind = "ExternalOutput" if name in debug_outs else "Internal"
        return nc.dram_tensor(name, list(shape), dt, kind=kind).ap()

    x_own = din("x_own", [SH, D])
    x_ctx = din("x_ctx", [SH, D])
    w_in = din("w_in", [D, c.INW])
    w_glu = din("w_glu", [SW, SW])
    w_pa = din("w_pa", [SW, D])
    w_pb = din("w_pb", [AW, D])
    w_out = din("w_out", [D, D])
    w_fg = din("w_fg", [D, c.DFF])
    w_fu = din("w_fu", [D, c.DFF])
    w_fd = din("w_fd", [c.DFF, D])
    i_g1T = din("g1T", [128, KD])
    i_g3T = din("g3T", [128, KD])
    i_g2rep = din("g2rep", [128, D])
    i_g4rep = din("g4rep", [128, D])
    i_are_pg = din("are_pg", [128, NG])
    i_aim_pg = din("aim_pg", [128, NG])
    i_ldt_pg = din("ldt_pg", [128, NG])
    i_are_gp = din("are_gp", [NG, 64])
    i_aim_gp = din("aim_gp", [NG, 64])
    i_ldt_gp = din("ldt_gp", [NG, 64])
    i_bre = din("bre_cg", [16, NG * 64])
    i_bim = din("bim_cg", [16, NG * 64])
    i_cc1 = din("cc1", [128, NG * 16])
    i_cc2 = din("cc2", [128, NG * 16])
    i_dsk = din("dskip", [16, NG])
    i_bglu = din("bgluT", [128, KS])
    i_flag = din("flag", [128, 1])
    i_w1k = din("w1k", [32 * 128, 128])
    i_w1v = din("w1v", [32 * 128, 128])
    i_w2k = din("w2k", [128, 128])
    i_w2v = din("w2v", [128, 128])
    i_pek = din("pekT", [128, 32])
    i_pev = din("pevT", [128, 32])
    i_cmpb = din("cmpbias", [NCT * 128, SH])
    i_ovl = din("ovl", [NCT * 128, NB])
    i_selA = din("selA", [SH, NB])
    i_selB = din("selB", [SH, NB])
    i_selM = din("selM", [SH, NB])
    i_exp = din("expand", [NB, NKT * 128])
    i_caus = din("caus", [4 * 128, 512])
    i_wlo = din("wlo", [128, 128])
    i_whi = din("whi", [128, 128])
    i_wctx = din("wctx", [128, 128])
    i_ident = din("ident", [128, 128])
    y_out = nc.dram_tensor("y", [SH, D], F32, kind="ExternalOutput").ap()

    s_uT = dscr("s_uT", [SW, SEQ], BF16)
    s_kT = dscr("s_kT", [3, KVW, SEQ], BF16)
    s_vcT = dscr("s_vcT", [KVW, SEQ], BF16)
    s_vt = dscr("s_vt", [2, SEQ, KVW], BF16)
    s_qT = dscr("s_qT", [AW, SH], BF16)
    s_gn = dscr("s_gn", [SH, 3 * NH], F32)
    s_gaT = dscr("s_gaT", [D, SH], BF16)
    s_gbT = dscr("s_gbT", [D, SH], BF16)
    s_bbar = dscr("s_bbar", [NG, SEQ // TT, 16, 256], BF16)
    s_zs = dscr("s_zs", [SEQ // TT, 2, NG * 64], F32)
    s_tab = dscr("s_tab", [128, NG, 64 + 2 * (TT // 32)], F32)
    s_yaT = dscr("s_yaT", [SW, SH], F32)
    s_ybT = dscr("s_ybT", [AW, SH], BF16)
    s_otok = dscr("s_otok", [SH, D], F32)
    s_h1 = dscr("s_h1", [SH, D], F32)
    s_ftok = dscr("s_ftok", [SH, D], F32)
    db = {n: Buf(n) for n in ("uT", "kT", "vcT", "vt", "qT", "gn", "gaT", "gbT", "bbar", "zs", "yaT", "ybT",
                              "otok", "h1", "ftok", "tab")}

    ARENA_N = 104000
    arena_t = es.enter_context(nc.sbuf_tensor("arena", [128, ARENA_N], BF16))
    AR = Arena(arena_t, ARENA_N)

    def sb(shape, dt):
        return AR.alloc(list(shape), dt)

    bufn = [0]

    def nb(prefix="b"):
        bufn[0] += 1
        return Buf("%s%d" % (prefix, bufn[0]))

    banks = []
    for i in range(8):
        t = es.enter_context(nc.psum_tensor("bank%d" % i, [128, 512], F32))
        banks.append((t, Buf("bank%d" % i)))

    ident = sb([128, 128], BF16)
    b_ident = nb("ident")
    S.dma("pool", ident, i_ident, [], [b_ident], b_ident)
    stg_bf = Rot([(sb([128, 512], BF16), nb("stgb")) for i in range(4)])
    stg_f = Rot([(sb([128, 512], F32), nb("stgf")) for i in range(3)])
    wst = {"rot": None, "sz": 0}

    def make_wslots(n, size):
        wst["rot"] = Rot([(sb([128, size], BF16), nb("wslot")) for i in range(n)])
        wst["sz"] = size
    stat = sb([128, 8], F32)
    b_stat = nb("stat")
    persist_mark = AR.off
    evac_i = [0]

    def evac_eng():
        evac_i[0] += 1
        return "act" if evac_i[0] % 2 else "dve"

    def evac(eng, out, in_, reads, writes, func=None, scale=1.0):
        if func is not None or eng == "act":
            f = func if func is not None else AF.Copy
            return S.op("act", lambda a: a.activation(out=out, in_=in_, func=f, scale=float(scale)), reads, writes)
        if scale != 1.0:
            return S.op("dve", lambda v: v.tensor_scalar(out=out, in0=in_, scalar1=float(scale), scalar2=None,
                                                         op0=ALU.mult), reads, writes)
        return S.op("dve", lambda v: v.tensor_copy(out=out, in_=in_), reads, writes)

    def wview(w_ap):
        return w_ap.rearrange("(kc p) n -> p kc n", p=128)

    wcache = {}

    def load_w(w3, kc0, nk, c0, cw, ck=None, ntiles=0):
        if ck is None and wst.get("ck"):
            ck, ntiles = wst["ck"]
        wt, wb = wst["rot"].next()
        sz = wst["sz"]
        assert nk * cw <= sz, (nk, cw, sz)
        flat = wt[:, 0:nk * cw]
        dst = flat.rearrange("p (k n) -> p k n", n=cw)
        if ck is None:
            S.dma("pool", dst, w3[:, kc0:kc0 + nk, c0:c0 + cw], [], [wb], wb)
            return dst, wb
        if ck not in wcache:
            wcache[ck] = dict(ap=nc.dram_tensor("wc_" + ck, [ntiles, 128, sz], BF16, kind="Internal").ap(),
                              buf=Buf("wc_" + ck), idx={})
        ent = wcache[ck]
        key = (kc0, nk, c0, cw)
        if key not in ent["idx"]:
            i = len(ent["idx"])
            assert i < ntiles, (ck, i, ntiles)
            ent["idx"][key] = i
            S.dma("pool", dst, w3[:, kc0:kc0 + nk, c0:c0 + cw], [], [wb], wb)
            flush_spill()
            wst["pend"] = (ent["ap"][i, :, 0:nk * cw], flat, wb, ent["buf"])
        else:
            i = ent["idx"][key]
            S.dma("pool", flat, ent["ap"][i, :, 0:nk * cw], [ent["buf"]], [wb], wb)
            flush_spill()
        return dst, wb

    def flush_spill():
        p = wst.get("pend")
        if p is not None:
            wst["pend"] = None
            S.dma("pool", p[0], p[1], [p[2]], [p[3]], p[2])

    _orig_barrier = S.barrier

    def _barrier():
        flush_spill()
        _orig_barrier()

    S.barrier = _barrier

    mm_rot = Rot(banks[0:4])
    mm8_rot = Rot(banks[0:8])

    def projF(w3, c0, width, nk, act_tile, b_act, dst_fn, dst_buf, func=None, scale=1.0, post=None, ck=None, nt=0):
        for cb in range(0, width, 512):
            cw = min(512, width - cb)
            wt, wb = load_w(w3, 0, nk, c0 + cb, cw, ck, nt)
            for m0 in range(0, cw, 128):
                mw = min(128, cw - m0)
                bk, bkb = mm_rot.next()
                fns = [lambda t, k=k, m0=m0, mw=mw, bk=bk, wt=wt: t.matmul(
                    bk[0:mw, 0:TT], wt[:, k, m0:m0 + mw], act_tile[:, k, :], start=(k == 0), stop=(k == nk - 1))
                    for k in range(nk)]
                S.group(fns, [wb, b_act], [bkb])
                if post is not None:
                    post(cb + m0, mw, bk, bkb)
                    continue
                st, stb = stg_bf.next()
                evac(evac_eng(), st[0:mw, 0:TT], bk[0:mw, 0:TT], [bkb], [stb], func=func, scale=scale)
                S.dma("sp", dst_fn(cb + m0, mw), st[0:mw, 0:TT], [stb], [dst_buf], stb)

    def projT(w3, c0, width, nk, act_tile, b_act, dst_fn, dst_buf, func=None, f32out=False, kchunk=None, ck=None, nt=0,
              wide=False):
        kch = kchunk or nk
        for cb in range(0, width, 512):
            cw = min(512, width - cb)
            bks = [(mm8_rot if wide else mm_rot).next() for _ in range(4)]
            for k0 in range(0, nk, kch):
                kn = min(kch, nk - k0)
                wt, wb = load_w(w3, k0, kn, c0 + cb, cw, ck, nt)
                for sub in range(4):
                    bk, bkb = bks[sub]
                    fns = [lambda t, k=k, k0=k0, sub=sub, bk=bk, wt=wt, cw=cw: t.matmul(
                        bk[:, 0:cw], act_tile[:, k0 + k, sub * 128:(sub + 1) * 128], wt[:, k, 0:cw],
                        start=(k0 + k == 0), stop=(k0 + k == nk - 1)) for k in range(kn)]
                    S.group(fns, [wb, b_act], [bkb])
            for sub in range(4):
                bk, bkb = bks[sub]
                st, stb = (stg_f if f32out else stg_bf).next()
                evac(evac_eng(), st[:, 0:cw], bk[:, 0:cw], [bkb], [stb], func=func)
                S.dma("sp", dst_fn(sub, cb, cw), st[:, 0:cw], [stb], [dst_buf], stb)

    def alloc_norm():
        d = {}
        d["xrot"] = Rot([(sb([128, D], F32), nb("xin")) for i in range(2)])
        d["xn"] = sb([128, 4, D], BF16)
        d["b_xn"] = [nb("xn") for i in range(4)]
        d["hnT"] = sb([128, KD, TT], BF16)
        d["b_hnT"] = nb("hnT")
        d["gT"] = sb([128, KD], F32)
        d["b_g"] = nb("gT")
        return d

    def rstd_of(nd, src_ap, bsrc, junk, b_junk, col=0):
        S.op("act", lambda a: a.activation(out=junk, in_=src_ap, func=AF.Square,
                                           accum_out=stat[:, col:col + 1]), [bsrc], [b_junk, b_stat])
        S.op("dve", lambda v: v.tensor_scalar(out=stat[:, col + 1:col + 2], in0=stat[:, col:col + 1],
                                              scalar1=1.0 / D, scalar2=EPS, op0=ALU.mult, op1=ALU.add),
             [b_stat], [b_stat])
        S.op("act", lambda a: a.activation(out=stat[:, col + 1:col + 2], in_=stat[:, col + 1:col + 2], func=AF.Sqrt),
             [b_stat], [b_stat])
        S.op("dve", lambda v: v.reciprocal(out=stat[:, col + 2:col + 3], in_=stat[:, col + 1:col + 2]),
             [b_stat], [b_stat])
        return stat[:, col + 2:col + 3]

    def transposes_to_hnT(nd):
        xn, hnT, gT = nd["xn"], nd["hnT"], nd["gT"]
        for kc in range(KD):
            bk, bkb = banks[6 + kc % 2]
            pst = bk[:].bitcast(BF16)
            for sub in range(4):
                S.group([lambda t, sub=sub, kc=kc, pst=pst: t.transpose(
                    out=pst[:, sub * 128:(sub + 1) * 128], in_=xn[:, sub, kc * 128:(kc + 1) * 128], identity=ident)],
                    nd["b_xn"] + [b_ident], [bkb])
            S.op("dve", lambda v, kc=kc, pst=pst: v.tensor_scalar(
                out=hnT[:, kc, :], in0=pst[:, 0:TT], scalar1=gT[:, kc:kc + 1], scalar2=None, op0=ALU.mult),
                 [bkb, nd["b_g"]], [nd["b_hnT"]])

    def phaseA():
        make_wslots(3, 16384 if KD * 512 <= 16384 else KD * 512)
        wst["ck"] = None
        nd = alloc_norm()
        S.dma("sp", nd["gT"], i_g1T, [], [nd["b_g"]], nd["b_g"])
        hnT, b_hnT = nd["hnT"], nd["b_hnT"]
        w_in3 = wview(w_in)
        HS = 128 ** -0.5
        o = 0
        segs = {}
        for nm, wd in (("u", SW), ("q", AW), ("kc", KVW), ("vc", KVW), ("ks", KVW), ("vs", KVW), ("kw", KVW),
                       ("vw", KVW), ("gn", 3 * NH), ("ga", D), ("gb", D)):
            segs[nm] = (o, wd)
            o += wd

        def tile(xsrc, ti, own):
            tok0 = ti * TT
            apos = (SH if own else 0) + tok0
            for sub in range(4):
                xt, xb = nd["xrot"].next()
                S.dma("sp", xt, xsrc[tok0 + sub * 128:tok0 + (sub + 1) * 128, :], [], [xb], xb)
                rs = rstd_of(nd, xt, xb, nd["xn"][:, sub, :], nd["b_xn"][sub])
                S.op("dve", lambda v, xt=xt, sub=sub, rs=rs: v.tensor_scalar(
                    out=nd["xn"][:, sub, :], in0=xt, scalar1=rs, scalar2=None, op0=ALU.mult),
                     [xb, b_stat], [nd["b_xn"][sub]])
            transposes_to_hnT(nd)
            win = ["kw", "vw"] if (own or ti == NTO - 1) else []
            names = ["u", "kc", "vc", "ks", "vs"] + win + (["q", "gn", "ga", "gb"] if own else [])
            for nm in names:
                c0, wd = segs[nm]
                if nm == "u":
                    projF(w_in3, c0, wd, KD, hnT, b_hnT, lambda r, n: s_uT[r:r + n, apos:apos + TT], db["uT"])
                elif nm in ("kc", "ks", "kw"):
                    ki = ("kc", "ks", "kw").index(nm)
                    projF(w_in3, c0, wd, KD, hnT, b_hnT, lambda r, n, ki=ki: s_kT[ki, r:r + n, apos:apos + TT],
                          db["kT"])
                elif nm == "vc":
                    projF(w_in3, c0, wd, KD, hnT, b_hnT, lambda r, n: s_vcT[r:r + n, apos:apos + TT], db["vcT"])
                elif nm in ("vs", "vw"):
                    vi = ("vs", "vw").index(nm)
                    projT(w_in3, c0, wd, KD, hnT, b_hnT,
                          lambda sub, cb, cw, vi=vi: s_vt[vi, apos + sub * 128:apos + (sub + 1) * 128, cb:cb + cw],
                          db["vt"])
                elif nm == "q":
                    projF(w_in3, c0, wd, KD, hnT, b_hnT, lambda r, n: s_qT[r:r + n, tok0:tok0 + TT], db["qT"],
                          scale=HS)
                elif nm == "gn":
                    projT(w_in3, c0, wd, KD, hnT, b_hnT,
                          lambda sub, cb, cw: s_gn[tok0 + sub * 128:tok0 + (sub + 1) * 128, cb:cb + cw], db["gn"],
                          func=AF.Sigmoid, f32out=True)
                elif nm == "ga":
                    projF(w_in3, c0, wd, KD, hnT, b_hnT, lambda r, n: s_gaT[r:r + n, tok0:tok0 + TT], db["gaT"],
                          func=AF.Sigmoid)
                elif nm == "gb":
                    projF(w_in3, c0, wd, KD, hnT, b_hnT, lambda r, n: s_gbT[r:r + n, tok0:tok0 + TT], db["gbT"],
                          func=AF.Sigmoid)

        for ti in range(NTO):
            tile(x_ctx, ti, False)
        for ti in range(NTO):
            tile(x_own, ti, True)

    if "A" in phases:
        phaseA()
    S.barrier()
    AR.off = persist_mark

    def phaseB():
        NK = c.NK

        def tt(eng, out, a, b, op, reads, writes):
            return S.op(eng, lambda v: v.tensor_tensor(out=out, in0=a, in1=b, op=op), reads, writes)

        def ts(eng, out, a, s1, s2, op0, op1, reads, writes):
            if op1 is None:
                return S.op(eng, lambda v: v.tensor_scalar(out=out, in0=a, scalar1=s1, scalar2=None, op0=op0),
                            reads, writes)
            return S.op(eng, lambda v: v.tensor_scalar(out=out, in0=a, scalar1=s1, scalar2=s2, op0=op0, op1=op1),
                        reads, writes)

        def consts(P, Fd, i_ar, i_ai, i_ld, npow):
            bb = nb("s5c")
            B = [bb]
            T = lambda: sb([P, Fd], F32)
            ar, ai, ld = T(), T(), T()
            S.dma("sp", ar, i_ar, [], B, bb)
            S.dma("sp", ai, i_ai, [], B, bb)
            S.dma("sp", ld, i_ld, [], B, bb)
            dt_, lam, th, dec, x2, ps_, pc_, s_, c_, t1, t2 = (T() for _ in range(11))
            S.op("act", lambda a: a.activation(out=dt_, in_=ld, func=AF.Exp), B, B)
            tt("dve", lam, dt_, ar, ALU.mult, B, B)
            tt("dve", th, dt_, ai, ALU.mult, B, B)
            S.op("act", lambda a: a.activation(out=dec, in_=lam, func=AF.Exp), B, B)
            ts("dve", th, th, 1.0 / 64, None, ALU.mult, None, B, B)
            tt("dve", x2, th, th, ALU.mult, B, B)

            def horner(out, coeffs):
                ts("dve", out, x2, coeffs[0], coeffs[1], ALU.mult, ALU.add, B, B)
                for cf in coeffs[2:]:
                    tt("dve", out, out, x2, ALU.mult, B, B)
                    ts("dve", out, out, cf, None, ALU.add, None, B, B)

            horner(ps_, [1.0 / 362880, -1.0 / 5040, 1.0 / 120, -1.0 / 6, 1.0])
            tt("dve", s_, ps_, th, ALU.mult, B, B)
            horner(c_, [1.0 / 40320, -1.0 / 720, 1.0 / 24, -0.5, 1.0])

            def dbl(co, so, ci, si):
                tt("dve", t1, si, si, ALU.mult, B, B)
                tt("dve", t2, ci, ci, ALU.mult, B, B)
                S.op("dve", lambda v: v.scalar_tensor_tensor(out=so, in0=si, scalar=2.0, in1=ci, op0=ALU.mult,
                                                             op1=ALU.mult), B, B)
                tt("dve", co, t2, t1, ALU.subtract, B, B)

            def renorm(cc, ss):
                tt("dve", t1, ss, ss, ALU.mult, B, B)
                tt("dve", t2, cc, cc, ALU.mult, B, B)
                tt("dve", t1, t1, t2, ALU.add, B, B)
                ts("dve", t1, t1, -0.5, 1.5, ALU.mult, ALU.add, B, B)
                tt("dve", cc, cc, t1, ALU.mult, B, B)
                tt("dve", ss, ss, t1, ALU.mult, B, B)

            c2, s2 = T(), T()
            cur = (c_, s_)
            oth = (c2, s2)
            for i in range(6):
                dbl(oth[0], oth[1], cur[0], cur[1])
                cur, oth = oth, cur
            renorm(cur[0], cur[1])
            wr, wi = [cur[0]], [cur[1]]
            for k in range(1, npow):
                a, b = T(), T()
                dbl(a, b, wr[-1], wi[-1])
                renorm(a, b)
                wr.append(a)
                wi.append(b)
            abr, abi, den, m, zr, zi = (T() for _ in range(6))
            tt("dve", abr, dec, wr[0], ALU.mult, B, B)
            tt("dve", abi, dec, wi[0], ALU.mult, B, B)
            tt("dve", t1, ar, ar, ALU.mult, B, B)
            tt("dve", t2, ai, ai, ALU.mult, B, B)
            tt("dve", den, t1, t2, ALU.add, B, B)
            S.op("dve", lambda v: v.reciprocal(out=den, in_=den), B, B)
            ts("dve", m, abr, -1.0, None, ALU.add, None, B, B)
            tt("dve", t1, m, ar, ALU.mult, B, B)
            tt("dve", t2, abi, ai, ALU.mult, B, B)
            tt("dve", t1, t1, t2, ALU.add, B, B)
            tt("dve", zr, t1, den, ALU.mult, B, B)
            tt("dve", t1, abi, ar, ALU.mult, B, B)
            tt("dve", t2, m, ai, ALU.mult, B, B)
            tt("dve", t1, t1, t2, ALU.subtract, B, B)
            tt("dve", zi, t1, den, ALU.mult, B, B)
            return dict(dec=dec, wr=wr, wi=wi, zr=zr, zi=zi, buf=bb)

        NT = SEQ // TT
        LT = int(math.log2(TT))
        mark0 = AR.off
        cg = consts(NG, 64, i_are_gp, i_aim_gp, i_ldt_gp, LT + 1)
        BG = [cg["buf"]]
        zr, zi, cT, sT = cg["zr"], cg["zi"], cg["wr"][LT], cg["wi"][LT]
        z2r, z2i, zt = sb([NG, 64], F32), sb([NG, 64], F32), sb([NG, 64], F32)
        cur, oth = (zr, zi), (z2r, z2i)
        for j in range(NT):
            S.dma("sp", s_zs[j, 0].rearrange("(g p) -> g p", p=64), cur[0], BG, [db["zs"]], db["zs"])
            S.dma("sp", s_zs[j, 1].rearrange("(g p) -> g p", p=64), cur[1], BG, [db["zs"]], db["zs"])
            if j == NT - 1:
                break
            tt("dve", zt, cur[1], sT, ALU.mult, BG, BG)
            tt("dve", oth[0], cur[0], cT, ALU.mult, BG, BG)
            tt("dve", oth[0], oth[0], zt, ALU.add, BG, BG)
            tt("dve", zt, cur[0], sT, ALU.mult, BG, BG)
            tt("dve", oth[1], cur[1], cT, ALU.mult, BG, BG)
            tt("dve", oth[1], oth[1], zt, ALU.subtract, BG, BG)
            cur, oth = oth, cur
        S.barrier()
        AR.off = mark0
        cp = consts(128, NG, i_are_pg, i_aim_pg, i_ldt_pg, LT + 1)
        b_cp = cp["buf"]
        BP = [b_cp]
        cT, sT = cp["wr"][LT], cp["wi"][LT]
        cosP = sb([128, NT, NG], F32)
        sinP = sb([128, NT, NG], F32)
        ztp = sb([128, NG], F32)
        S.op("dve", lambda v: v.memset(cosP[:, 0, :], 1.0), [], BP)
        S.op("dve", lambda v: v.memset(sinP[:, 0, :], 0.0), [], BP)
        for j in range(1, NT):
            tt("dve", ztp, sinP[:, j - 1, :], sT, ALU.mult, BP, BP)
            tt("dve", cosP[:, j, :], cosP[:, j - 1, :], cT, ALU.mult, BP, BP)
            tt("dve", cosP[:, j, :], cosP[:, j, :], ztp, ALU.subtract, BP, BP)
            tt("dve", ztp, cosP[:, j - 1, :], sT, ALU.mult, BP, BP)
            tt("dve", sinP[:, j, :], sinP[:, j - 1, :], cT, ALU.mult, BP, BP)
            tt("dve", sinP[:, j, :], sinP[:, j, :], ztp, ALU.add, BP, BP)
        NA_ = TT // 32
        mt = AR.off
        for (nm, nlen, k0) in (("B", 32, 0), ("A", NA_, 5)):
            Tc = sb([128, NG, nlen], F32)
            Ts = sb([128, NG, nlen], F32)
            q1 = sb([128, NG, nlen // 2], F32)
            q2 = sb([128, NG, nlen // 2], F32)
            b_T = nb("T2")
            BT = [b_T]
            S.op("dve", lambda v: v.memset(Tc[:, :, 0:1], 1.0), [], BT)
            S.op("dve", lambda v: v.memset(Ts[:, :, 0:1], 0.0), [], BT)
            for k in range(int(math.log2(nlen))):
                n_ = 1 << k
                wrb = cp["wr"][k0 + k].unsqueeze(2).to_broadcast([128, NG, n_])
                wib = cp["wi"][k0 + k].unsqueeze(2).to_broadcast([128, NG, n_])
                tt("dve", q1[:, :, 0:n_], Ts[:, :, 0:n_], wib, ALU.mult, BT + BP, BT)
                tt("dve", q2[:, :, 0:n_], Tc[:, :, 0:n_], wrb, ALU.mult, BT + BP, BT)
                tt("dve", Tc[:, :, n_:2 * n_], q2[:, :, 0:n_], q1[:, :, 0:n_], ALU.subtract, BT, BT)
                tt("dve", q1[:, :, 0:n_], Tc[:, :, 0:n_], wib, ALU.mult, BT + BP, BT)
                tt("dve", q2[:, :, 0:n_], Ts[:, :, 0:n_], wrb, ALU.mult, BT + BP, BT)
                tt("dve", Ts[:, :, n_:2 * n_], q1[:, :, 0:n_], q2[:, :, 0:n_], ALU.add, BT, BT)
            o_ = 0 if nm == "B" else 64
            S.dma("sp", s_tab[:, :, o_:o_ + nlen], Tc, BT, [db["tab"]], b_T)
            S.dma("sp", s_tab[:, :, o_ + nlen:o_ + 2 * nlen], Ts, BT, [db["tab"]], b_T)
        S.barrier()
        AR.off = mt
        W1 = sb([128, NT // 2, NG * 16], BF16)
        W2 = sb([128, NT // 2, NG * 16], BF16)
        b_W = nb("W12")
        sgn = sb([128, 1], F32)
        S.op("dve", lambda v: v.memset(sgn[0:64, :], 1.0), [], [b_W])
        S.op("dve", lambda v: v.memset(sgn[64:128, :], -1.0), [], [b_W])
        dsk = sb([16, NG], F32)
        flag = sb([128, 1], F32)
        diagd = sb([16, NG, 16], BF16)
        identf = sb([16, 16], F32)
        b_misc = nb("misc")
        S.dma("sp", dsk, i_dsk, [], [b_misc], b_misc)
        S.dma("sp", flag, i_flag, [], [b_misc], b_misc)
        S.op("dve", lambda v: v.tensor_copy(out=identf, in_=ident[0:16, 0:16]), [b_ident], [b_misc])
        S.op("dve", lambda v: v.tensor_tensor(out=diagd, in0=identf.unsqueeze(1).to_broadcast([16, NG, 16]),
                                              in1=dsk.unsqueeze(2).to_broadcast([16, NG, 16]), op=ALU.mult),
             [b_misc], [b_misc])
        mark1 = AR.off
        cc1 = sb([128, NG * 16], F32)
        cc2 = sb([128, NG * 16], F32)
        wa = sb([128, NG * 16], F32)
        wb_ = sb([128, NG * 16], F32)
        b_cc = nb("cc")
        BCC = [b_cc]
        S.dma("sp", cc1, i_cc1, [], BCC, b_cc)
        S.dma("sp", cc2, i_cc2, [], BCC, b_cc)
        v3w = lambda a: a.rearrange("p (g c) -> p g c", c=16)
        for j in range(NT // 2, NT):
            cb = cosP[:, j, :].unsqueeze(2).to_broadcast([128, NG, 16])
            sb_ = sinP[:, j, :].unsqueeze(2).to_broadcast([128, NG, 16])
            tt("dve", v3w(wa), v3w(cc1), cb, ALU.mult, BCC + BP, BCC)
            tt("dve", v3w(wb_), v3w(cc2), sb_, ALU.mult, BCC + BP, BCC)
            S.op("dve", lambda v, j=j: v.scalar_tensor_tensor(out=W1[:, j - NT // 2, :], in0=wa, scalar=sgn[:, 0:1], in1=wb_,
                                                              op0=ALU.mult, op1=ALU.subtract), BCC + [b_W], [b_W])
            tt("dve", v3w(wa), v3w(cc2), cb, ALU.mult, BCC + BP, BCC)
            tt("dve", v3w(wb_), v3w(cc1), sb_, ALU.mult, BCC + BP, BCC)
            S.op("dve", lambda v: v.scalar_tensor_tensor(out=wa, in0=wb_, scalar=sgn[:, 0:1], in1=wa,
                                                         op0=ALU.mult, op1=ALU.add), BCC + [b_W], BCC)
            S.op("dve", lambda v, j=j: v.tensor_scalar(out=W2[:, j - NT // 2, :], in0=wa, scalar1=-1.0, scalar2=None,
                                                       op0=ALU.mult), BCC, [b_W])
        S.barrier()
        AR.off = mark1
        GC = min(16, NG)
        NGC = NG // GC
        PB = NGC * 16
        n = GC * 64
        zrb, zib, bre, bim, t1, t2 = (sb([PB, n], F32) for _ in range(6))
        obs = [(sb([PB, GC, 256], BF16), nb("ob")) for _ in range(2)]
        bz, bbi = nb("bz"), nb("bbi")
        v3 = lambda a: a.rearrange("c (g p) -> c g p", p=64)
        for gc in range(NGC):
            ps_ = slice(gc * 16, (gc + 1) * 16)
            S.dma("sp", bre[ps_, :], i_bre[:, gc * n:(gc + 1) * n], [], [bbi], bbi)
            S.dma("sp", bim[ps_, :], i_bim[:, gc * n:(gc + 1) * n], [], [bbi], bbi)
        for j in range(NT):
            for gc in range(NGC):
                ps_ = slice(gc * 16, (gc + 1) * 16)
                S.dma("sp", zrb[ps_, :], s_zs[j, 0, gc * n:(gc + 1) * n].partition_broadcast(16), [db["zs"]], [bz], bz)
                S.dma("sp", zib[ps_, :], s_zs[j, 1, gc * n:(gc + 1) * n].partition_broadcast(16), [db["zs"]], [bz], bz)
            ob, b_ob = obs[j % 2]
            B = [bz]
            tt("dve", t1, zrb, bre, ALU.mult, B + [bbi], B)
            tt("dve", t2, zib, bim, ALU.mult, B + [bbi], B)
            tt("dve", t1, t1, t2, ALU.subtract, B, B)
            tt("dve", t2, zrb, bim, ALU.mult, B + [bbi], B)
            tt("dve", zrb, zib, bre, ALU.mult, B + [bbi], B)
            tt("dve", t2, t2, zrb, ALU.add, B, B)
            S.op("dve", lambda v: v.tensor_copy(out=ob[:, :, 0:64], in_=v3(t1)), B, [b_ob])
            S.op("dve", lambda v: v.tensor_copy(out=ob[:, :, 64:128], in_=v3(t2)), B, [b_ob])
            S.op("dve", lambda v: v.tensor_copy(out=ob[:, :, 128:192], in_=v3(t2)), B, [b_ob])
            S.op("dve", lambda v: v.tensor_scalar(out=ob[:, :, 192:256], in0=v3(t1), scalar1=-1.0, scalar2=None,
                                                  op0=ALU.mult), B, [b_ob])
            for gc in range(NGC):
                ps_ = slice(gc * 16, (gc + 1) * 16)
                S.dma("sp", s_bbar[gc * GC:(gc + 1) * GC, j].rearrange("g c f -> c g f"), ob[ps_], [b_ob],
                      [db["bbar"]], b_ob)
        S.barrier()
        AR.off = mark1

        NTAB = NUG = (4 if NT >= 5 else 5)
        tabs = [(sb([128, TT], F32), sb([128, TT], F32), nb("tab")) for _ in range(NTAB)]
        ugs = [(sb([16, SEQ], BF16), sb([16, NT, 256], BF16), nb("ug")) for _ in range(NUG)]
        tmpA = (sb([128, TT], F32), nb("tmpA"))
        tmpB = (sb([128, TT], F32), nb("tmpB"))
        tls = [(sb([128, 64 + 2 * NA_], F32), nb("tl")) for _ in range(NTAB)]
        t1s = [(sb([128, TT], F32), nb("t1s")) for _ in range(2)]
        t2s = [(sb([128, TT], F32), nb("t2s")) for _ in range(2)]
        bps = [(sb([128, TT], F32), nb("bp")) for _ in range(2)]
        gts = [(sb([128, TT], F32), nb("gt")) for _ in range(3)]
        X1s = [(sb([128, TT], BF16), nb("X1")) for _ in range(2)]
        X2s = [(sb([128, TT], BF16), nb("X2")) for _ in range(2)]
        ysts = [(sb([16, TT], F32), nb("yst")) for _ in range(3)]
        init = sb([128, 1], F32)
        b_init = nb("init")
        NTOT = NG * NT

        def load_group(g):
            ug, bbt, b_ug = ugs[g % NUG]
            S.dma("sp", ug, s_uT[g * 16:(g + 1) * 16, :], [db["uT"]], [b_ug], b_ug)
            S.dma("sp", bbt, s_bbar[g].rearrange("j c f -> c j f"), [db["bbar"]], [b_ug], b_ug)

        def tab_level(g, lvl):
            Ct, St, b_tab = tabs[g % NTAB]
            tl, b_tl = tls[g % NTAB]
            ta, b_ta = tmpA
            tb, b_tb = tmpB
            shp = [128, NA_, 32]
            Bc = tl[:, 0:32].unsqueeze(1).to_broadcast(shp)
            Bs = tl[:, 32:64].unsqueeze(1).to_broadcast(shp)
            Ac = tl[:, 64:64 + NA_].unsqueeze(2).to_broadcast(shp)
            As = tl[:, 64 + NA_:64 + 2 * NA_].unsqueeze(2).to_broadcast(shp)
            v3t = lambda a: a.rearrange("p (a b) -> p a b", b=32)
            if lvl == 0:
                S.dma("sp", tl, s_tab[:, g, :], [db["tab"]], [b_tl], b_tl)
                tt("dve", v3t(ta), Ac, Bc, ALU.mult, [b_tl], [b_ta])
            elif lvl == 1:
                tt("dve", v3t(tb), As, Bs, ALU.mult, [b_tl], [b_tb])
            elif lvl == 2:
                tt("dve", Ct, ta, tb, ALU.subtract, [b_ta, b_tb], [b_tab])
            elif lvl == 3:
                tt("dve", v3t(ta), As, Bc, ALU.mult, [b_tl], [b_ta])
            elif lvl == 4:
                tt("dve", v3t(tb), Ac, Bs, ALU.mult, [b_tl], [b_tb])
            elif lvl == 5:
                tt("dve", St, ta, tb, ALU.add, [b_ta, b_tb], [b_tab])

        NLV = 6
        for g in range(min(2, NG)):
            load_group(g)
            for lvl in range(NLV):
                tab_level(g, lvl)
        lv_per_it = (NLV + NT - 1) // NT
        prevgt = {}
        for it in range(NTOT + 7):
            gi, ji = divmod(it, NT)
            if it < NTOT:
                if ji == 0 and gi + 2 < NG:
                    load_group(gi + 2)
                if gi + 2 < NG:
                    for lvl in range(ji * lv_per_it, min(NLV, (ji + 1) * lv_per_it)):
                        tab_level(gi + 2, lvl)
            i = it
            if 0 <= i < NTOT:
                g, j = divmod(i, NT)
                ug, bbt, b_ug = ugs[g % NUG]
                js = slice(j * TT, (j + 1) * TT)
                bA, bAb = banks[(i % 2) * 2]
                bB, bBb = banks[(i % 2) * 2 + 1]
                S.group([lambda t: t.matmul(bA[:, :], bbt[:, j, 0:128], ug[:, js], start=True, stop=True)], [b_ug], [bAb])
                S.group([lambda t: t.matmul(bB[:, :], bbt[:, j, 128:256], ug[:, js], start=True, stop=True)], [b_ug], [bBb])
            i = it - 1
            if 0 <= i < NTOT:
                g, j = divmod(i, NT)
                Ct, St, b_tab = tabs[g % NTAB]
                bA, bAb = banks[(i % 2) * 2]
                bB, bBb = banks[(i % 2) * 2 + 1]
                t1, b_t1 = t1s[i % 2]
                t2, b_t2 = t2s[i % 2]
                tt("dve", t1, bA[:, :], Ct, ALU.mult, [bAb, b_tab], [b_t1])
                tt("dve", t2, bB[:, :], St, ALU.mult, [bBb, b_tab], [b_t2])
            i = it - 2
            if 0 <= i < NTOT:
                t1, b_t1 = t1s[i % 2]
                t2, b_t2 = t2s[i % 2]
                bp, b_bp = bps[i % 2]
                tt("pool", bp, t1, t2, ALU.add, [b_t1, b_t2], [b_bp])
            i = it - 3
            if 0 <= i < NTOT:
                g, j = divmod(i, NT)
                bp, b_bp = bps[i % 2]
                gt, b_gt = gts[i % 3]
                rbc = cp["dec"][:, g:g + 1].to_broadcast([128, TT])
                if j == 0:
                    ini, rd = 0.0, []
                elif j == NT // 2:
                    pgt = prevgt[i - 1]
                    S.op("dve", lambda v: v.tensor_tensor(out=init, in0=pgt[0][:, TT - 1:TT], in1=flag, op=ALU.mult),
                         [pgt[1], b_misc], [b_init])
                    ini, rd = init, [b_init]
                else:
                    pgt = prevgt[i - 1]
                    ini, rd = pgt[0][:, TT - 1:TT], [pgt[1]]
                S.op("dve", lambda v: v.tensor_tensor_scan(out=gt, data0=rbc, data1=bp, initial=ini, op0=ALU.mult,
                                                           op1=ALU.add), [b_bp, b_cp] + rd, [b_gt])
                prevgt[i] = (gt, b_gt)
                prevgt.pop(i - 2, None)
            i = it - 4
            if 0 <= i < NTOT and (i % NT) >= NT // 2:
                g, j = divmod(i, NT)
                Ct, St, b_tab = tabs[g % NTAB]
                gt, b_gt = gts[i % 3]
                X1, b_X1 = X1s[i % 2]
                X2, b_X2 = X2s[i % 2]
                tt("pool", X1, gt, Ct, ALU.mult, [b_gt, b_tab], [b_X1])
                tt("pool", X2, gt, St, ALU.mult, [b_gt, b_tab], [b_X2])
            i = it - 5
            if 0 <= i < NTOT and (i % NT) >= NT // 2:
                g, j = divmod(i, NT)
                ug, bbt, b_ug = ugs[g % NUG]
                js = slice(j * TT, (j + 1) * TT)
                X1, b_X1 = X1s[i % 2]
                X2, b_X2 = X2s[i % 2]
                bY, bYb = banks[4 + i % 2]
                gsl = slice(g * 16, (g + 1) * 16)
                S.group([lambda t: t.matmul(bY[0:16, :], W1[:, j - NT // 2, gsl], X1, start=True, stop=False),
                         lambda t: t.matmul(bY[0:16, :], W2[:, j - NT // 2, gsl], X2, start=False, stop=False),
                         lambda t: t.matmul(bY[0:16, :], diagd[:, g, :], ug[:, js], start=False, stop=True)],
                        [b_W, b_X1, b_X2, b_ug, b_misc], [bYb])
            i = it - 6
            if 0 <= i < NTOT and (i % NT) >= NT // 2:
                g, j = divmod(i, NT)
                bY, bYb = banks[4 + i % 2]
                yst, b_yst = ysts[i % 3]
                S.op("act", lambda a: a.activation(out=yst, in_=bY[0:16, :], func=AF.Copy), [bYb], [b_yst])
                o0 = (j - NT // 2) * TT
                S.dma("sp", s_yaT[g * 16:(g + 1) * 16, o0:o0 + TT], yst, [b_yst], [db["yaT"]], b_yst)

    if "B" in phases:
        phaseB()
    S.barrier()
    AR.off = persist_mark

    def phaseN():
        NQ4 = NQC
        b_k = nb("ncon")
        BK = [b_k]

        def cload(shape, src, dt=BF16, q="pool", **kw):
            t = sb(shape, dt)
            S.dma(q, t, src, [], BK, b_k, **kw)
            return t

        w1k = cload([128, 32, 128], i_w1k.rearrange("(l d) h -> d l h", d=128))
        w1v = cload([128, 32, 128], i_w1v.rearrange("(l d) h -> d l h", d=128))
        w2k = cload([128, 128], i_w2k)
        w2v = cload([128, 128], i_w2v)
        pek = cload([128, 32], i_pek)
        pev = cload([128, 32], i_pev)
        cmpb = cload([128, NCT, SH], i_cmpb.rearrange("(a p) t -> p a t", p=128))
        expd = cload([NB, NKT * 128], i_exp, max_dma_last_dim=4096).rearrange("j (k m) -> j k m", m=128)
        caus = cload([128, 4, 512], i_caus.rearrange("(v p) t -> p v t", p=128))
        wlo = cload([128, 128], i_wlo)
        whi = cload([128, 128], i_whi)
        wctx = cload([128, 128], i_wctx)
        selA = cload([128, NQT, NB], i_selA.rearrange("(q p) j -> p q j", p=128), F32, "sp")
        selB = cload([128, NQT, NB], i_selB.rearrange("(q p) j -> p q j", p=128), F32, "sp")
        selM = cload([128, NQT, NB], i_selM.rearrange("(q p) j -> p q j", p=128), F32, "sp")
        gnt = sb([128, NQT, 3 * NH], F32)
        S.dma("sp", gnt, s_gn.rearrange("(q p) j -> p q j", p=128), [db["gn"]], BK, b_k)
        NW = 128 + 1 + NB
        pebias = sb([128, 2], F32)
        for i, (w1, pe) in enumerate(((w1k, pek), (w1v, pev))):
            bk, bkb = banks[4 + i]
            S.group([lambda t, l=l, w1=w1, pe=pe, bk=bk: t.matmul(bk[:, 0:1], w1[:, l, :], pe[:, l:l + 1],
                                                                 start=(l == 0), stop=(l == 31)) for l in range(32)],
                    BK, [bkb])
            S.op("dve", lambda v, i=i, bk=bk: v.tensor_copy(out=pebias[:, i:i + 1], in_=bk[:, 0:1]), [bkb], BK)
        mark = AR.off
        for g in range(NKV):
            AR.off = mark
            b_kv = nb("kv")
            BKV = [b_kv]
            kcT = sb([128, SEQ], BF16)
            vcT = sb([128, SEQ], BF16)
            ksT = sb([128, SEQ], BF16)
            kwT = sb([128, SEQ], BF16)
            vs = sb([128, NKT, 132], BF16)
            vw = sb([128, NKT, 132], BF16)
            qT = sb([128, HPG, SH], BF16)
            gs = slice(g * 128, (g + 1) * 128)
            S.dma("sp", kcT, s_kT[0, gs, :], [db["kT"]], BKV, b_kv)
            S.dma("sp", ksT, s_kT[1, gs, :], [db["kT"]], BKV, b_kv)
            w0 = (NQT - 4) * 128
            S.dma("sp", kwT[:, w0:], s_kT[2, gs, w0:], [db["kT"]], BKV, b_kv)
            S.dma("sp", vcT, s_vcT[gs, :], [db["vcT"]], BKV, b_kv)
            S.dma("sp", vs[:, :, 0:128], s_vt[0, :, gs].rearrange("(k p) d -> p k d", p=128), [db["vt"]], BKV, b_kv)
            S.dma("sp", vw[:, NQT - 4:, 0:128], s_vt[1, w0:, gs].rearrange("(k p) d -> p k d", p=128), [db["vt"]], BKV,
                  b_kv)
            S.dma("sp", qT, s_qT[g * HPG * 128:(g + 1) * HPG * 128, :].rearrange("(h d) t -> d h t", d=128),
                  [db["qT"]], BKV, b_kv)
            S.op("pool", lambda v, vs=vs: v.memset(vs[:, :, 128:129], 1.0), [], BKV)
            S.op("pool", lambda v, vw=vw: v.memset(vw[:, :, 128:129], 1.0), [], BKV)
            hT = sb([128, 2, NC_], BF16)
            kcmpT = sb([128, NC_], BF16)
            vext = sb([128, NCT, NW], BF16)
            b_cmp = nb("cmp")
            BC = [b_cmp]
            S.op("pool", lambda v, vext=vext: v.memset(vext[:, :, 128:129], 1.0), [], BC)
            S.dma("pool", vext[:, :, 129:NW], i_ovl.rearrange("(a p) j -> p a j", p=128), [], BC, b_cmp)
            gtmp = sb([128, 3, NC_], F32)
            for i, (srcT, w1) in enumerate(((kcT, w1k), (vcT, w1v))):
                bk, bkb = banks[i]
                S.group([lambda t, l=l, w1=w1, srcT=srcT, bk=bk: t.matmul(
                    bk[:, 0:NC_], w1[:, l, :], srcT[:, l:l + 16 * (NC_ - 1) + 1:16], start=(l == 0), stop=(l == 31))
                    for l in range(32)], BK + BKV, [bkb])
                xx, x3, sg = gtmp[:, 0, :], gtmp[:, 1, :], gtmp[:, 2, :]
                S.op("dve", lambda v, xx=xx, bk=bk, i=i: v.tensor_scalar(
                    out=xx, in0=bk[:, 0:NC_], scalar1=pebias[:, i:i + 1], scalar2=None, op0=ALU.add), [bkb] + BK, BC)
                S.op("dve", lambda v, xx=xx, x3=x3: v.tensor_tensor(out=x3, in0=xx, in1=xx, op=ALU.mult), BC, BC)
                S.op("dve", lambda v, x3=x3: v.tensor_scalar(out=x3, in0=x3, scalar1=0.044715, scalar2=1.0,
                                                              op0=ALU.mult, op1=ALU.add), BC, BC)
                S.op("dve", lambda v, xx=xx, x3=x3: v.tensor_tensor(out=x3, in0=x3, in1=xx, op=ALU.mult), BC, BC)
                S.op("act", lambda a, x3=x3, sg=sg: a.activation(out=sg, in_=x3, func=AF.Sigmoid, scale=1.5957691216),
                     BC, BC)
                S.op("dve", lambda v, xx=xx, sg=sg, i=i: v.tensor_tensor(out=hT[:, i, :], in0=xx, in1=sg, op=ALU.mult),
                     BC, BC)
            bk, bkb = banks[2]
            S.group([lambda t, bk=bk: t.matmul(bk[:, 0:NC_], w2k, hT[:, 0, :], start=True, stop=True)], BK + BC,
                    [bkb])
            S.op("act", lambda a, bk=bk: a.activation(out=kcmpT, in_=bk[:, 0:NC_], func=AF.Copy), [bkb], BC)
            for a_ in range(NCT):
                na = min(128, NC_ - a_ * 128)
                bk, bkb = banks[3]
                S.group([lambda t, bk=bk, a_=a_, na=na: t.matmul(bk[0:na, 0:128], hT[:, 1, a_ * 128:a_ * 128 + na],
                                                                 w2v, start=True, stop=True)], BK + BC, [bkb])
                S.op("dve", lambda v, bk=bk, a_=a_, na=na: v.tensor_copy(out=vext[0:na, a_, 0:128],
                                                                         in_=bk[0:na, 0:128]), [bkb], BC)
            yb = sb([128, NQT, HPG * 128], F32)
            b_yb = [nb("yb") for _ in range(NQT)]
            imp = sb([128, NQT, NB], F32)
            b_imp = [nb("imp") for _ in range(NQT)]
            sc8 = sb([128, 16], F32)
            b_sc = nb("sc8")
            erot = Rot([(sb([128, 512], BF16), nb("e")) for _ in range(4)])

            def gate_scale(qt, h, br, den_ap, den_reads):
                col = br * NH + g * HPG + h
                S.op("dve", lambda v: v.tensor_scalar(out=sc8[:, 1:2], in0=den_ap, scalar1=1e-30, scalar2=None,
                                                      op0=ALU.max), den_reads, [b_sc])
                S.op("dve", lambda v: v.reciprocal(out=sc8[:, 2:3], in_=sc8[:, 1:2]), [b_sc], [b_sc])
                S.op("dve", lambda v: v.tensor_tensor(out=sc8[:, 0:1], in0=sc8[:, 2:3], in1=gnt[:, qt, col:col + 1],
                                                      op=ALU.mult), [b_sc] + BK, [b_sc])

            def accum_out(qt, h, num_ap, num_reads, first):
                dst = yb[:, qt, h * 128:(h + 1) * 128]
                if first:
                    S.op("dve", lambda v: v.tensor_scalar(out=dst, in0=num_ap, scalar1=sc8[:, 0:1], scalar2=None,
                                                          op0=ALU.mult), num_reads + [b_sc], [b_yb[qt]])
                else:
                    S.op("dve", lambda v: v.scalar_tensor_tensor(out=dst, in0=num_ap, scalar=sc8[:, 0:1], in1=dst,
                                                                 op0=ALU.mult, op1=ALU.add),
                         num_reads + [b_sc], [b_yb[qt]])

            selbT = sb([NB, SH], BF16)
            b_selT = nb("selT")
            scr = sb([128, 3, NB], F32)
            m8 = sb([128, 16], F32)
            selb = sb([128, NB], BF16)
            b_s3 = nb("s3")
            B3 = [b_s3]

            def n3(qt):
                sc_, wk_, se_ = scr[:, 0, :], scr[:, 1, :], scr[:, 2, :]
                S.op("dve", lambda v: v.tensor_tensor(out=sc_, in0=imp[:, qt, :], in1=selA[:, qt, :], op=ALU.mult),
                     [b_imp[qt]] + BK, B3)
                S.op("dve", lambda v: v.tensor_tensor(out=sc_, in0=sc_, in1=selB[:, qt, :], op=ALU.add), BK, B3)
                S.op("dve", lambda v: v.max(out=m8[:, 0:8], in_=sc_), B3, B3)
                S.op("dve", lambda v: v.match_replace(out=wk_, in_to_replace=m8[:, 0:8], in_values=sc_,
                                                      imm_value=-3.0e38), B3, B3)
                S.op("dve", lambda v: v.max(out=m8[:, 8:16], in_=wk_), B3, B3)
                S.op("dve", lambda v: v.tensor_scalar(out=se_, in0=sc_, scalar1=m8[:, 15:16], scalar2=None,
                                                      op0=ALU.is_ge), B3, B3)
                S.op("dve", lambda v: v.tensor_tensor(out=se_, in0=se_, in1=selM[:, qt, :], op=ALU.mult), BK, B3)
                S.op("dve", lambda v: v.tensor_scalar(out=selb, in0=se_, scalar1=-1.0, scalar2=-NEG, op0=ALU.add,
                                                      op1=ALU.mult), B3, B3)
                bk, bkb = banks[6 + qt % 2]
                pst = bk[:].bitcast(BF16)
                S.group([lambda t: t.transpose(out=pst[0:NB, 0:128], in_=selb, identity=ident)], B3 + [b_ident], [bkb])
                S.op("act", lambda a: a.activation(out=selbT[:, qt * 128:(qt + 1) * 128], in_=pst[0:NB, 0:128],
                                                   func=AF.Copy), [bkb], [b_selT])

            def n2_scores(h, qc):
                qs = slice(qc * 512, (qc + 1) * 512)
                es_ = []
                for a_ in range(NCT):
                    na = min(128, NC_ - a_ * 128)
                    bk, bkb = banks[a_ % 2]
                    S.group([lambda t: t.matmul(bk[0:na, :], kcmpT[:, a_ * 128:a_ * 128 + na], qT[:, h, qs],
                                                start=True, stop=False),
                             lambda t: t.matmul(bk[0:na, :], ident[0:na, 0:na], cmpb[0:na, a_, qs], start=False,
                                                stop=True)], BK + BKV + BC + [b_ident], [bkb])
                    e, b_e = erot.next()
                    S.op("act", lambda a: a.activation(out=e[0:na, :], in_=bk[0:na, :], func=AF.Exp), [bkb], [b_e])
                    es_.append((e, b_e, na))
                return es_

            def n2_pv(h, qc, es_):
                for q4 in range(4):
                    qt = qc * 4 + q4
                    bk, bkb = banks[2 + q4 % 2]
                    S.group([lambda t, a_=a_, e=es_[a_][0], na=es_[a_][2]: t.matmul(
                        bk[:, 0:NW], e[0:na, q4 * 128:(q4 + 1) * 128], vext[0:na, a_, :], start=(a_ == 0),
                        stop=(a_ == NCT - 1)) for a_ in range(NCT)], [x[1] for x in es_] + BC, [bkb])
                    gate_scale(qt, h, 0, bk[:, 128:129], [bkb])
                    accum_out(qt, h, bk[:, 0:128], [bkb], True)
                    if h == 0:
                        S.op("dve", lambda v: v.tensor_scalar(out=imp[:, qt, :], in0=bk[:, 129:NW],
                                                              scalar1=sc8[:, 2:3], scalar2=None, op0=ALU.mult),
                             [bkb, b_sc], [b_imp[qt]])
                    else:
                        S.op("dve", lambda v: v.scalar_tensor_tensor(out=imp[:, qt, :], in0=bk[:, 129:NW],
                                                                     scalar=sc8[:, 2:3], in1=imp[:, qt, :],
                                                                     op0=ALU.mult, op1=ALU.add),
                             [bkb, b_sc], [b_imp[qt]])

            pend = None
            for qc in range(NQ4):
                for h in range(HPG):
                    es_ = n2_scores(h, qc)
                    if pend is not None:
                        n2_pv(*pend)
                        if pend[0] == HPG - 1:
                            for q4 in range(4):
                                n3(pend[1] * 4 + q4)
                    pend = (h, qc, es_)
            n2_pv(*pend)
            for q4 in range(4):
                n3(pend[1] * 4 + q4)
            def n4_pv(h, qc, ki, nk_, kt, e, b_e, obk):
                for q4 in range(4):
                    ob, obb = obk[q4]
                    S.group([lambda t: t.matmul(ob[:, 0:129], e[:, q4 * 128:(q4 + 1) * 128], vs[:, kt, 0:129],
                                                start=(ki == 0), stop=(ki == nk_ - 1))], [b_e] + BKV, [obb])
                if ki == nk_ - 1:
                    for q4 in range(4):
                        qt = qc * 4 + q4
                        ob, obb = obk[q4]
                        gate_scale(qt, h, 1, ob[:, 128:129], [obb])
                        accum_out(qt, h, ob[:, 0:128], [obb], False)

            pend = None
            stepn = 0
            for h in range(HPG):
                for qc in range(NQ4):
                    qs = slice(qc * 512, (qc + 1) * 512)
                    kts = list(range(NQT)) + [NQT + i for i in range(4 * qc + 4)]
                    obk = [banks[4 + q4] for q4 in range(4)]
                    for ki, kt in enumerate(kts):
                        bk, bkb = banks[stepn % 2]
                        stepn += 1
                        fns = [lambda t: t.matmul(bk[:, :], ksT[:, kt * 128:(kt + 1) * 128], qT[:, h, qs], start=True,
                                                  stop=False)]
                        diag = kt - NQT - 4 * qc
                        last_is_sel = not (0 <= diag < 4)
                        fns.append(lambda t: t.matmul(bk[:, :], expd[:, kt, :], selbT[:, qs], start=False,
                                                      stop=last_is_sel))
                        if not last_is_sel:
                            fns.append(lambda t: t.matmul(bk[:, :], ident, caus[:, diag, :], start=False, stop=True))
                        S.group(fns, BK + BKV + [b_selT, b_ident], [bkb])
                        e, b_e = erot.next()
                        S.op("act", lambda a: a.activation(out=e, in_=bk[:, :], func=AF.Exp), [bkb], [b_e])
                        if pend is not None:
                            n4_pv(*pend)
                        pend = (h, qc, ki, len(kts), kt, e, b_e, obk)
            n4_pv(*pend)
            def n5_pv(h, qt, wi_, kt, e, b_e, ob, obb):
                S.group([lambda t: t.matmul(ob[:, 0:129], e[:, 0:128], vw[:, kt, 0:129], start=(wi_ == 0),
                                            stop=(wi_ == 4))], [b_e] + BKV, [obb])
                if wi_ == 4:
                    gate_scale(qt, h, 2, ob[:, 128:129], [obb])
                    accum_out(qt, h, ob[:, 0:128], [obb], False)

            pend = None
            stepn = 0
            for h in range(HPG):
                for qt in range(NQT):
                    A_ = NQT + qt
                    qs = slice(qt * 128, (qt + 1) * 128)
                    ob, obb = banks[4 + qt % 4]
                    for wi_ in range(5):
                        kt = A_ - 4 + wi_
                        bk, bkb = banks[stepn % 2]
                        stepn += 1
                        extra = []
                        if wi_ == 0:
                            extra.append(wlo)
                        if wi_ == 4:
                            extra.append(whi)
                        if kt < NQT:
                            extra.append(wctx)
                        fns = [lambda t: t.matmul(bk[:, 0:128], kwT[:, kt * 128:(kt + 1) * 128], qT[:, h, qs],
                                                  start=True, stop=(len(extra) == 0))]
                        for xi, xm in enumerate(extra):
                            fns.append(lambda t, xm=xm, l=(xi == len(extra) - 1): t.matmul(
                                bk[:, 0:128], ident, xm, start=False, stop=l))
                        S.group(fns, BK + BKV + [b_ident], [bkb])
                        e, b_e = erot.next()
                        S.op("act", lambda a: a.activation(out=e[:, 0:128], in_=bk[:, 0:128], func=AF.Exp),
                             [bkb], [b_e])
                        if pend is not None:
                            n5_pv(*pend)
                        pend = (h, qt, wi_, kt, e, b_e, ob, obb)
            n5_pv(*pend)
            ybb = sb([128, HPG * 128], BF16)
            b_ybb = nb("ybb")
            for qt in range(NQT):
                S.op("act", lambda a, qt=qt: a.activation(out=ybb, in_=yb[:, qt, :], func=AF.Copy),
                     [b_yb[qt]], [b_ybb])
                bk, bkb = banks[qt % 2]
                pst = bk[:].bitcast(BF16)
                for h in range(HPG):
                    S.group([lambda t, pst=pst, h=h: t.transpose(out=pst[:, h * 128:(h + 1) * 128],
                                                                 in_=ybb[:, h * 128:(h + 1) * 128], identity=ident)],
                            [b_ybb, b_ident], [bkb])
                st, stb = stg_bf.next()
                S.op("dve", lambda v, st=st, pst=pst: v.tensor_copy(out=st[:, 0:HPG * 128], in_=pst[:, 0:HPG * 128]),
                     [bkb], [stb])
                r0 = g * HPG * 128
                S.dma("sp", s_ybT[r0:r0 + HPG * 128, qt * 128:(qt + 1) * 128].rearrange("(h d) t -> d h t", d=128),
                      st[:, 0:HPG * 128].rearrange("d (h t) -> d h t", t=128), [stb], [db["ybT"]], stb)
            S.barrier()

    if "N" in phases:
        phaseN()
    S.barrier()
    AR.off = persist_mark


    def phaseC():
        make_wslots(3, 8192)
        bglu = sb([128, KS], F32)
        b_bg = nb("bglu")
        S.dma("sp", bglu, i_bglu, [], [b_bg], b_bg)
        ya = sb([128, KS, TT], BF16)
        b_ya = nb("ya")
        ya2 = sb([128, KS, TT], BF16)
        b_ya2 = nb("ya2")
        ybt = sb([128, KA, TT], BF16)
        b_ybt = nb("ybt")
        mg = sb([128, KD, TT], BF16)
        b_mg = nb("mg")
        yin = Rot([(sb([128, TT], F32), nb("yin")) for _ in range(3)])
        tmp = Rot([(sb([128, TT], F32), nb("ctmp")) for _ in range(4)])
        gat = Rot([(sb([128, TT], BF16), nb("gate")) for _ in range(3)])
        w_glu3, w_pa3, w_pb3, w_out3 = wview(w_glu), wview(w_pa), wview(w_pb), wview(w_out)
        for ti in range(NTO):
            tok0 = ti * TT
            ts_ = slice(tok0, tok0 + TT)
            S.dma("sp", ybt, s_ybT[:, ts_].rearrange("(k p) t -> p k t", p=128), [db["ybT"]], [b_ybt], b_ybt)
            for k in range(KS):
                yi, b_yi = yin.next()
                t1, b_t1 = tmp.next()
                S.dma("sp", yi, s_yaT[k * 128:(k + 1) * 128, ts_], [db["yaT"]], [b_yi], b_yi)
                S.op("dve", lambda v, yi=yi, t1=t1: v.tensor_tensor(out=t1, in0=yi, in1=yi, op=ALU.mult), [b_yi], [b_t1])
                S.op("dve", lambda v, t1=t1: v.tensor_scalar(out=t1, in0=t1, scalar1=0.044715, scalar2=1.0,
                                                              op0=ALU.mult, op1=ALU.add), [b_t1], [b_t1])
                S.op("pool", lambda v, yi=yi, t1=t1: v.tensor_tensor(out=t1, in0=t1, in1=yi, op=ALU.mult),
                     [b_yi, b_t1], [b_t1])
                S.op("act", lambda a, t1=t1: a.activation(out=t1, in_=t1, func=AF.Sigmoid, scale=1.5957691216),
                     [b_t1], [b_t1])
                S.op("pool", lambda v, yi=yi, t1=t1, k=k: v.tensor_tensor(out=ya[:, k, :], in0=yi, in1=t1, op=ALU.mult),
                     [b_yi, b_t1], [b_ya])

            def glu_post(r0, mw, bk, bkb):
                oc = r0 // 128
                t1, b_t1 = tmp.next()
                S.op("act", lambda a: a.activation(out=t1[0:mw, :], in_=bk[0:mw, 0:TT], func=AF.Sigmoid,
                                                   bias=bglu[0:mw, oc:oc + 1]), [bkb, b_bg], [b_t1])
                S.op("dve", lambda v: v.tensor_tensor(out=ya2[0:mw, oc, :], in0=ya[0:mw, oc, :], in1=t1[0:mw, :],
                                                      op=ALU.mult), [b_t1, b_ya], [b_ya2])

            wst["ck"] = ("glu", (SW + 511) // 512)
            projF(w_glu3, 0, SW, KS, ya, b_ya, None, None, post=glu_post)

            def merge_post(first, gsrc, gbuf):
                def post(r0, mw, bk, bkb):
                    oc = r0 // 128
                    gt_, b_gt = gat.next()
                    S.dma("sp", gt_[0:mw, :], gsrc[r0:r0 + mw, ts_], [gbuf], [b_gt], b_gt)
                    if first:
                        S.op("dve", lambda v: v.tensor_tensor(out=mg[0:mw, oc, :], in0=bk[0:mw, 0:TT], in1=gt_[0:mw, :],
                                                              op=ALU.mult), [bkb, b_gt], [b_mg])
                    else:
                        t1, b_t1 = tmp.next()
                        S.op("dve", lambda v: v.tensor_tensor(out=t1[0:mw, :], in0=bk[0:mw, 0:TT], in1=gt_[0:mw, :],
                                                              op=ALU.mult), [bkb, b_gt], [b_t1])
                        S.op("pool", lambda v: v.tensor_tensor(out=mg[0:mw, oc, :], in0=mg[0:mw, oc, :],
                                                               in1=t1[0:mw, :], op=ALU.add), [b_t1], [b_mg])
                return post

            wst["ck"] = ("pa", (D + 511) // 512)
            projF(w_pa3, 0, D, KS, ya2, b_ya2, None, None, post=merge_post(True, s_gaT, db["gaT"]))
            wst["ck"] = ("pb", (D + 511) // 512)
            projF(w_pb3, 0, D, KA, ybt, b_ybt, None, None, post=merge_post(False, s_gbT, db["gbT"]))
            wst["ck"] = ("out", ((D + 511) // 512) * ((KD + 15) // 16))
            projT(w_out3, 0, D, KD, mg, b_mg,
                  lambda sub, cb, cw: s_otok[tok0 + sub * 128:tok0 + (sub + 1) * 128, cb:cb + cw], db["otok"],
                  f32out=True, kchunk=16, wide=True)

    if "C" in phases:
        phaseC()
    S.barrier()
    AR.off = persist_mark

    def phaseD():
        make_wslots(4, 8192)
        wst["ck"] = None
        hnT = sb([128, KD, TT], BF16)
        b_hnT = nb("hnT")
        g3T = sb([128, KD], F32)
        b_g3 = nb("g3T")
        S.dma("sp", g3T, i_g3T, [], [b_g3], b_g3)
        a0 = AR.off
        actT = sb([128, KF, TT], BF16)
        b_act = nb("actT")
        a1 = AR.off
        AR.off = a0
        ot = sb([128, D], F32)
        xt = sb([128, D], F32)
        grep = sb([128, D], F32)
        xn = sb([128, 4, D], BF16)
        assert AR.off <= a1 or True
        AR.off = max(AR.off, a1)
        b_ot, b_xt, b_gr = nb("ot"), nb("xt"), nb("grep")
        b_ot2, b_xt2 = nb("ot2"), nb("xt2")
        b_xn = [nb("xn") for _ in range(4)]
        nd = dict(xn=xn, b_xn=b_xn, hnT=hnT, b_hnT=b_hnT, gT=g3T, b_g=b_g3)
        sgr = Rot([(sb([128, TT], F32), nb("sg")) for _ in range(3)])
        w_fg3, w_fu3, w_fd3 = wview(w_fg), wview(w_fu), wview(w_fd)
        for ti in range(NTO):
            tok0 = ti * TT
            S.dma("sp", grep, i_g2rep, [], [b_gr], b_gr)
            hnf = hnT.rearrange("p k t -> p (k t)").bitcast(F32)
            alt0 = [(ot, b_ot, xt, b_xt), (hnf[:, 0:D], b_ot2, hnf[:, D:2 * D], b_xt2)]
            for sub in range(4):
                rows = slice(tok0 + sub * 128, tok0 + (sub + 1) * 128)
                o_, bo_, x_, bx_ = alt0[sub % 2]
                S.dma("sp", o_, s_otok[rows, :], [db["otok"]], [bo_], bo_)
                S.dma("sp", x_, x_own[rows, :], [], [bx_], bx_)
                rs = rstd_of(nd, o_, bo_, xn[:, sub, :], b_xn[sub])
                S.op("dve", lambda v: v.scalar_tensor_tensor(out=o_, in0=o_, scalar=rs, in1=grep, op0=ALU.mult,
                                                             op1=ALU.mult), [b_stat, b_gr], [bo_])
                S.op("dve", lambda v: v.tensor_tensor(out=x_, in0=x_, in1=o_, op=ALU.add), [bo_], [bx_])
                S.dma("sp", s_h1[rows, :], x_, [bx_], [db["h1"]], bx_)
                rs2 = rstd_of(nd, x_, bx_, xn[:, sub, :], b_xn[sub])
                S.op("dve", lambda v: v.tensor_scalar(out=xn[:, sub, :], in0=x_, scalar1=rs2, scalar2=None,
                                                      op0=ALU.mult), [bx_, b_stat], [b_xn[sub]])
            S.barrier()
            transposes_to_hnT(nd)
            S.barrier()
            for cb in range(0, c.DFF, 256):
                cw = min(256, c.DFF - cb)
                wg, wgb = load_w(w_fg3, 0, KD, cb, cw)
                wu, wub = load_w(w_fu3, 0, KD, cb, cw)
                for m0 in range(0, cw, 128):
                    fc = (cb + m0) // 128
                    bg, bgb = mm_rot.next()
                    bu, bub = mm_rot.next()
                    S.group([lambda t, k=k, m0=m0, bg=bg, wg=wg: t.matmul(bg[:, 0:TT], wg[:, k, m0:m0 + 128], hnT[:, k, :],
                                                                          start=(k == 0), stop=(k == KD - 1))
                             for k in range(KD)], [wgb, b_hnT], [bgb])
                    S.group([lambda t, k=k, m0=m0, bu=bu, wu=wu: t.matmul(bu[:, 0:TT], wu[:, k, m0:m0 + 128], hnT[:, k, :],
                                                                          start=(k == 0), stop=(k == KD - 1))
                             for k in range(KD)], [wub, b_hnT], [bub])
                    sg, b_sg = sgr.next()
                    S.op("act", lambda a, sg=sg, bg=bg: a.activation(out=sg, in_=bg[:, 0:TT], func=AF.Silu),
                         [bgb], [b_sg])
                    S.op("dve", lambda v, sg=sg, bu=bu, fc=fc: v.tensor_tensor(out=actT[:, fc, :], in0=sg,
                                                                               in1=bu[:, 0:TT], op=ALU.mult),
                         [b_sg, bub], [b_act])
            wst["ck"] = None
            projT(w_fd3, 0, D, KF, actT, b_act,
                  lambda sub, cb, cw: s_ftok[tok0 + sub * 128:tok0 + (sub + 1) * 128, cb:cb + cw], db["ftok"],
                  f32out=True, kchunk=16, wide=True)
            S.barrier()
            S.dma("sp", grep, i_g4rep, [], [b_gr], b_gr)
            xnf = xn.rearrange("p a d -> p (a d)").bitcast(F32)
            alt = [(ot, b_ot, xt, b_xt), (xnf[:, 0:D], b_ot2, xnf[:, D:2 * D], b_xt2)]
            junk3 = hnT.rearrange("p k t -> p (k t)")[:, 0:D]
            for sub in range(4):
                rows = slice(tok0 + sub * 128, tok0 + (sub + 1) * 128)
                o_, bo_, x_, bx_ = alt[sub % 2]
                S.dma("sp", o_, s_ftok[rows, :], [db["ftok"]], [bo_], bo_)
                S.dma("sp", x_, s_h1[rows, :], [db["h1"]], [bx_], bx_)
                rs = rstd_of(nd, o_, bo_, junk3, b_hnT)
                S.op("dve", lambda v: v.scalar_tensor_tensor(out=o_, in0=o_, scalar=rs, in1=grep, op0=ALU.mult,
                                                             op1=ALU.mult), [b_stat, b_gr], [bo_])
                S.op("dve", lambda v: v.tensor_tensor(out=x_, in0=x_, in1=o_, op=ALU.add), [bo_], [bx_])
                S.dma("sp", y_out[rows, :], x_, [bx_], [], bx_)
            S.barrier()

    if "D" in phases:
        phaseD()
    S.emit()
    return nc, es


def _masks(c, s):
    SH, NB, NC_, NCT, NKT = c.SH, c.NB, c.NC, c.NCT, c.NKT
    SHb, SHc = SH // 64, SH // 16
    i = np.arange(SH)
    tg = s * SH + i
    n = np.arange(NCT * 128)
    if s == 1:
        ng = n.copy()
        nvalid = n < NC_
    else:
        ng = n - SHc
        nvalid = (n >= SHc) & (n < NC_)
    valid = nvalid[:, None] & ((16 * ng[:, None] + 31) <= tg[None, :])
    cmpbias = np.where(valid, 0.0, NEG).astype(np.float32)
    j = np.arange(NB)
    if s == 1:
        jg = j.copy()
        jvalid = np.ones(NB, bool)
    else:
        jg = j - SHb
        jvalid = j >= SHb
    ov = (16 * ng[:, None] < 64 * (jg[None, :] + 1)) & (16 * ng[:, None] + 32 > 64 * jg[None, :])
    ovl = (ov & nvalid[:, None] & jvalid[None, :]).astype(np.float32)
    cur = tg // 64
    allowed = jvalid[None, :] & (jg[None, :] * 64 <= tg[:, None])
    forced = allowed & ((jg[None, :] == 0) | (jg[None, :] == cur[:, None]) | (jg[None, :] == cur[:, None] - 1))
    selA = (allowed & ~forced).astype(np.float32)
    selB = np.where(forced, 1e9, np.where(allowed, 0.0, -1e30)).astype(np.float32)
    selM = allowed.astype(np.float32)
    m = np.arange(NKT * 128)
    expand = (j[:, None] == (m[None, :] // 64)).astype(np.float32)
    sl = np.arange(128)
    tl = np.arange(512)
    caus = np.concatenate([np.where((128 * v + sl[:, None]) <= tl[None, :], 0.0, NEG) for v in range(4)], 0)
    t1 = np.arange(128)
    wlo = np.where(sl[:, None] > t1[None, :], 0.0, NEG)
    whi = np.where(sl[:, None] <= t1[None, :], 0.0, NEG)
    wctx = np.full((128, 128), 0.0 if s == 1 else NEG)
    f = lambda a: np.ascontiguousarray(a, dtype=np.float32)
    return dict(cmpbias=f(cmpbias), ovl=f(ovl), selA=f(selA), selB=f(selB), selM=f(selM), expand=f(expand),
                caus=f(caus), wlo=f(wlo), whi=f(whi), wctx=f(wctx), ident=f(np.eye(128)))


def _shared(c, inp):
    f = lambda a: np.ascontiguousarray(a, dtype=np.float32)
    KD, NG, KS, D = c.KD, c.NG, c.KS, c.D
    m = {}
    m["w_in"] = f(inp["w_in"][0])
    m["w_glu"] = f(inp["ssm_w_glu"][0])
    m["w_pa"] = f(inp["w_proj_a"][0])
    m["w_pb"] = f(inp["w_proj_b"][0])
    m["w_out"] = f(inp["w_out"][0])
    m["w_fg"] = f(inp["w_ffn_gate"][0])
    m["w_fu"] = f(inp["w_ffn_up"][0])
    m["w_fd"] = f(inp["w_ffn_down"][0])
    m["g1T"] = f(np.asarray(inp["norm_mix_pre"][0]).reshape(KD, 128).T)
    m["g3T"] = f(np.asarray(inp["norm_ffn_pre"][0]).reshape(KD, 128).T)
    m["g2rep"] = f(np.broadcast_to(np.asarray(inp["norm_mix_post"][0])[None, :], (128, D)))
    m["g4rep"] = f(np.broadcast_to(np.asarray(inp["norm_ffn_post"][0])[None, :], (128, D)))
    are, aim, ldt = np.asarray(inp["ssm_a_re"][0]), np.asarray(inp["ssm_a_im"][0]), np.asarray(inp["ssm_log_dt"][0])
    m["are_pg"] = f(np.concatenate([are.T, are.T], 0))
    m["aim_pg"] = f(np.concatenate([aim.T, aim.T], 0))
    m["ldt_pg"] = f(np.broadcast_to(ldt[None, :], (128, NG)))
    m["are_gp"] = f(are)
    m["aim_gp"] = f(aim)
    m["ldt_gp"] = f(np.broadcast_to(ldt[:, None], (NG, 64)))
    m["bre_cg"] = f(np.asarray(inp["ssm_b_re"][0]).transpose(2, 0, 1).reshape(16, NG * 64))
    m["bim_cg"] = f(np.asarray(inp["ssm_b_im"][0]).transpose(2, 0, 1).reshape(16, NG * 64))
    creT = np.asarray(inp["ssm_c_re"][0]).transpose(2, 0, 1).reshape(64, NG * 16)
    cimT = np.asarray(inp["ssm_c_im"][0]).transpose(2, 0, 1).reshape(64, NG * 16)
    m["cc1"] = f(np.concatenate([creT, cimT], 0))
    m["cc2"] = f(np.concatenate([cimT, creT], 0))
    m["dskip"] = f(np.asarray(inp["ssm_d"][0]).reshape(NG, 16).T)
    m["bgluT"] = f(np.asarray(inp["ssm_b_glu"][0]).reshape(KS, 128).T)
    m["w1k"] = f(inp["cmp_w1_k"][0])
    m["w1v"] = f(inp["cmp_w1_v"][0])
    m["w2k"] = f(inp["cmp_w2_k"][0])
    m["w2v"] = f(inp["cmp_w2_v"][0])
    m["pekT"] = f(np.asarray(inp["cmp_pe_k"][0]).T)
    m["pevT"] = f(np.asarray(inp["cmp_pe_v"][0]).T)
    return m


def make_in_maps(c, inp):
    shared = _shared(c, inp)
    masks = [_masks(c, 0), _masks(c, 1)]
    x = np.asarray(inp["x"], dtype=np.float32)
    maps = []
    for b in range(c.B):
        for s in range(2):
            m = dict(shared)
            m.update(masks[s])
            m["x_own"] = np.ascontiguousarray(x[b, s * c.SH:(s + 1) * c.SH])
            m["x_ctx"] = np.ascontiguousarray(x[b, (1 - s) * c.SH:(2 - s) * c.SH])
            m["flag"] = np.full((128, 1), float(s), np.float32)
            maps.append(m)
    return maps


_CACHE = {}


def kernel(**inputs):
    c = Cfg()
    if "nc" not in _CACHE:
        _CACHE["nc"] = build(c)
    nc, es = _CACHE["nc"]
    maps = make_in_maps(c, inputs)
    res = run_bass_kernel_spmd(nc, maps, core_ids=list(range(2 * c.B)))
    out = np.empty((c.B, c.S, c.D), np.float32)
    for b in range(c.B):
        for s in range(2):
            out[b, s * c.SH:(s + 1) * c.SH] = res.results[b * 2 + s]["y"]
    return out
```
